# Optimizing a Trainium2 kernel written in Bass

```python
import math
import jax, jax.numpy as jnp
from jax import lax
import numpy as np

D_MODEL = 1024
BATCH = 4
SEQ = 8192
DEPTH = 1

RMS_EPS = 1e-6
ROPE_THETA = 10000.0
GLA_HEADS = 4
GLA_DK = D_MODEL // 2 // GLA_HEADS
GLA_DV = D_MODEL // GLA_HEADS
GLA_QK = GLA_HEADS * GLA_DK
GLA_V = GLA_HEADS * GLA_DV
GLA_LOWRANK = 16
GLA_GATE_NORM = 16.0
GLA_CHUNK = 64
DIFF_HEADS = 8
DIFF_HD = D_MODEL // (2 * DIFF_HEADS)
DIFF_VD = 2 * DIFF_HD
DIFF_QK = DIFF_HEADS * 2 * DIFF_HD
DIFF_V = DIFF_HEADS * DIFF_VD
Q_BLOCK = 128
PEER_HEADS = 8
PEER_NKEYS = 128
PEER_TOPK = 16
PEER_DQ = 256
PEER_NEXPERTS = PEER_NKEYS * PEER_NKEYS
PEER_BLOCK = 128
N_ADA = 6
POS_OFFSET_MAX = 1024

kernel_name = "hybrid_gla_diffattn_peer_adaln"


def rms_norm(x, g):
    xf = x.astype(jnp.float32)
    y = xf * lax.rsqrt(jnp.mean(xf * xf, axis=-1, keepdims=True) + RMS_EPS)
    return (y * g.astype(jnp.float32)).astype(x.dtype)


def modulate(h, shift, scale):
    return h * (1.0 + scale[:, None, :]) + shift[:, None, :]


def rope(x, pos):
    hd = x.shape[-1]
    inv = ROPE_THETA ** (-jnp.arange(0, hd, 2, dtype=jnp.float32) / hd)
    ang = pos.astype(jnp.float32)[..., None] * inv
    ang = jnp.concatenate([ang, ang], axis=-1)
    shape = ang.shape[:2] + (1,) * (x.ndim - 3) + (hd,)
    cos = jnp.cos(ang).reshape(shape)
    sin = jnp.sin(ang).reshape(shape)
    xf = x.astype(jnp.float32)
    x1, x2 = xf[..., : hd // 2], xf[..., hd // 2:]
    rot = jnp.concatenate([-x2, x1], axis=-1)
    return (xf * cos + rot * sin).astype(x.dtype)


def gla_scan(q, k, v, logg):
    B, H, S, dk = q.shape
    dv = v.shape[-1]
    nc = S // GLA_CHUNK
    f32 = jnp.float32
    qc = q.astype(f32).reshape(B, H, nc, GLA_CHUNK, dk)
    kc = k.astype(f32).reshape(B, H, nc, GLA_CHUNK, dk)
    vc = v.astype(f32).reshape(B, H, nc, GLA_CHUNK, dv)
    b = jnp.cumsum(logg.astype(f32).reshape(B, H, nc, GLA_CHUNK, dk), axis=3)
    b_last = b[..., -1:, :]
    qe = qc * jnp.exp(b)
    ke = kc * jnp.exp(-b)
    kend = kc * jnp.exp(b_last - b)
    mask = jnp.tril(jnp.ones((GLA_CHUNK, GLA_CHUNK), dtype=bool))
    att = jnp.where(mask, jnp.einsum('bhncd,bhnjd->bhncj', qe, ke), 0.0)
    o_intra = jnp.einsum('bhncj,bhnje->bhnce', att, vc)
    dec = jnp.exp(b_last[..., 0, :])

    def step(state, inp):
        qn, kn, vn, dn = inp
        o = jnp.einsum('bhcd,bhde->bhce', qn, state)
        state = dn[..., None] * state + jnp.einsum('bhcd,bhce->bhde', kn, vn)
        return state, o

    xs = (jnp.moveaxis(qe, 2, 0), jnp.moveaxis(kend, 2, 0),
          jnp.moveaxis(vc, 2, 0), jnp.moveaxis(dec, 2, 0))
    _, o_inter = lax.scan(step, jnp.zeros((B, H, dk, dv), f32), xs)
    o = o_intra + jnp.moveaxis(o_inter, 0, 2)
    return o.reshape(B, H, S, dv)


def diff_attention(q, k, v, lam, pos):
    B, S, H, _, hd = q.shape
    vd = v.shape[-1]
    q = rope(q, pos) * (hd ** -0.5)
    k = rope(k, pos)
    qh = q.transpose(0, 2, 3, 1, 4)
    kh = k.transpose(0, 2, 3, 1, 4)
    vh = v.transpose(0, 2, 1, 3)
    nb = S // Q_BLOCK
    qb = jnp.moveaxis(qh.reshape(B, H, 2, nb, Q_BLOCK, hd), 3, 0)

    def block(qblk):
        s = jnp.einsum('bhmqd,bhmkd->bhmqk', qblk, kh).astype(jnp.float32)
        p = jax.nn.softmax(s, axis=-1)
        a = p[:, :, 0] - lam * p[:, :, 1]
        return jnp.einsum('bhqk,bhkd->bhqd', a.astype(vh.dtype), vh)

    o = lax.map(block, qb)
    o = jnp.moveaxis(o, 0, 2).reshape(B, H, S, vd)
    return o.transpose(0, 2, 1, 3)


def peer(h, wq, subkeys, u_tab, v_tab):
    B, S, D = h.shape
    K = PEER_TOPK
    q = (h @ wq).reshape(B, S, PEER_HEADS, 2, PEER_DQ // 2)
    s = jnp.einsum('bshpd,hpnd->bshpn', q, subkeys).astype(jnp.float32)
    s_top, i_top = lax.top_k(s, K)
    cand_s = (s_top[..., 0, :, None] + s_top[..., 1, None, :]).reshape(B, S, PEER_HEADS, K * K)
    cand_i = (i_top[..., 0, :, None] * PEER_NKEYS + i_top[..., 1, None, :]).reshape(B, S, PEER_HEADS, K * K)
    s_fin, sel = lax.top_k(cand_s, K)
    idx = jnp.take_along_axis(cand_i, sel, axis=-1)
    g = jax.nn.softmax(s_fin, axis=-1)
    nb = (B * S) // PEER_BLOCK
    hb = h.reshape(nb, PEER_BLOCK, D)
    ib = idx.reshape(nb, PEER_BLOCK, PEER_HEADS * K)
    gb = g.reshape(nb, PEER_BLOCK, PEER_HEADS * K).astype(h.dtype)

    def block(args):
        hx, ix, gx = args
        u = u_tab[ix]
        act = jax.nn.gelu(jnp.einsum('td,ted->te', hx, u), approximate=False)
        vv = v_tab[ix]
        return jnp.einsum('te,ted->td', gx * act, vv)

    out = lax.map(block, (hb, ib, gb))
    return out.reshape(B, S, D)


def setup_inputs(seed: int = 0) -> dict:
    key = jax.random.key(seed)
    ks = jax.random.split(key, 32)
    D = D_MODEL
    in_width = 2 * GLA_QK + 2 * GLA_V + 2 * GLA_LOWRANK + 2 * DIFF_QK + DIFF_V + 2 * D

    def nrm(k, shape, scale):
        return jax.random.normal(k, shape, jnp.float32) * scale

    def gain(k, shape):
        return 1.0 + 0.02 * jax.random.normal(k, shape, jnp.float32)

    positions = (jnp.arange(SEQ, dtype=jnp.int32)[None, :]
                 + jax.random.randint(ks[2], (BATCH, 1), 0, POS_OFFSET_MAX, dtype=jnp.int32))
    return {
        "x": nrm(ks[0], (BATCH, SEQ, D), 1.0),
        "c": nrm(ks[1], (BATCH, D), 1.0),
        "positions": positions,
        "w_ada": nrm(ks[3], (DEPTH, D, N_ADA * D), 0.5 * D ** -0.5),
        "b_ada": nrm(ks[4], (DEPTH, N_ADA * D), 0.01),
        "norm1_g": gain(ks[5], (DEPTH, D)),
        "w_in": nrm(ks[6], (DEPTH, D, in_width), D ** -0.5),
        "gla_wa_fw": nrm(ks[7], (DEPTH, GLA_LOWRANK, GLA_QK), GLA_LOWRANK ** -0.5),
        "gla_ba_fw": nrm(ks[8], (DEPTH, GLA_QK), 0.01),
        "gla_wa_bw": nrm(ks[9], (DEPTH, GLA_LOWRANK, GLA_QK), GLA_LOWRANK ** -0.5),
        "gla_ba_bw": nrm(ks[10], (DEPTH, GLA_QK), 0.01),
        "gla_norm_g": gain(ks[11], (DEPTH, GLA_V)),
        "diff_lq1": nrm(ks[12], (DEPTH, DIFF_HD), 0.1),
        "diff_lk1": nrm(ks[13], (DEPTH, DIFF_HD), 0.1),
        "diff_lq2": nrm(ks[14], (DEPTH, DIFF_HD), 0.1),
        "diff_lk2": nrm(ks[15], (DEPTH, DIFF_HD), 0.1),
        "diff_norm_g": gain(ks[16], (DEPTH, DIFF_V)),
        "w_gla_proj": nrm(ks[17], (DEPTH, GLA_V, D), GLA_V ** -0.5),
        "w_diff_proj": nrm(ks[18], (DEPTH, DIFF_V, D), DIFF_V ** -0.5),
        "w_out": nrm(ks[19], (DEPTH, D, D), D ** -0.5),
        "norm2_g": gain(ks[20], (DEPTH, D)),
        "peer_wq": nrm(ks[21], (DEPTH, D, PEER_HEADS * PEER_DQ), D ** -0.5),
        "peer_subkeys": nrm(ks[22], (DEPTH, PEER_HEADS, 2, PEER_NKEYS, PEER_DQ // 2), (PEER_DQ // 2) ** -0.5),
        "peer_u": nrm(ks[23], (DEPTH, PEER_NEXPERTS, D), D ** -0.5),
        "peer_v": nrm(ks[24], (DEPTH, PEER_NEXPERTS, D), PEER_HEADS ** -0.5),
        "w_final_ada": nrm(ks[25], (D, 2 * D), 0.5 * D ** -0.5),
        "b_final_ada": nrm(ks[26], (2 * D,), 0.01),
        "normf_g": gain(ks[27], (D,)),
    }


def reference(x, c, positions, w_ada, b_ada, norm1_g, w_in, gla_wa_fw, gla_ba_fw,
              gla_wa_bw, gla_ba_bw, gla_norm_g, diff_lq1, diff_lk1, diff_lq2, diff_lk2,
              diff_norm_g, w_gla_proj, w_diff_proj, w_out, norm2_g, peer_wq,
              peer_subkeys, peer_u, peer_v, w_final_ada, b_final_ada, normf_g):
    B, S, D = x.shape
    splits = [GLA_QK, GLA_QK, GLA_V, GLA_V, GLA_LOWRANK, GLA_LOWRANK,
              DIFF_QK, DIFF_QK, DIFF_V, D, D]
    offsets = np.cumsum(splits)[:-1].tolist()
    c_act = jax.nn.silu(c)

    for l in range(DEPTH):
        lambda_init = 0.8 - 0.6 * math.exp(-0.3 * l)
        mod = c_act @ w_ada[l] + b_ada[l]
        sh1, sc1, gt1, sh2, sc2, gt2 = jnp.split(mod, N_ADA, axis=-1)

        h = modulate(rms_norm(x, norm1_g[l]), sh1, sc1)
        proj = h @ w_in[l]
        (gq, gk, gv, gr, lr_fw, lr_bw, dq, dk, dv, gate_a, gate_b) = jnp.split(proj, offsets, axis=-1)

        def heads(t, d):
            return t.reshape(B, S, GLA_HEADS, d).transpose(0, 2, 1, 3)
        q_g = heads(gq, GLA_DK) * (GLA_DK ** -0.5)
        k_g = heads(gk, GLA_DK)
        v_g = heads(gv, GLA_DV)
        logg_fw = heads(jax.nn.log_sigmoid((lr_fw @ gla_wa_fw[l] + gla_ba_fw[l]).astype(jnp.float32)) / GLA_GATE_NORM, GLA_DK)
        logg_bw = heads(jax.nn.log_sigmoid((lr_bw @ gla_wa_bw[l] + gla_ba_bw[l]).astype(jnp.float32)) / GLA_GATE_NORM, GLA_DK)
        o_fw = gla_scan(q_g, k_g, v_g, logg_fw)
        o_bw = jnp.flip(gla_scan(jnp.flip(q_g, 2), jnp.flip(k_g, 2), jnp.flip(v_g, 2),
                                 jnp.flip(logg_bw, 2)), 2)
        o_g = (o_fw + o_bw).transpose(0, 2, 1, 3).astype(x.dtype)
        o_g = rms_norm(o_g, gla_norm_g[l].reshape(GLA_HEADS, GLA_DV)).reshape(B, S, GLA_V)
        y_gla = (o_g * jax.nn.silu(gr)) @ w_gla_proj[l]

        lam = (jnp.exp(jnp.sum(diff_lq1[l].astype(jnp.float32) * diff_lk1[l].astype(jnp.float32)))
               - jnp.exp(jnp.sum(diff_lq2[l].astype(jnp.float32) * diff_lk2[l].astype(jnp.float32)))
               + lambda_init)
        q_d = dq.reshape(B, S, DIFF_HEADS, 2, DIFF_HD)
        k_d = dk.reshape(B, S, DIFF_HEADS, 2, DIFF_HD)
        v_d = dv.reshape(B, S, DIFF_HEADS, DIFF_VD)
        o_d = diff_attention(q_d, k_d, v_d, lam, positions).astype(x.dtype)
        o_d = rms_norm(o_d, diff_norm_g[l].reshape(DIFF_HEADS, DIFF_VD)) * (1.0 - lambda_init)
        y_diff = o_d.reshape(B, S, DIFF_V) @ w_diff_proj[l]

        merged = jax.nn.sigmoid(gate_a) * y_gla + jax.nn.sigmoid(gate_b) * y_diff
        x = x + gt1[:, None, :] * (merged @ w_out[l])

        h2 = modulate(rms_norm(x, norm2_g[l]), sh2, sc2)
        x = x + gt2[:, None, :] * peer(h2, peer_wq[l], peer_subkeys[l], peer_u[l], peer_v[l])

    fmod = c_act @ w_final_ada + b_final_ada
    f_shift, f_scale = jnp.split(fmod, 2, axis=-1)
    return modulate(rms_norm(x, normf_g), f_shift, f_scale)
```

```python
import math, os
from contextlib import ExitStack
import numpy as np
import concourse.bass as bass
import concourse.mybir as mybir
from concourse.bass_utils import run_bass_kernel_spmd

F32 = mybir.dt.float32
BF16 = mybir.dt.bfloat16
I32 = mybir.dt.int32
AF = mybir.ActivationFunctionType
ALU = mybir.AluOpType
AX = mybir.AxisListType

D = 1024
NW1 = 3104
NW2 = 5120
NW3 = 2048
EPS = 1e-6
TWO_PI = 2.0 * math.pi


class Sched:
    NDSEM = 6

    def __init__(self, nc, stack):
        self.nc = nc
        self.engs = {"pe": nc.tensor, "dve": nc.vector, "act": nc.scalar, "pool": nc.gpsimd, "sp": nc.sync}
        self.sem = {k: stack.enter_context(nc.semaphore("s_" + k)) for k in self.engs}
        self.cnt = {k: 0 for k in self.engs}
        self.dq = ["sp", "act", "pool"]
        self.dsem = {q: [stack.enter_context(nc.semaphore("d_%s%d" % (q, i))) for i in range(self.NDSEM)] for q in self.dq}
        self.dcnt = {q: 0 for q in self.dq}
        self.waited = {}
        self.lastw = {}
        self.readers = {}
        self.out_tokens = []

    def _need(self, stream, tok):
        sem, val, tstream, isdma = tok
        if (not isdma) and tstream == stream and stream == "pe":
            return
        key = (stream, id(sem))
        if self.waited.get(key, 0) >= val:
            return
        self.waited[key] = val
        self.engs[stream].wait_ge(sem, val)

    def _deps(self, stream, r, w):
        for k in r:
            t = self.lastw.get(k)
            if t is not None:
                self._need(stream, t)
        for k in w:
            t = self.lastw.get(k)
            if t is not None:
                self._need(stream, t)
            for t in self.readers.get(k, {}).values():
                self._need(stream, t)

    def _commit(self, tok, r, w):
        for k in r:
            self.readers.setdefault(k, {})[(tok[2], tok[3], id(tok[0]))] = tok
        for k in w:
            self.lastw[k] = tok
            self.readers[k] = {}

    def op(self, stream, fn, r=(), w=()):
        self._deps(stream, r, w)
        ins = fn(self.engs[stream])
        self.cnt[stream] += 1
        ins.then_inc(self.sem[stream], 1)
        tok = (self.sem[stream], self.cnt[stream], stream, False)
        self._commit(tok, r, w)
        return tok

    def dma(self, q, fn, r=(), w=(), is_out=False):
        j = self.dcnt[q]
        sem = self.dsem[q][j % self.NDSEM]
        if j >= self.NDSEM:
            self._need(q, (sem, 16 * (j // self.NDSEM), q, True))
        self._deps(q, r, w)
        ins = fn(self.engs[q])
        ins.then_inc(sem, 16)
        self.dcnt[q] += 1
        tok = (sem, 16 * (j // self.NDSEM + 1), q, True)
        self._commit(tok, r, w)
        if is_out:
            self.out_tokens.append(tok)
        return tok

    def barrier(self):
        toks = []
        for s in self.engs:
            if self.cnt[s] > 0:
                toks.append((self.sem[s], self.cnt[s], s, False))
        for q in self.dq:
            j = self.dcnt[q]
            for i in range(max(0, j - self.NDSEM), j):
                toks.append((self.dsem[q][i % self.NDSEM], 16 * (i // self.NDSEM + 1), q, True))
        for s in self.engs:
            for t in toks:
                if (not t[3]) and t[2] == s:
                    continue
                self._need(s, t)

    def finish(self):
        for tok in self.out_tokens:
            self._need("sp", tok)


class Rot:
    def __init__(self, tiles, name, keys=None):
        self.tiles = tiles
        self.name = name
        self.keys = keys
        self.i = 0

    def next(self):
        j = self.i % len(self.tiles)
        self.i += 1
        return self.tiles[j], (self.keys[j] if self.keys else "%s#%d" % (self.name, j))


def build(T_OWN, debug=False):
    KSKIP = os.environ.get('K_SKIP', '')
    T_ALL = 2 * T_OWN
    NT_OWN = T_OWN // 512
    NT_ALL = T_ALL // 512
    nc = bass.Bass("TRN2", target_bir_lowering=False)

    def din(name, shape, dt=F32):
        return nc.dram_tensor(name, shape, dt, kind="ExternalInput").ap()

    def dscr(name, shape, dt=F32):
        return nc.dram_tensor(name, shape, dt, kind="ExternalOutput" if debug else "Internal").ap()

    xT = din("xT", [D, T_ALL])
    pos = din("pos", [1, T_ALL], I32)
    cT = din("cT", [128, 8])
    w_ada = din("w_ada", [D, 6144])
    w_fada = din("w_fada", [D, 2048])
    badaT = din("badaT", [128, 64])
    gvecT = din("gvecT", [128, 32])
    w_all = din("w_all", [D, NW1 + NW2 + NW3])
    wa_A = din("wa_A", [16, 512]); ba_A = din("ba_A", [1, 512])
    wa_B = din("wa_B", [16, 512]); ba_B = din("ba_B", [1, 512])
    lqk = din("lqk", [1, 256])
    dng = din("dng", [1, 1024])
    w_gp = din("w_gp", [D, D]); w_dp = din("w_dp", [D, D]); w_out = din("w_out", [D, D])
    w_pq = din("w_pq", [D, 2048])
    skT = din("skT", [128, 2048])
    peer_u = din("peer_u", [16384, D]); peer_v = din("peer_v", [16384, D])
    ident_d = din("ident", [128, 128])
    tri_d = din("tri4", [128, 512])
    ropec_d = din("ropec", [128, 2])
    pconst_d = din("pconst", [128, 4096], I32)
    pconstf_d = din("pconstf", [128, 64])
    pcs_d = din("pcs", [128, 8], I32)
    out = nc.dram_tensor("out", [T_OWN, D], F32, kind="ExternalOutput").ap()

    S_hT = dscr("S_hT", [D, T_ALL], BF16)
    S_rows = dscr("S_rows", [1, 40 * 128])
    S_gqT = dscr("S_gqT", [512, T_OWN]); S_gkT = dscr("S_gkT", [512, T_OWN])
    S_grT = dscr("S_grT", [1024, T_OWN], BF16)
    S_lrA = dscr("S_lrA", [16, T_OWN]); S_lrB = dscr("S_lrB", [16, T_ALL])
    S_gk = dscr("S_gk", [T_ALL, 512]); S_gv = dscr("S_gv", [T_ALL, 1024], BF16)
    S_qT = dscr("S_qT", [1024, T_OWN], BF16); S_kT = dscr("S_kT", [1024, T_ALL], BF16)
    S_dv = dscr("S_dv", [T_ALL, 1024], BF16)
    S_saT = dscr("S_saT", [1024, T_OWN], BF16); S_sbT = dscr("S_sbT", [1024, T_OWN], BF16)
    S_oT = dscr("S_oT", [1024, T_OWN])
    S_odT = dscr("S_odT", [1024, T_OWN], BF16)
    S_x1 = dscr("S_x1", [T_OWN, D])
    S_idx = dscr("S_idx", [T_OWN, 128], I32)
    S_gw = dscr("S_gw", [T_OWN, 128])

    with ExitStack() as st:
        S = Sched(nc, st)

        uniq = [0]

        def sb(name, shape, dt=F32, stack=st):
            uniq[0] += 1
            return stack.enter_context(nc.sbuf_tensor("s%d_%s" % (uniq[0], name), shape, dt))

        def rot(name, shape, dt, n, stack=st):
            return Rot([sb("%s%d" % (name, i), shape, dt, stack) for i in range(n)], name)

        pb = [st.enter_context(nc.psum_tensor("pb%d" % i, [128, 512], F32)) for i in range(7)]
        ptb = st.enter_context(nc.psum_tensor("ptb", [128, 1024], BF16))
        PBK = ["pb%d" % i for i in range(7)]

        ident = sb("ident", [128, 128]); tri4 = sb("tri4", [128, 512]); ropec = sb("ropec", [128, 2])
        ident_bf = sb("ident_bf", [128, 128], BF16)
        ones_bf = sb("ones_bf", [128, 128], BF16); ones_f = sb("ones_f", [1, 128])
        S.dma("sp", lambda e: e.dma_start(out=ident[:], in_=ident_d), w=["ident"])
        S.dma("sp", lambda e: e.dma_start(out=tri4[:], in_=tri_d), w=["tri4"])
        S.dma("sp", lambda e: e.dma_start(out=ropec[:], in_=ropec_d), w=["ropec"])
        S.op("dve", lambda e: e.tensor_copy(out=ident_bf[:], in_=ident[:]), r=["ident"], w=["ident_bf"])
        S.op("pool", lambda e: e.memset(ones_bf[:], 1.0), w=["ones_bf"])
        S.op("pool", lambda e: e.memset(ones_f[:], 1.0), w=["ones_f"])
        triA, triB, striA, striB = (tri4[:, i * 128:(i + 1) * 128] for i in range(4))

        modT = sb("modT", [128, 64]); gvec = sb("gvec", [128, 32]); avec = sb("avec", [128, 24])
        lam = sb("lam", [128, 1]); dngr = sb("dngr", [128, 1024])
        sh1 = lambda kc: modT[:, kc:kc + 1]
        gt1 = lambda kc: modT[:, 16 + kc:17 + kc]

        with ExitStack() as p0:
            cTt = sb("cTt", [128, 8], F32, p0); cact = sb("cact", [128, 8], F32, p0)
            badat = sb("badat", [128, 64], F32, p0)
            wa_r = rot("wada", [128, 6144], F32, 2, p0)
            wf_r = rot("wfada", [128, 2048], F32, 2, p0)
            lq = sb("lq", [128, 256], F32, p0); lj = sb("lj", [128, 64], F32, p0); l2 = sb("l2", [128, 2], F32, p0)
            rows = sb("rows", [128, 40], F32, p0)
            S.dma("sp", lambda e: e.dma_start(out=cTt[:], in_=cT), w=["cTt"])
            S.dma("sp", lambda e: e.dma_start(out=badat[:], in_=badaT), w=["badat"])
            S.dma("sp", lambda e: e.dma_start(out=gvec[:], in_=gvecT), w=["gvec"])
            S.dma("sp", lambda e: e.dma_start(out=lq[:], in_=lqk.partition_broadcast(128)), w=["lq"])
            S.dma("sp", lambda e: e.dma_start(out=dngr[:], in_=dng.partition_broadcast(128)), w=["dngr"])
            S.op("act", lambda e: e.activation(out=cact[:], in_=cTt[:], func=AF.Silu), r=["cTt"], w=["cact"])
            mps = pb[0]
            for kc in range(8):
                wt, wk = wa_r.next(); ft, fk = wf_r.next()
                S.dma("sp", lambda e: e.dma_start(out=wt[:], in_=w_ada[kc * 128:(kc + 1) * 128, :]), w=[wk])
                S.dma("act", lambda e: e.dma_start(out=ft[:], in_=w_fada[kc * 128:(kc + 1) * 128, :]), w=[fk])
                for j in range(48):
                    S.op("pe", lambda e: e.matmul(mps[:, j:j + 1], lhsT=wt[:, j * 128:(j + 1) * 128], rhs=cact[:, kc:kc + 1], start=(kc == 0 and j == 0), stop=False), r=[wk, "cact"], w=["pb0"])
                for j in range(16):
                    S.op("pe", lambda e: e.matmul(mps[:, 48 + j:49 + j], lhsT=ft[:, j * 128:(j + 1) * 128], rhs=cact[:, kc:kc + 1], start=False, stop=(kc == 7 and j == 15)), r=[fk, "cact"], w=["pb0"])
            S.op("dve", lambda e: e.tensor_tensor(out=modT[:], in0=mps[:, 0:64], in1=badat[:], op=ALU.add), r=["pb0", "badat"], w=["modT"])
            for i, (c0, g0) in enumerate(((8, 0), (32, 8), (56, 16))):
                S.op("dve", lambda e: e.scalar_tensor_tensor(out=avec[:, i * 8:(i + 1) * 8], in0=modT[:, c0:c0 + 8], scalar=1.0, in1=gvec[:, g0:g0 + 8], op0=ALU.add, op1=ALU.mult), r=["modT", "gvec"], w=["avec"])
            for i in range(2):
                S.op("dve", lambda e: e.scalar_tensor_tensor(out=lj[:], in0=lq[:, i * 128:i * 128 + 64], scalar=1.0, in1=lq[:, i * 128 + 64:i * 128 + 128], op0=ALU.mult, op1=ALU.mult, accum_out=l2[:, i:i + 1]), r=["lq"], w=["lj", "l2"])
            S.op("act", lambda e: e.activation(out=l2[:], in_=l2[:], func=AF.Exp), r=["l2"], w=["l2"])
            S.op("dve", lambda e: e.scalar_tensor_tensor(out=lam[:], in0=l2[:, 0:1], scalar=0.2, in1=l2[:, 1:2], op0=ALU.add, op1=ALU.subtract), r=["l2"], w=["lam"])
            S.op("dve", lambda e: e.tensor_scalar(out=dngr[:], in0=dngr[:], scalar1=0.8, scalar2=None, op0=ALU.mult), r=["dngr"], w=["dngr"])
            S.op("dve", lambda e: e.tensor_copy(out=rows[:, 0:8], in_=avec[:, 8:16]), r=["avec"], w=["rows"])
            S.op("dve", lambda e: e.tensor_copy(out=rows[:, 8:16], in_=modT[:, 24:32]), r=["modT"], w=["rows"])
            S.op("dve", lambda e: e.tensor_copy(out=rows[:, 16:24], in_=modT[:, 40:48]), r=["modT"], w=["rows"])
            S.op("dve", lambda e: e.tensor_copy(out=rows[:, 24:32], in_=avec[:, 16:24]), r=["avec"], w=["rows"])
            S.op("dve", lambda e: e.tensor_copy(out=rows[:, 32:40], in_=modT[:, 48:56]), r=["modT"], w=["rows"])
            S.dma("sp", lambda e: e.dma_start(out=S_rows.rearrange("o (j p) -> p (o j)", p=128), in_=rows[:], allow_slow_non_contiguous=True), r=["rows"])
            S.barrier()
            if os.environ.get('K_STOP') == '1':
                S.finish(); return nc

        def norm_mod(xt, xk, hT, hk, a_col, sh_col, ntok, stk_rs, ps, psk):
            sq, sqk = stk_rs["sq"].next()
            rs, rsk = stk_rs["rs"].next()
            S.op("act", lambda e: e.activation(out=sq[:, :, 0:ntok], in_=xt[:, :, 0:ntok], func=AF.Square), r=[xk], w=[sqk])
            for kc in range(8):
                S.op("pe", lambda e: e.matmul(ps[:, 0:ntok], lhsT=ones_bf[:], rhs=sq[:, kc, 0:ntok], start=(kc == 0), stop=(kc == 7)), r=["ones_bf", sqk], w=[psk])
            S.op("act", lambda e: e.activation(out=rs[:, 0:ntok], in_=ps[:, 0:ntok], func=AF.Sqrt, bias=EPS, scale=1.0 / D), r=[psk], w=[rsk])
            S.op("dve", lambda e: e.reciprocal(out=rs[:, 0:ntok], in_=rs[:, 0:ntok]), r=[rsk], w=[rsk])
            for kc in range(8):
                S.op("dve", lambda e: e.scalar_tensor_tensor(out=xt[:, kc, 0:ntok], in0=xt[:, kc, 0:ntok], scalar=a_col(kc), in1=rs[:, 0:ntok], op0=ALU.mult, op1=ALU.mult), r=[xk, rsk, "avec"], w=[xk])
            for kc in range(8):
                S.op("act", lambda e: e.activation(out=hT[:, kc, 0:ntok], in_=xt[:, kc, 0:ntok], func=AF.Identity, bias=sh_col(kc), scale=1.0), r=[xk, "modT"], w=[hk])

        xT_v = xT.rearrange("(k p) t -> p k t", p=128)
        hT_v = S_hT.rearrange("(k p) t -> p k t", p=128)
        with ExitStack() as pa:
            x_r = rot("xa", [128, 8, 512], F32, 2, pa)
            h_r = rot("ha", [128, 8, 512], BF16, 2, pa)
            rs_r = {"sq": rot("sqa", [128, 8, 512], BF16, 2, pa), "rs": rot("rsa", [128, 512], F32, 2, pa)}
            for tt in range(NT_ALL):
                xt, xk = x_r.next(); ht, hk = h_r.next()
                t0 = tt * 512
                S.dma("sp", lambda e: e.dma_start(out=xt[:], in_=xT_v[:, :, t0:t0 + 512]), w=[xk])
                norm_mod(xt, xk, ht, hk, lambda kc: avec[:, kc:kc + 1], sh1, 512, rs_r, pb[tt % 2], PBK[tt % 2])
                S.dma("act", lambda e: e.dma_start(out=hT_v[:, :, t0:t0 + 512], in_=ht[:]), r=[hk])
            S.barrier()
            if os.environ.get('K_STOP') == '2':
                S.finish(); return nc

        psrot = Rot(pb[0:6], "pb", PBK[0:6])

        psrotD = Rot(pb[0:4], "pbD", PBK[0:4])
        psrotA = Rot(pb[4:7], "pbA", PBK[4:7])
        AFFINITY = not os.environ.get("K_NOAFF")

        def nps(eng="dve"):
            if not AFFINITY:
                return psrot.next()
            return (psrotD if eng == "dve" else psrotA).next()

        wst_r = rot("wst", [128, 1024], F32, 3)
        lwc = [0]

        def load_w(Wt, wkey, c0, n, src=None, kcs=8):
            src = w_all if src is None else src
            for kc in range(kcs):
                for p0 in range(0, n, 1024):
                    pn = min(1024, n - p0)
                    wst, wstk = wst_r.next()
                    q = "sp" if lwc[0] % 2 == 0 else "act"
                    S.dma(q, lambda e: e.dma_start(out=wst[:, 0:pn], in_=src[kc * 128:(kc + 1) * 128, c0 + p0:c0 + p0 + pn]), w=[wstk])
                    eng = "pool" if (lwc[0] % 2 == 0 and not os.environ.get("K_NOPOOL")) else "dve"
                    lwc[0] += 1
                    dst = Wt[:, kc, p0:p0 + pn] if kcs > 1 else Wt[:, p0:p0 + pn]
                    S.op(eng, lambda e: e.tensor_copy(out=dst, in_=wst[:, 0:pn]), r=[wstk], w=[wkey])

        def fm_chunk(ps, psk, Wt, wkey, wc, n, ht, hk):
            for kc in range(8):
                S.op("pe", lambda e: e.matmul(ps[0:n, :], lhsT=Wt[:, kc, wc:wc + n], rhs=ht[:, kc, :], start=(kc == 0), stop=(kc == 7)), r=[wkey, hk], w=[psk])

        def tm_chunk(ps, psk, Wt, wkey, wc, n, ht, hk, sub):
            for kc in range(8):
                S.op("pe", lambda e: e.matmul(ps[:, 0:n], lhsT=ht[:, kc, sub * 128:(sub + 1) * 128], rhs=Wt[:, kc, wc:wc + n], start=(kc == 0), stop=(kc == 7)), r=[wkey, hk], w=[psk])

        stq = ["sp", "act"]
        stc = [0]

        def store(dst, src, key):
            q = stq[stc[0] % 2] if not os.environ.get("K_Q1") else "sp"; stc[0] += 1
            S.dma(q, lambda e: e.dma_start(out=dst, in_=src), r=[key])

        with ExitStack() as pp:
            W1 = sb("W1", [128, 8, NW1], BF16, pp)
            load_w(W1, "W1", 0, NW1)
            h_r = rot("hb", [128, 8, 512], BF16, 2, pp)
            sf_r = rot("sf", [128, 512], F32, 4, pp)
            sh_r = rot("sh", [128, 512], BF16, 4, pp)
            for tt in range(NT_ALL):
                own = tt < NT_OWN
                t0 = tt * 512
                ht, hk = h_r.next()
                S.dma("sp", lambda e: e.dma_start(out=ht[:], in_=hT_v[:, :, t0:t0 + 512]), w=[hk])
                if own and 'f' not in KSKIP:
                    for c in (range(8) if 'g' not in KSKIP else ()):
                        ps, psk = nps(); fm_chunk(ps, psk, W1, "W1", c * 128, 128, ht, hk)
                        sf, sfk = sf_r.next()
                        S.op("dve", lambda e: e.tensor_copy(out=sf[:], in_=ps[:]), r=[psk], w=[sfk])
                        dst = (S_gqT if c < 4 else S_gkT)[(c % 4) * 128:(c % 4 + 1) * 128, t0:t0 + 512]
                        store(dst, sf[:], sfk)
                    for c in (range(8) if 'r' not in KSKIP else ()):
                        ps, psk = nps("act" if os.environ.get("K_GRF") in ("exp", "ident") else "dve"); fm_chunk(ps, psk, W1, "W1", 2048 + c * 128, 128, ht, hk)
                        sh, shk = sh_r.next()
                        if os.environ.get("K_GRF") == "exp":
                            S.op("act", lambda e: e.activation(out=sh[:], in_=ps[:], func=AF.Exp), r=[psk], w=[shk])
                        elif os.environ.get("K_GRF") == "ident":
                            S.op("act", lambda e: e.activation(out=sh[:], in_=ps[:], func=AF.Identity), r=[psk], w=[shk])
                        else:
                            sf, sfk = sf_r.next()
                            S.op("dve", lambda e: e.tensor_copy(out=sf[:], in_=ps[:]), r=[psk], w=[sfk])
                            S.op("act", lambda e: e.activation(out=sh[:], in_=sf[:], func=AF.Silu), r=[sfk], w=[shk])
                        store(S_grT[c * 128:(c + 1) * 128, t0:t0 + 512], sh[:], shk)
                    if "l" not in KSKIP:
                        ps, psk = nps(); fm_chunk(ps, psk, W1, "W1", 3072, 16, ht, hk)
                        sf, sfk = sf_r.next()
                        S.op("dve", lambda e: e.tensor_copy(out=sf[0:16, :], in_=ps[0:16, :]), r=[psk], w=[sfk])
                        store(S_lrA[:, t0:t0 + 512], sf[0:16, :], sfk)
                if "l" not in KSKIP:
                    ps, psk = nps(); fm_chunk(ps, psk, W1, "W1", 3088, 16, ht, hk)
                    sf, sfk = sf_r.next()
                    S.op("dve", lambda e: e.tensor_copy(out=sf[0:16, :], in_=ps[0:16, :]), r=[psk], w=[sfk])
                    store(S_lrB[:, t0:t0 + 512], sf[0:16, :], sfk)
                for sub in (range(4) if "t" not in KSKIP else ()):
                    r0 = t0 + sub * 128
                    ps, psk = nps(); tm_chunk(ps, psk, W1, "W1", 512, 512, ht, hk, sub)
                    sf, sfk = sf_r.next()
                    S.op("dve", lambda e: e.tensor_copy(out=sf[:], in_=ps[:]), r=[psk], w=[sfk])
                    store(S_gk[r0:r0 + 128, :], sf[:], sfk)
                    for c in range(2):
                        ps, psk = nps("act"); tm_chunk(ps, psk, W1, "W1", 1024 + c * 512, 512, ht, hk, sub)
                        sh, shk = sh_r.next()
                        S.op("act", lambda e: e.activation(out=sh[:], in_=ps[:], func=AF.Identity), r=[psk], w=[shk])
                        store(S_gv[r0:r0 + 128, c * 512:(c + 1) * 512], sh[:], shk)
            S.barrier()
            if os.environ.get('K_STOP') == '3':
                S.finish(); return nc

        with ExitStack() as pp:
            W2 = sb("W2", [128, 8, NW2], BF16, pp)
            load_w(W2, "W2", NW1, NW2)
            h_r = rot("hb", [128, 8, 512], BF16, 2, pp)
            sh_r = rot("sh", [128, 512], BF16, 4, pp)
            posi = sb("posi", [128, 512], I32, pp); ang = sb("ang", [128, 512], F32, pp)
            kf = sb("kf", [128, 512], F32, pp)
            cos_t = sb("cos_t", [128, 512], F32, pp); sin_t = sb("sin_t", [128, 512], F32, pp)
            t1_r = rot("t1", [128, 512], F32, 2, pp); t2_r = rot("t2", [128, 512], F32, 2, pp)
            for tt in range(NT_ALL):
                own = tt < NT_OWN
                t0 = tt * 512
                ht, hk = h_r.next()
                S.dma("sp", lambda e: e.dma_start(out=ht[:], in_=hT_v[:, :, t0:t0 + 512]), w=[hk])
                S.dma("sp", lambda e: e.dma_start(out=posi[:], in_=pos[:, t0:t0 + 512].partition_broadcast(128)), w=["posi"])
                S.op("dve", lambda e: e.tensor_copy(out=ang[:], in_=posi[:]), r=["posi"], w=["ang"])
                S.op("dve", lambda e: e.tensor_scalar(out=ang[:], in0=ang[:], scalar1=ropec[:, 0:1], scalar2=None, op0=ALU.mult), r=["ang", "ropec"], w=["ang"])
                S.op("dve", lambda e: e.tensor_scalar(out=kf[:], in0=ang[:], scalar1=1.0 / TWO_PI, scalar2=None, op0=ALU.mult), r=["ang"], w=["kf"])
                S.op("dve", lambda e: e.tensor_copy(out=posi[:], in_=kf[:]), r=["kf"], w=["posi"])
                S.op("dve", lambda e: e.tensor_copy(out=kf[:], in_=posi[:]), r=["posi"], w=["kf"])
                S.op("dve", lambda e: e.scalar_tensor_tensor(out=ang[:], in0=kf[:], scalar=-TWO_PI, in1=ang[:], op0=ALU.mult, op1=ALU.add), r=["kf", "ang"], w=["ang"])
                S.op("dve", lambda e: e.tensor_scalar(out=ang[:], in0=ang[:], scalar1=math.pi, scalar2=-math.pi, op0=ALU.min, op1=ALU.max), r=["ang"], w=["ang"])
                S.op("act", lambda e: e.activation(out=sin_t[:], in_=ang[:], func=AF.Sin, scale=ropec[:, 1:2]), r=["ang", "ropec"], w=["sin_t"])
                S.op("act", lambda e: e.activation(out=kf[:], in_=ang[:], func=AF.Abs), r=["ang"], w=["kf"])
                S.op("act", lambda e: e.activation(out=cos_t[:], in_=kf[:], func=AF.Sin, bias=math.pi / 2, scale=-1.0), r=["kf"], w=["cos_t"])
                for which in ((0, 1) if own else (1,)):
                    base = which * 2048
                    for h in range(8):
                        ps, psk = nps(); fm_chunk(ps, psk, W2, "W2", base + h * 128, 128, ht, hk)
                        ps2, psk2 = nps(); fm_chunk(ps2, psk2, W2, "W2", base + 1024 + h * 128, 128, ht, hk)
                        t1, t1k = t1_r.next(); t2, t2k = t2_r.next()
                        S.op("dve", lambda e: e.tensor_tensor(out=t1[:], in0=ps[:], in1=cos_t[:], op=ALU.mult), r=[psk, "cos_t"], w=[t1k])
                        S.op("dve", lambda e: e.tensor_tensor(out=t2[:], in0=ps2[:], in1=sin_t[:], op=ALU.mult), r=[psk2, "sin_t"], w=[t2k])
                        sh, shk = sh_r.next()
                        S.op("pool", lambda e: e.tensor_tensor(out=sh[:], in0=t1[:], in1=t2[:], op=ALU.add), r=[t1k, t2k], w=[shk])
                        dst = (S_qT if which == 0 else S_kT)[h * 128:(h + 1) * 128, t0:t0 + 512]
                        store(dst, sh[:], shk)
                for sub in range(4):
                    r0 = t0 + sub * 128
                    for c in range(2):
                        ps, psk = nps("act"); tm_chunk(ps, psk, W2, "W2", 4096 + c * 512, 512, ht, hk, sub)
                        sh, shk = sh_r.next()
                        S.op("act", lambda e: e.activation(out=sh[:], in_=ps[:], func=AF.Identity), r=[psk], w=[shk])
                        store(S_dv[r0:r0 + 128, c * 512:(c + 1) * 512], sh[:], shk)
            S.barrier()
            if os.environ.get('K_STOP') == '4':
                S.finish(); return nc

        with ExitStack() as pp:
            W3 = sb("W3", [128, 8, NW3], BF16, pp)
            sf3_r = rot("sf3", [128, 512], F32, 4, pp)
            load_w(W3, "W3", NW1 + NW2, NW3)
            h_r = rot("hb", [128, 8, 512], BF16, 2, pp)
            sh_r = rot("sh", [128, 512], BF16, 4, pp)
            for tt in range(NT_OWN):
                t0 = tt * 512
                ht, hk = h_r.next()
                S.dma("sp", lambda e: e.dma_start(out=ht[:], in_=hT_v[:, :, t0:t0 + 512]), w=[hk])
                for c in range(16):
                    ps, psk = nps(); fm_chunk(ps, psk, W3, "W3", c * 128, 128, ht, hk)
                    sh, shk = sh_r.next()
                    sf, sfk = sf3_r.next()
                    S.op("dve", lambda e: e.tensor_copy(out=sf[:], in_=ps[:]), r=[psk], w=[sfk])
                    S.op("act", lambda e: e.activation(out=sh[:], in_=sf[:], func=AF.Sigmoid), r=[sfk], w=[shk])
                    dst = (S_saT if c < 8 else S_sbT)[(c % 8) * 128:(c % 8 + 1) * 128, t0:t0 + 512]
                    store(dst, sh[:], shk)
            S.barrier()
            if os.environ.get('K_STOP') == '5':
                S.finish(); return nc

        DKS = 128 ** -0.5
        oT_v = S_oT.rearrange("(c p) t -> p c t", p=128)
        gqT_v = S_gqT.rearrange("(h p) t -> p h t", p=128)
        gkT_v = S_gkT.rearrange("(h p) t -> p h t", p=128)
        with ExitStack() as pc:
            S32 = sb("S32", [128, 4, 256], F32, pc); Sbf = sb("Sbf", [128, 4, 256], BF16, pc)
            wa_t = sb("wa_t", [16, 512], F32, pc); ba_t = sb("ba_t", [1, 512], F32, pc)
            lr_r = rot("lr", [16, 128], F32, 2, pc)
            gk_r = rot("gkt", [128, 512], F32, 2, pc); gv_r = rot("gvt", [128, 1024], BF16, 2, pc)
            gq_r = rot("gqT", [128, 4, 128], F32, 2, pc); gkT_r = rot("gkT", [128, 4, 128], F32, 2, pc)
            sp_r = rot("spt", [128, 512], F32, 2, pc)
            eb_r = rot("expb", [128, 512], F32, 2, pc); enb_r = rot("expnb", [128, 512], F32, 2, pc)
            qe_r = rot("qe", [128, 4, 128], BF16, 2, pc); ke_r = rot("ke", [128, 4, 128], BF16, 2, pc)
            ee_r = rot("eend", [128, 512], F32, 2, pc); kend_r = rot("kend", [128, 512], BF16, 2, pc)
            attm_r = rot("attm", [128, 4, 128], BF16, 2, pc)
            osb_r = rot("osb", [128, 8, 128], F32, 2, pc); oa_r = rot("oa", [128, 8, 128], F32, 2, pc)
            for di in range(2):
                tri = triA if di == 0 else triB
                stri = striA if di == 0 else striB
                S.dma("sp", lambda e: e.dma_start(out=wa_t[:], in_=(wa_A if di == 0 else wa_B)), w=["wa_t"])
                S.dma("sp", lambda e: e.dma_start(out=ba_t[:], in_=(ba_A if di == 0 else ba_B)), w=["ba_t"])
                S.op("dve", lambda e: e.memset(S32[:], 0.0), w=["S32"])
                S.op("pool", lambda e: e.memset(Sbf[:], 0.0), w=["Sbf"])
                S_lr = S_lrA if di == 0 else S_lrB
                tiles = list(range(T_OWN // 128)) if di == 0 else list(range(T_ALL // 128 - 1, -1, -1))
                order = (0, 1) if di == 0 else (1, 0)
                for ti in tiles:
                    t0 = ti * 128
                    wout = ti < T_OWN // 128
                    lr, lrk = lr_r.next(); gk, gkk = gk_r.next(); gv, gvk = gv_r.next()
                    S.dma("sp", lambda e: e.dma_start(out=lr[:], in_=S_lr[:, t0:t0 + 128]), w=[lrk])
                    S.dma("act", lambda e: e.dma_start(out=gk[:], in_=S_gk[t0:t0 + 128, :]), w=[gkk])
                    S.dma("sp", lambda e: e.dma_start(out=gv[:], in_=S_gv[t0:t0 + 128, :]), w=[gvk])
                    if wout:
                        gq, gqk = gq_r.next(); gkT, gkTk = gkT_r.next()
                        S.dma("act", lambda e: e.dma_start(out=gq[:], in_=gqT_v[:, :, t0:t0 + 128]), w=[gqk])
                        S.dma("sp", lambda e: e.dma_start(out=gkT[:], in_=gkT_v[:, :, t0:t0 + 128]), w=[gkTk])
                    S.op("pe", lambda e: e.matmul(pb[0][:, :], lhsT=lr[0:16, :], rhs=wa_t[0:16, :], start=True, stop=False), r=[lrk, "wa_t"], w=["pb0"])
                    S.op("pe", lambda e: e.matmul(pb[0][:, :], lhsT=ones_f[0:1, 0:128], rhs=ba_t[0:1, :], start=False, stop=True), r=["ones_f", "ba_t"], w=["pb0"])
                    sp, spk = sp_r.next()
                    S.op("act", lambda e: e.activation(out=sp[:], in_=pb[0][:, :], func=AF.Exp, scale=-1.0), r=["pb0"], w=[spk])
                    S.op("act", lambda e: e.activation(out=sp[:], in_=sp[:], func=AF.Ln, bias=1.0, scale=1.0), r=[spk], w=[spk])
                    for h in range(4):
                        S.op("pe", lambda e: e.matmul(pb[1][:, h * 128:(h + 1) * 128], lhsT=sp[:, h * 128:(h + 1) * 128], rhs=tri, start=True, stop=True), r=[spk, "tri4"], w=["pb1"])
                    eb, ebk = eb_r.next()
                    S.op("act", lambda e: e.activation(out=eb[:], in_=pb[1][:, :], func=AF.Exp, scale=-1.0 / 16), r=["pb1"], w=[ebk])
                    if wout:
                        enb, enbk = enb_r.next()
                        S.op("act", lambda e: e.activation(out=enb[:], in_=pb[1][:, :], func=AF.Exp, scale=1.0 / 16), r=["pb1"], w=[enbk])
                        qe, qek = qe_r.next(); ke, kek = ke_r.next()
                        S.op("dve", lambda e: e.scalar_tensor_tensor(out=qe[:], in0=gq[:], scalar=DKS, in1=eb[:].rearrange("p (h t) -> p h t", h=4), op0=ALU.mult, op1=ALU.mult), r=[gqk, ebk], w=[qek])
                        S.op("pool", lambda e: e.tensor_tensor(out=ke[:], in0=gkT[:], in1=enb[:].rearrange("p (h t) -> p h t", h=4), op=ALU.mult), r=[gkTk, enbk], w=[kek])
                    S.op("pe", lambda e: e.matmul(pb[0][:, :], lhsT=stri, rhs=sp[:], start=True, stop=True), r=[spk, "tri4"], w=["pb0"])
                    ee, eek = ee_r.next(); kend, kendk = kend_r.next()
                    S.op("act", lambda e: e.activation(out=ee[:], in_=pb[0][:, :], func=AF.Exp, scale=-1.0 / 16), r=["pb0"], w=[eek])
                    S.op("dve", lambda e: e.tensor_tensor(out=kend[:], in0=gk[:], in1=ee[:], op=ALU.mult), r=[gkk, eek], w=[kendk])
                    if wout:
                        for h in range(4):
                            S.op("pe", lambda e: e.matmul(pb[3][:, h * 128:(h + 1) * 128], lhsT=ke[:, h, :], rhs=qe[:, h, :], start=True, stop=True), r=[kek, qek], w=["pb3"])
                        attm, attmk = attm_r.next()
                        S.op("dve", lambda e: e.tensor_tensor(out=attm[:], in0=pb[3][:, :].rearrange("p (h t) -> p h t", h=4), in1=tri.unsqueeze(1).to_broadcast([128, 4, 128]), op=ALU.mult), r=["pb3", "tri4"], w=[attmk])
                        for ch in range(8):
                            h, dc = ch // 2, ch % 2
                            ob = pb[4 + ch // 4]; obk = PBK[4 + ch // 4]
                            S.op("pe", lambda e: e.matmul(ob[:, (ch % 4) * 128:(ch % 4 + 1) * 128], lhsT=gv[:, h * 256 + dc * 128:h * 256 + (dc + 1) * 128], rhs=attm[:, h, :], start=(ch % 4 == 0), stop=False), r=[gvk, attmk], w=[obk])
                    for ci, chunk in enumerate(order):
                        r0 = chunk * 64
                        if wout:
                            for ch in range(8):
                                h, dc = ch // 2, ch % 2
                                ob = pb[4 + ch // 4]; obk = PBK[4 + ch // 4]
                                c0 = (ch % 4) * 128 + r0
                                S.op("pe", lambda e: e.matmul(ob[:, c0:c0 + 64], lhsT=Sbf[:, h, dc * 128:(dc + 1) * 128], rhs=qe[:, h, r0:r0 + 64], start=False, stop=(ci == 1 and ch % 4 == 3)), r=["Sbf", qek], w=[obk])
                        for h in range(4):
                            ub = pb[2] if h < 2 else pb[6]
                            ubk = "pb2" if h < 2 else "pb6"
                            S.op("pe", lambda e: e.matmul(ub[:, (h % 2) * 256:(h % 2 + 1) * 256], lhsT=kend[r0:r0 + 64, h * 128:(h + 1) * 128], rhs=gv[r0:r0 + 64, h * 256:(h + 1) * 256], start=True, stop=True), r=[kendk, gvk], w=[ubk])
                        dcol = (r0 + 63) if di == 0 else r0
                        for h in range(4):
                            ub = pb[2] if h < 2 else pb[6]
                            ubk = "pb2" if h < 2 else "pb6"
                            S.op("dve", lambda e: e.scalar_tensor_tensor(out=S32[:, h, :], in0=S32[:, h, :], scalar=eb[:, h * 128 + dcol:h * 128 + dcol + 1], in1=ub[:, (h % 2) * 256:(h % 2 + 1) * 256], op0=ALU.mult, op1=ALU.add), r=["S32", ebk, ubk], w=["S32"])
                        S.op("act", lambda e: e.activation(out=Sbf[:], in_=S32[:], func=AF.Identity), r=["S32"], w=["Sbf"])
                    if wout:
                        osb, osbk = osb_r.next()
                        if di == 0:
                            for half in range(2):
                                S.op("dve", lambda e: e.tensor_copy(out=osb[:, half * 4:(half + 1) * 4, :], in_=pb[4 + half][:, :].rearrange("p (c t) -> p c t", c=4)), r=[PBK[4 + half]], w=[osbk])
                        else:
                            oa, oak = oa_r.next()
                            S.dma("sp", lambda e: e.dma_start(out=oa[:], in_=oT_v[:, :, t0:t0 + 128]), w=[oak])
                            for half in range(2):
                                S.op("dve", lambda e: e.tensor_tensor(out=osb[:, half * 4:(half + 1) * 4, :], in0=pb[4 + half][:, :].rearrange("p (c t) -> p c t", c=4), in1=oa[:, half * 4:(half + 1) * 4, :], op=ALU.add), r=[PBK[4 + half], oak], w=[osbk])
                        S.dma("act", lambda e: e.dma_start(out=oT_v[:, :, t0:t0 + 128], in_=osb[:]), r=[osbk])
                S.barrier()
                if os.environ.get('K_STOP') == '6':
                    S.finish(); return nc

        NKB = T_ALL // 128
        NQG = T_OWN // 512
        dv_v = S_dv.rearrange("(kb p) f -> p kb f", p=128)
        with ExitStack() as pd:
            kT = sb("kT", [128, T_ALL], BF16, pd); qT = sb("qT", [128, T_OWN], BF16, pd)
            vaug = sb("vaug", [128, NKB, 129], BF16, pd)
            pT_r = rot("pT", [128, 512], BF16, 4, pd)
            rz = sb("rz", [128, 4], F32, pd)
            t1_r = rot("dt1", [128, 128], F32, 2, pd); o_r = rot("do", [128, 128], F32, 2, pd)
            od_r = rot("od", [128, 128], BF16, 2, pd); junk = sb("djunk", [128, 128], F32, pd)
            odT_r = rot("odT", [128, 512], BF16, 2, pd)
            epsc = sb("epsc", [128, 1], F32, pd)
            S.op("pool", lambda e: e.memset(vaug[:, :, 128:129], 1.0), w=["vaug"])
            sps = [(pb[i], PBK[i]) for i in range(3)]
            spi = 0

            def oreg(r):
                b = 3 + r // 3
                return pb[b][:, (r % 3) * 129:(r % 3) * 129 + 129], PBK[b]

            for h in range(8):
                S.dma("sp", lambda e: e.dma_start(out=kT[:], in_=S_kT[h * 128:(h + 1) * 128, :]), w=["kT"])
                S.dma("act", lambda e: e.dma_start(out=qT[:], in_=S_qT[h * 128:(h + 1) * 128, :]), w=["qT"])
                S.dma("sp", lambda e: e.dma_start(out=vaug[:, :, 0:128], in_=dv_v[:, :, h * 128:(h + 1) * 128]), w=["vaug"])
                for qg in range(NQG):
                    q0 = qg * 512
                    for kb in range(NKB):
                        for m in range(2):
                            sp_, spk_ = sps[spi % 3]; spi += 1
                            S.op("pe", lambda e: e.matmul(sp_[:, :], lhsT=kT[m * 64:(m + 1) * 64, kb * 128:(kb + 1) * 128], rhs=qT[m * 64:(m + 1) * 64, q0:q0 + 512], start=True, stop=True), r=["kT", "qT"], w=[spk_])
                            pT, pTk = pT_r.next()
                            S.op("act", lambda e: e.activation(out=pT[:], in_=sp_[:, :], func=AF.Exp, scale=0.125), r=[spk_], w=[pTk])
                            for qi in range(4):
                                r = m * 4 + qi
                                oap, obk = oreg(r)
                                S.op("pe", lambda e: e.matmul(oap, lhsT=pT[:, qi * 128:(qi + 1) * 128], rhs=vaug[:, kb, :], start=(kb == 0 and r % 3 == 0), stop=(kb == NKB - 1 and (r % 3 == 2 or r == 7))), r=[pTk, "vaug"], w=[obk])
                    odT, odTk = odT_r.next()
                    for qi in range(4):
                        o0, o0k = oreg(qi); o1, o1k = oreg(4 + qi)
                        S.op("dve", lambda e: e.reciprocal(out=rz[:, 0:1], in_=o0[:, 128:129]), r=[o0k], w=["rz"])
                        S.op("dve", lambda e: e.reciprocal(out=rz[:, 1:2], in_=o1[:, 128:129]), r=[o1k], w=["rz"])
                        S.op("dve", lambda e: e.tensor_tensor(out=rz[:, 2:3], in0=rz[:, 1:2], in1=lam[:], op=ALU.mult), r=["rz", "lam"], w=["rz"])
                        t1, t1k = t1_r.next(); o_, ok_ = o_r.next(); od, odk = od_r.next()
                        S.op("dve", lambda e: e.tensor_scalar(out=t1[:], in0=o1[:, 0:128], scalar1=rz[:, 2:3], scalar2=None, op0=ALU.mult), r=[o1k, "rz"], w=[t1k])
                        S.op("dve", lambda e: e.scalar_tensor_tensor(out=o_[:], in0=o0[:, 0:128], scalar=rz[:, 0:1], in1=t1[:], op0=ALU.mult, op1=ALU.subtract), r=[o0k, "rz", t1k], w=[ok_])
                        S.op("act", lambda e: e.activation(out=junk[:], in_=o_[:], func=AF.Square, accum_out=rz[:, 3:4]), r=[ok_], w=["djunk", "rz"])
                        S.op("act", lambda e: e.activation(out=rz[:, 3:4], in_=rz[:, 3:4], func=AF.Sqrt, bias=EPS, scale=1.0 / 128), r=["rz"], w=["rz"])
                        S.op("dve", lambda e: e.reciprocal(out=rz[:, 3:4], in_=rz[:, 3:4]), r=["rz"], w=["rz"])
                        S.op("dve", lambda e: e.scalar_tensor_tensor(out=od[:], in0=o_[:], scalar=rz[:, 3:4], in1=dngr[:, h * 128:(h + 1) * 128], op0=ALU.mult, op1=ALU.mult), r=[ok_, "rz", "dngr"], w=[odk])
                        S.op("pe", lambda e: e.transpose(ptb[:, 0:128], od[:], ident_bf[:]), r=[odk, "ident_bf"], w=["ptb"])
                        S.op("dve", lambda e: e.tensor_copy(out=odT[:, qi * 128:(qi + 1) * 128], in_=ptb[:, 0:128]), r=["ptb"], w=[odTk])
                    store(S_odT[h * 128:(h + 1) * 128, q0:q0 + 512], odT[:], odTk)
            S.barrier()
            if os.environ.get('K_STOP') == '7':
                S.finish(); return nc

        NTE = T_OWN // 256
        cv = lambda ap: ap.rearrange("(c p) t -> p c t", p=128)
        grT_v = cv(S_grT); odT_v = cv(S_odT); saT_v = cv(S_saT); sbT_v = cv(S_sbT)
        with ExitStack() as pe_:
            Wgp = sb("Wgp", [128, 8, 1024], BF16, pe_); Wdp = sb("Wdp", [128, 8, 1024], BF16, pe_); Wo = sb("Wo", [128, 8, 1024], BF16, pe_)
            load_w(Wgp, "Wgp", 0, 1024, w_gp); load_w(Wdp, "Wdp", 0, 1024, w_dp); load_w(Wo, "Wo", 0, 1024, w_out)
            o_r = rot("eo", [128, 8, 256], F32, 2, pe_); gr_r = rot("egr", [128, 8, 256], BF16, 2, pe_)
            sq_r = rot("esq", [128, 8, 256], BF16, 1, pe_); rsd = sb("ersd", [128, 4, 256], F32, pe_)
            og_r = rot("eog", [128, 8, 256], BF16, 1, pe_); od_r2 = rot("eod", [128, 8, 256], BF16, 2, pe_)
            sa_r = rot("esa", [128, 8, 256], BF16, 2, pe_); sb_r = rot("esb", [128, 8, 256], BF16, 2, pe_)
            mg_r = rot("emg", [128, 8, 256], BF16, 1, pe_); m1_r = rot("em1", [128, 256], F32, 2, pe_); m2_r = rot("em2", [128, 256], F32, 2, pe_)
            x_r2 = rot("ex", [128, 8, 256], F32, 2, pe_); x1tm_r = rot("ex1tm", [128, 1024], F32, 2, pe_)
            for te in range(NTE):
                t0 = te * 256
                o_, ok_ = o_r.next(); gr, grk = gr_r.next(); odt, odtk = od_r2.next(); sa, sak = sa_r.next(); sbb, sbk = sb_r.next(); xt, xk = x_r2.next()
                S.dma("sp", lambda e: e.dma_start(out=o_[:], in_=oT_v[:, :, t0:t0 + 256]), w=[ok_])
                S.dma("act", lambda e: e.dma_start(out=gr[:], in_=grT_v[:, :, t0:t0 + 256]), w=[grk])
                S.dma("sp", lambda e: e.dma_start(out=odt[:], in_=odT_v[:, :, t0:t0 + 256]), w=[odtk])
                S.dma("act", lambda e: e.dma_start(out=sa[:], in_=saT_v[:, :, t0:t0 + 256]), w=[sak])
                S.dma("sp", lambda e: e.dma_start(out=sbb[:], in_=sbT_v[:, :, t0:t0 + 256]), w=[sbk])
                S.dma("act", lambda e: e.dma_start(out=xt[:], in_=xT_v[:, :, t0:t0 + 256]), w=[xk])
                sq, sqk = sq_r.next()
                S.op("act", lambda e: e.activation(out=sq[:], in_=o_[:], func=AF.Square), r=[ok_], w=[sqk])
                for h in range(4):
                    bk = pb[4 + h // 2]; bkk = PBK[4 + h // 2]
                    for dc in range(2):
                        S.op("pe", lambda e: e.matmul(bk[:, (h % 2) * 256:(h % 2 + 1) * 256], lhsT=ones_bf[:], rhs=sq[:, h * 2 + dc, :], start=(dc == 0), stop=(dc == 1)), r=["ones_bf", sqk], w=[bkk])
                for half in range(2):
                    S.op("act", lambda e: e.activation(out=rsd[:, half * 2:(half + 1) * 2, :], in_=pb[4 + half][:, :].rearrange("p (h t) -> p h t", h=2), func=AF.Sqrt, bias=EPS, scale=1.0 / 256), r=[PBK[4 + half]], w=["ersd"])
                S.op("dve", lambda e: e.reciprocal(out=rsd[:], in_=rsd[:]), r=["ersd"], w=["ersd"])
                for ch in range(8):
                    S.op("dve", lambda e: e.scalar_tensor_tensor(out=o_[:, ch, :], in0=o_[:, ch, :], scalar=gvec[:, 24 + ch:25 + ch], in1=rsd[:, ch // 2, :], op0=ALU.mult, op1=ALU.mult), r=[ok_, "gvec", "ersd"], w=[ok_])
                og, ogk = og_r.next()
                S.op("pool", lambda e: e.tensor_tensor(out=og[:], in0=o_[:], in1=gr[:], op=ALU.mult), r=[ok_, grk], w=[ogk])
                mg, mgk = mg_r.next()
                for oc in range(8):
                    ps, psk = nps()
                    for kc in range(8):
                        S.op("pe", lambda e: e.matmul(ps[:, 0:256], lhsT=Wgp[:, kc, oc * 128:(oc + 1) * 128], rhs=og[:, kc, :], start=(kc == 0), stop=False), r=["Wgp", ogk], w=[psk])
                    for kc in range(8):
                        S.op("pe", lambda e: e.matmul(ps[:, 256:512], lhsT=Wdp[:, kc, oc * 128:(oc + 1) * 128], rhs=odt[:, kc, :], start=False, stop=(kc == 7)), r=["Wdp", odtk], w=[psk])
                    m1, m1k = m1_r.next(); m2, m2k = m2_r.next()
                    S.op("dve", lambda e: e.tensor_tensor(out=m1[:], in0=ps[:, 0:256], in1=sa[:, oc, :], op=ALU.mult), r=[psk, sak], w=[m1k])
                    S.op("dve", lambda e: e.tensor_tensor(out=m2[:], in0=ps[:, 256:512], in1=sbb[:, oc, :], op=ALU.mult), r=[psk, sbk], w=[m2k])
                    S.op("pool", lambda e: e.tensor_tensor(out=mg[:, oc, :], in0=m1[:], in1=m2[:], op=ALU.add), r=[m1k, m2k], w=[mgk])
                for oc in range(8):
                    ps, psk = nps()
                    for kc in range(8):
                        S.op("pe", lambda e: e.matmul(ps[:, 0:256], lhsT=Wo[:, kc, oc * 128:(oc + 1) * 128], rhs=mg[:, kc, :], start=(kc == 0), stop=(kc == 7)), r=["Wo", mgk], w=[psk])
                    S.op("dve", lambda e: e.scalar_tensor_tensor(out=xt[:, oc, :], in0=ps[:, 0:256], scalar=gt1(oc), in1=xt[:, oc, :], op0=ALU.mult, op1=ALU.add), r=[psk, "modT", xk], w=[xk])
                for sub in range(2):
                    x1tm, x1tmk = x1tm_r.next()
                    for half in range(2):
                        ps, psk = nps("act")
                        for k4 in range(4):
                            kc = half * 4 + k4
                            S.op("pe", lambda e: e.matmul(ps[:, k4 * 128:(k4 + 1) * 128], lhsT=xt[:, kc, sub * 128:(sub + 1) * 128], rhs=ident[:], start=True, stop=True), r=[xk, "ident"], w=[psk])
                        S.op("act", lambda e: e.activation(out=x1tm[:, half * 512:(half + 1) * 512], in_=ps[:, :], func=AF.Identity), r=[psk], w=[x1tmk])
                    store(S_x1[t0 + sub * 128:t0 + (sub + 1) * 128, :], x1tm[:], x1tmk)
            S.barrier()
            if os.environ.get('K_STOP') == '8':
                S.finish(); return nc

        NTF = T_OWN // 128
        with ExitStack() as pf:
            Wq = sb("Wq", [128, 8, 2048], BF16, pf); skb = sb("skb", [128, 2048], BF16, pf)
            load_w(Wq, "Wq", 0, 2048, w_pq)
            load_w(skb, "skb", 0, 2048, skT, kcs=1)
            rowsb = sb("rowsb", [128, 5120], F32, pf)
            S.dma("sp", lambda e: e.dma_start(out=rowsb[:], in_=S_rows.partition_broadcast(128)), w=["rowsb"])
            a2row, sh2row, gt2row, afrow, fshrow = (rowsb[:, i * 1024:(i + 1) * 1024] for i in range(5))
            pci = sb("pci", [128, 4096], I32, pf); pcs = sb("pcs", [128, 8], I32, pf); pcf = sb("pcf", [128, 64], F32, pf)
            S.dma("sp", lambda e: e.dma_start(out=pci[:], in_=pconst_d), w=["pci"])
            S.dma("sp", lambda e: e.dma_start(out=pcs[:], in_=pcs_d), w=["pcs"])
            S.dma("sp", lambda e: e.dma_start(out=pcf[:], in_=pconstf_d), w=["pcf"])
            niota = pci[:, 0:2048]; ciota = pci[:, 2048:4096]
            cst = lambda j, n: pcs[:, j:j + 1].to_broadcast([128, n])
            x1_r = rot("fx1", [128, 1024], F32, 2, pf)
            h2 = sb("fh2", [128, 1024], F32, pf); h2b = sb("fh2b", [128, 1024], BF16, pf); h2T = sb("fh2T", [128, 8, 128], BF16, pf)
            qpT = sb("fqpT", [128, 16, 128], BF16, pf)
            sc = sb("fsc", [128, 2048], F32, pf); sc2 = sb("fsc2", [128, 2048], F32, pf)
            cand = sb("fcand", [128, 2048], F32, pf); tmpb = sb("ftmp", [128, 2048], F32, pf)
            v16 = sb("fv16", [128, 256], F32, pf); i16i = sb("fi16i", [128, 256], I32, pf); i16f = sb("fi16f", [128, 256], F32, pf)
            f16 = sb("ff16", [128, 128], F32, pf); cpi = sb("fcpi", [128, 128], I32, pf); cpj = sb("fcpj", [128, 128], I32, pf)
            aif = sb("faif", [128, 128], F32, pf); bjf = sb("fbjf", [128, 128], F32, pf)
            iA = sb("fiA", [128, 128], F32, pf); iB = sb("fiB", [128, 128], F32, pf)
            eif = sb("feif", [128, 128], F32, pf); eii = sb("feii", [128, 128], I32, pf)
            gm = sb("fgm", [128, 8], F32, pf); ge = sb("fge", [128, 128], F32, pf); gs = sb("fgs", [128, 8], F32, pf)
            gw = sb("fgw", [128, 128], F32, pf); aa = sb("faa", [128, 128], F32, pf); ww = sb("fww", [128, 128], F32, pf)
            st4 = sb("fst4", [128, 4], F32, pf)
            g_r = rot("fgat", [128, 1024], F32, 4, pf)
            junk = sb("fjunk", [128, 1024], F32, pf); acc = sb("facc", [128, 1024], F32, pf)
            x2 = sb("fx2", [128, 1024], F32, pf); ot_r = rot("fout", [128, 1024], F32, 2, pf)
            v16v = v16[:].rearrange("p (h q k) -> p h q k", h=8, q=2)
            i16v = i16f[:].rearrange("p (h q k) -> p h q k", h=8, q=2)
            B4 = [128, 8, 16, 16]
            r4 = lambda t: t[:].rearrange("p (h a b) -> p h a b", h=8, a=16)
            r3 = lambda t: t[:].rearrange("p (h k) -> p h k", h=8)
            for tf in range(NTF):
                t0 = tf * 128
                x1, x1k = x1_r.next()
                S.dma("sp", lambda e: e.dma_start(out=x1[:], in_=S_x1[t0:t0 + 128, :]), w=[x1k])
                S.op("act", lambda e: e.activation(out=junk[:], in_=x1[:], func=AF.Square, accum_out=st4[:, 0:1]), r=[x1k], w=["fjunk", "fst4"])
                S.op("act", lambda e: e.activation(out=st4[:, 1:2], in_=st4[:, 0:1], func=AF.Sqrt, bias=EPS, scale=1.0 / D), r=["fst4"], w=["fst4"])
                S.op("dve", lambda e: e.reciprocal(out=st4[:, 1:2], in_=st4[:, 1:2]), r=["fst4"], w=["fst4"])
                S.op("dve", lambda e: e.scalar_tensor_tensor(out=h2[:], in0=x1[:], scalar=st4[:, 1:2], in1=a2row, op0=ALU.mult, op1=ALU.mult), r=[x1k, "fst4", "rowsb"], w=["fh2"])
                S.op("dve", lambda e: e.tensor_tensor(out=h2[:], in0=h2[:], in1=sh2row, op=ALU.add), r=["fh2", "rowsb"], w=["fh2"])
                S.op("act", lambda e: e.activation(out=h2b[:], in_=h2[:], func=AF.Identity), r=["fh2"], w=["fh2b"])
                for kc in range(8):
                    S.op("pe", lambda e: e.transpose(ptb[:, kc * 128:(kc + 1) * 128], h2b[:, kc * 128:(kc + 1) * 128], ident_bf[:]), r=["fh2b", "ident_bf"], w=["ptb"])
                S.op("dve", lambda e: e.tensor_copy(out=h2T[:].rearrange("p k t -> p (k t)"), in_=ptb[:, :]), r=["ptb"], w=["fh2T"])
                for g4 in range(4):
                    ps, psk = nps("act")
                    for q in range(4):
                        hp = g4 * 4 + q
                        for kc in range(8):
                            S.op("pe", lambda e: e.matmul(ps[:, q * 128:(q + 1) * 128], lhsT=Wq[:, kc, hp * 128:(hp + 1) * 128], rhs=h2T[:, kc, :], start=(kc == 0), stop=(kc == 7)), r=["Wq", "fh2T"], w=[psk])
                    S.op("act", lambda e: e.activation(out=qpT[:, g4 * 4:(g4 + 1) * 4, :], in_=ps[:, :].rearrange("p (q t) -> p q t", q=4), func=AF.Identity), r=[psk], w=["fqpT"])
                for g4 in range(4):
                    ps, psk = nps()
                    for q in range(4):
                        hp = g4 * 4 + q
                        S.op("pe", lambda e: e.matmul(ps[:, q * 128:(q + 1) * 128], lhsT=qpT[:, hp, :], rhs=skb[:, hp * 128:(hp + 1) * 128], start=True, stop=True), r=["fqpT", "skb"], w=[psk])
                    S.op("dve", lambda e: e.tensor_copy(out=sc[:, g4 * 512:(g4 + 1) * 512], in_=ps[:, :]), r=[psk], w=["fsc"])
                sci = sc[:].bitcast(I32)
                S.op("dve", lambda e: e.tensor_tensor(out=sci, in0=sci, in1=cst(0, 2048), op=ALU.bitwise_and), r=["fsc", "pcs"], w=["fsc"])
                S.op("dve", lambda e: e.tensor_tensor(out=sci, in0=sci, in1=niota, op=ALU.bitwise_or), r=["fsc", "pci"], w=["fsc"])
                for hp in range(16):
                    sl = slice(hp * 128, (hp + 1) * 128)
                    S.op("dve", lambda e: e.max(out=v16[:, hp * 16:hp * 16 + 8], in_=sc[:, sl]), r=["fsc"], w=["fv16"])
                    S.op("dve", lambda e: e.match_replace(out=sc2[:, sl], in_to_replace=v16[:, hp * 16:hp * 16 + 8], in_values=sc[:, sl], imm_value=-1e30), r=["fsc", "fv16"], w=["fsc2"])
                    S.op("dve", lambda e: e.max(out=v16[:, hp * 16 + 8:hp * 16 + 16], in_=sc2[:, sl]), r=["fsc2"], w=["fv16"])
                S.op("dve", lambda e: e.tensor_tensor(out=i16i[:], in0=v16[:].bitcast(I32), in1=cst(2, 256), op=ALU.bitwise_and), r=["fv16", "pcs"], w=["fi16i"])
                S.op("dve", lambda e: e.tensor_copy(out=i16f[:], in_=i16i[:]), r=["fi16i"], w=["fi16f"])
                S.op("dve", lambda e: e.tensor_tensor(out=r4(cand), in0=v16v[:, :, 0, :].unsqueeze(3).to_broadcast(B4), in1=v16v[:, :, 1, :].unsqueeze(2).to_broadcast(B4), op=ALU.add), r=["fv16"], w=["fcand"])
                cai = cand[:].bitcast(I32)
                S.op("dve", lambda e: e.tensor_tensor(out=cai, in0=cai, in1=cst(1, 2048), op=ALU.bitwise_and), r=["fcand", "pcs"], w=["fcand"])
                S.op("dve", lambda e: e.tensor_tensor(out=cai, in0=cai, in1=ciota, op=ALU.bitwise_or), r=["fcand", "pci"], w=["fcand"])
                for h in range(8):
                    sl = slice(h * 256, (h + 1) * 256)
                    S.op("dve", lambda e: e.max(out=f16[:, h * 16:h * 16 + 8], in_=cand[:, sl]), r=["fcand"], w=["ff16"])
                    S.op("dve", lambda e: e.match_replace(out=tmpb[:, sl], in_to_replace=f16[:, h * 16:h * 16 + 8], in_values=cand[:, sl], imm_value=-1e30), r=["fcand", "ff16"], w=["ftmp"])
                    S.op("dve", lambda e: e.max(out=f16[:, h * 16 + 8:h * 16 + 16], in_=tmpb[:, sl]), r=["ftmp"], w=["ff16"])
                S.op("dve", lambda e: e.tensor_tensor(out=cpi[:], in0=f16[:].bitcast(I32), in1=cst(3, 128), op=ALU.bitwise_and), r=["ff16", "pcs"], w=["fcpi"])
                S.op("dve", lambda e: e.tensor_tensor(out=cpj[:], in0=cpi[:], in1=cst(5, 128), op=ALU.bitwise_and), r=["fcpi", "pcs"], w=["fcpj"])
                S.op("dve", lambda e: e.tensor_tensor(out=cpi[:], in0=cpi[:], in1=cst(4, 128), op=ALU.arith_shift_right), r=["fcpi", "pcs"], w=["fcpi"])
                S.op("dve", lambda e: e.tensor_copy(out=aif[:], in_=cpi[:]), r=["fcpi"], w=["faif"])
                S.op("dve", lambda e: e.tensor_copy(out=bjf[:], in_=cpj[:]), r=["fcpj"], w=["fbjf"])
                io4 = pcf[:, 0:16].unsqueeze(1).unsqueeze(1).to_broadcast(B4)
                for which, (sel, dst_) in enumerate(((aif, iA), (bjf, iB))):
                    S.op("dve", lambda e: e.tensor_tensor(out=r4(sc2), in0=io4, in1=r3(sel).unsqueeze(3).to_broadcast(B4), op=ALU.is_equal), r=["pcf", "faif", "fbjf"], w=["fsc2"])
                    S.op("dve", lambda e: e.tensor_tensor(out=r4(sc2), in0=r4(sc2), in1=i16v[:, :, which, :].unsqueeze(2).to_broadcast(B4), op=ALU.mult), r=["fsc2", "fi16f"], w=["fsc2"])
                    S.op("dve", lambda e: e.tensor_reduce(out=dst_[:], in_=sc2[:].rearrange("p (k i) -> p k i", i=16), axis=AX.X, op=ALU.add), r=["fsc2"], w=["fiA", "fiB"])
                S.op("dve", lambda e: e.scalar_tensor_tensor(out=eif[:], in0=iA[:], scalar=128.0, in1=iB[:], op0=ALU.mult, op1=ALU.add), r=["fiA", "fiB"], w=["feif"])
                S.op("dve", lambda e: e.tensor_copy(out=eii[:], in_=eif[:]), r=["feif"], w=["feii"])
                S.op("dve", lambda e: e.tensor_reduce(out=gm[:], in_=r3(f16), axis=AX.X, op=ALU.max), r=["ff16"], w=["fgm"])
                S.op("dve", lambda e: e.tensor_tensor(out=r3(ge), in0=r3(f16), in1=gm[:].unsqueeze(2).to_broadcast([128, 8, 16]), op=ALU.subtract), r=["ff16", "fgm"], w=["fge"])
                S.op("act", lambda e: e.activation(out=ge[:], in_=ge[:], func=AF.Exp), r=["fge"], w=["fge"])
                S.op("dve", lambda e: e.tensor_reduce(out=gs[:], in_=r3(ge), axis=AX.X, op=ALU.add), r=["fge"], w=["fgs"])
                S.op("dve", lambda e: e.reciprocal(out=gs[:], in_=gs[:]), r=["fgs"], w=["fgs"])
                S.op("dve", lambda e: e.tensor_tensor(out=r3(gw), in0=r3(ge), in1=gs[:].unsqueeze(2).to_broadcast([128, 8, 16]), op=ALU.mult), r=["fge", "fgs"], w=["fgw"])
                S.op("dve", lambda e: e.memset(aa[:], 0.0), w=["faa"])
                for k in range(128):
                    gt_, gtk = g_r.next()
                    S.dma("pool", lambda e: e.indirect_dma_start(out=gt_[:], out_offset=None, in_=peer_u, in_offset=bass.IndirectOffsetOnAxis(ap=eii[:, k:k + 1], axis=0)), r=["feii"], w=[gtk])
                    S.op("dve", lambda e: e.scalar_tensor_tensor(out=junk[:], in0=gt_[:], scalar=1.0, in1=h2[:], op0=ALU.mult, op1=ALU.mult, accum_out=aa[:, k:k + 1]), r=[gtk, "fh2"], w=["fjunk", "faa"])
                S.op("act", lambda e: e.activation(out=ww[:], in_=aa[:], func=AF.Gelu), r=["faa"], w=["fww"])
                S.op("dve", lambda e: e.tensor_tensor(out=ww[:], in0=ww[:], in1=gw[:], op=ALU.mult), r=["fww", "fgw"], w=["fww"])
                S.op("dve", lambda e: e.memset(acc[:], 0.0), w=["facc"])
                for k in range(128):
                    gt_, gtk = g_r.next()
                    S.dma("pool", lambda e: e.indirect_dma_start(out=gt_[:], out_offset=None, in_=peer_v, in_offset=bass.IndirectOffsetOnAxis(ap=eii[:, k:k + 1], axis=0)), r=["feii"], w=[gtk])
                    S.op("dve", lambda e: e.scalar_tensor_tensor(out=acc[:], in0=gt_[:], scalar=ww[:, k:k + 1], in1=acc[:], op0=ALU.mult, op1=ALU.add), r=[gtk, "fww", "facc"], w=["facc"])
                S.op("dve", lambda e: e.tensor_tensor(out=acc[:], in0=acc[:], in1=gt2row, op=ALU.mult), r=["facc", "rowsb"], w=["facc"])
                S.op("dve", lambda e: e.tensor_tensor(out=x2[:], in0=acc[:], in1=x1[:], op=ALU.add), r=["facc", x1k], w=["fx2"])
                S.op("act", lambda e: e.activation(out=junk[:], in_=x2[:], func=AF.Square, accum_out=st4[:, 2:3]), r=["fx2"], w=["fjunk", "fst4"])
                S.op("act", lambda e: e.activation(out=st4[:, 3:4], in_=st4[:, 2:3], func=AF.Sqrt, bias=EPS, scale=1.0 / D), r=["fst4"], w=["fst4"])
                S.op("dve", lambda e: e.reciprocal(out=st4[:, 3:4], in_=st4[:, 3:4]), r=["fst4"], w=["fst4"])
                ot, otk = ot_r.next()
                S.op("dve", lambda e: e.scalar_tensor_tensor(out=ot[:], in0=x2[:], scalar=st4[:, 3:4], in1=afrow, op0=ALU.mult, op1=ALU.mult), r=["fx2", "fst4", "rowsb"], w=[otk])
                S.op("dve", lambda e: e.tensor_tensor(out=ot[:], in0=ot[:], in1=fshrow, op=ALU.add), r=[otk, "rowsb"], w=[otk])
                S.dma("sp", lambda e: e.dma_start(out=out[t0:t0 + 128, :], in_=ot[:]), r=[otk], is_out=True)
            S.barrier()

        S.finish()
    return nc


def PHASES_AFTER_B(nc, S, st, L):
    pass


def _consts():
    p = np.arange(128)
    ident = np.eye(128, dtype=np.float32)
    j = p[:, None]; c = p[None, :]
    same = (j // 64) == (c // 64)
    triA = (same & (j <= c)).astype(np.float32)
    triB = (same & (j >= c)).astype(np.float32)
    striA = (same & (j > c)).astype(np.float32)
    striB = (same & (j < c)).astype(np.float32)
    tri4 = np.concatenate([triA, triB, striA, striB], axis=1)
    f = (p % 32).astype(np.float64)
    inv = (10000.0 ** (-(2.0 * f) / 64.0)).astype(np.float32)
    sign = np.where((p % 64) < 32, -1.0, 1.0).astype(np.float32)
    ropec = np.stack([inv, sign], axis=1).astype(np.float32)
    return ident, tri4, ropec


def make_in_maps(inputs, T_OWN=None):
    x = np.asarray(inputs["x"]); B, S_, _ = x.shape
    T_OWN = S_ // 2
    ident, tri4, ropec = _consts()
    w_in = np.asarray(inputs["w_in"])[0]
    cols = lambda a, b: w_in[:, a:b]
    perm = np.arange(1024).reshape(16, 64)
    perm = np.concatenate([perm[:, 32:], perm[:, :32]], axis=1).reshape(-1)
    dq = cols(3104, 4128); dk = cols(4128, 5152)
    pconst = np.zeros((128, 4096), np.int32)
    pconst[:, 0:2048] = (np.arange(2048) % 128)[None, :]
    pconst[:, 2048:4096] = (np.arange(2048) % 256)[None, :]
    pconstf = np.zeros((128, 64), np.float32)
    pconstf[:, 0:16] = np.arange(16, dtype=np.float32)[None, :]
    pcs = np.tile(np.array([-128, -256, 127, 255, 4, 15, 0, 0], np.int32)[None, :], (128, 1))
    common = {
        "w_ada": np.ascontiguousarray(inputs["w_ada"][0]),
        "w_fada": np.ascontiguousarray(inputs["w_final_ada"]),
        "lqk": np.concatenate([inputs["diff_lq1"][0], inputs["diff_lk1"][0], inputs["diff_lq2"][0], inputs["diff_lk2"][0]])[None, :].astype(np.float32),
        "dng": np.ascontiguousarray(inputs["diff_norm_g"]),
        "w_gp": np.ascontiguousarray(inputs["w_gla_proj"][0]), "w_dp": np.ascontiguousarray(inputs["w_diff_proj"][0]),
        "w_out": np.ascontiguousarray(inputs["w_out"][0]), "w_pq": np.ascontiguousarray(inputs["peer_wq"][0]),
        "skT": np.ascontiguousarray(np.transpose(inputs["peer_subkeys"][0].reshape(16, 128, 128), (2, 0, 1)).reshape(128, 2048)),
        "peer_u": np.ascontiguousarray(inputs["peer_u"][0]), "peer_v": np.ascontiguousarray(inputs["peer_v"][0]),
        "ident": ident, "tri4": tri4, "ropec": ropec, "pconst": pconst, "pconstf": pconstf, "pcs": pcs,
        "gvecT": np.ascontiguousarray(np.concatenate([inputs["norm1_g"][0].reshape(8, 128).T, inputs["norm2_g"][0].reshape(8, 128).T,
                                                       inputs["normf_g"].reshape(8, 128).T, inputs["gla_norm_g"][0].reshape(8, 128).T], axis=1)),
        "badaT": np.ascontiguousarray(np.concatenate([inputs["b_ada"][0].reshape(48, 128).T, inputs["b_final_ada"].reshape(16, 128).T], axis=1)),
    }
    w_all_hf = []
    for hf in range(2):
        lrA = cols(3072, 3088) if hf == 0 else cols(3088, 3104)
        lrB = cols(3088, 3104) if hf == 0 else cols(3072, 3088)
        w_all_hf.append(np.ascontiguousarray(np.concatenate(
            [cols(0, 3072), lrA, lrB, dq, dq[:, perm], dk, dk[:, perm], cols(5152, 6176), cols(6176, 8224)], axis=1)))
    fw = (inputs["gla_wa_fw"][0], inputs["gla_ba_fw"]); bw = (inputs["gla_wa_bw"][0], inputs["gla_ba_bw"])
    in_maps = []
    for b in range(B):
        for hf in range(2):
            xb = x[b]; pb_ = np.asarray(inputs["positions"])[b]
            if hf == 1:
                xb = xb[::-1]; pb_ = pb_[::-1]
            A, Bd = (fw, bw) if hf == 0 else (bw, fw)
            m = dict(common)
            m.update({
                "xT": np.ascontiguousarray(xb.T), "pos": np.ascontiguousarray(pb_[None, :].astype(np.int32)),
                "cT": np.ascontiguousarray(np.asarray(inputs["c"])[b].reshape(8, 128).T),
                "w_all": w_all_hf[hf],
                "wa_A": np.ascontiguousarray(A[0]), "ba_A": np.ascontiguousarray(A[1].reshape(1, 512)),
                "wa_B": np.ascontiguousarray(Bd[0]), "ba_B": np.ascontiguousarray(Bd[1].reshape(1, 512)),
            })
            in_maps.append(m)
    return in_maps, B, T_OWN


def kernel(**inputs):
    inputs = {k: np.asarray(v) for k, v in inputs.items()}
    in_maps, B, T_OWN = make_in_maps(inputs)
    nc = build(T_OWN)
    res = run_bass_kernel_spmd(nc, in_maps, core_ids=list(range(len(in_maps))))
    S_ = 2 * T_OWN
    out = np.zeros((B, S_, D), np.float32)
    for b in range(B):
        for hf in range(2):
            o = res.results[b * 2 + hf]["out"]
            if hf == 0:
                out[b, :T_OWN] = o
            else:
                out[b, T_OWN:] = o[::-1]
    return out
```

```python
import math, os
from contextlib import ExitStack
import numpy as np
import concourse.bass as bass
import concourse.mybir as mybir
from concourse.bass_utils import run_bass_kernel_spmd

F32 = mybir.dt.float32
BF16 = mybir.dt.bfloat16
I32 = mybir.dt.int32
AF = mybir.ActivationFunctionType
ALU = mybir.AluOpType
AX = mybir.AxisListType

D = 1024
NW1 = 3104
NW2 = 5120
NW3 = 2048
EPS = 1e-6
TWO_PI = 2.0 * math.pi


class Sched:
    NDSEM = 6
    NDQ = {"sp": 6, "act": 6, "pool": 12}

    def __init__(self, nc, stack):
        self.nc = nc
        self.engs = {"pe": nc.tensor, "dve": nc.vector, "act": nc.scalar, "pool": nc.gpsimd, "sp": nc.sync}
        self.sem = {k: stack.enter_context(nc.semaphore("s_" + k)) for k in self.engs}
        self.cnt = {k: 0 for k in self.engs}
        self.dq = ["sp", "act", "pool"]
        self.dsem = {q: [stack.enter_context(nc.semaphore("d_%s%d" % (q, i))) for i in range(self.NDQ[q])] for q in self.dq}
        self.dcnt = {q: 0 for q in self.dq}
        self.waited = {}
        self.lastw = {}
        self.readers = {}
        self.out_tokens = []

    def _need(self, stream, tok):
        sem, val, tstream, isdma = tok
        if (not isdma) and tstream == stream and stream == "pe":
            return
        key = (stream, id(sem))
        if self.waited.get(key, 0) >= val:
            return
        self.waited[key] = val
        self.engs[stream].wait_ge(sem, val)

    def _deps(self, stream, r, w):
        for k in r:
            t = self.lastw.get(k)
            if t is not None:
                self._need(stream, t)
        for k in w:
            t = self.lastw.get(k)
            if t is not None:
                self._need(stream, t)
            for t in self.readers.get(k, {}).values():
                self._need(stream, t)

    def _commit(self, tok, r, w):
        for k in r:
            self.readers.setdefault(k, {})[(tok[2], tok[3], id(tok[0]))] = tok
        for k in w:
            self.lastw[k] = tok
            self.readers[k] = {}

    def op(self, stream, fn, r=(), w=()):
        self._deps(stream, r, w)
        ins = fn(self.engs[stream])
        self.cnt[stream] += 1
        ins.then_inc(self.sem[stream], 1)
        tok = (self.sem[stream], self.cnt[stream], stream, False)
        self._commit(tok, r, w)
        return tok

    def dma(self, q, fn, r=(), w=(), is_out=False):
        j = self.dcnt[q]
        nd = self.NDQ[q]
        sem = self.dsem[q][j % nd]
        if j >= nd:
            self._need(q, (sem, 16 * (j // nd), q, True))
        self._deps(q, r, w)
        ins = fn(self.engs[q])
        ins.then_inc(sem, 16)
        self.dcnt[q] += 1
        tok = (sem, 16 * (j // nd + 1), q, True)
        self._commit(tok, r, w)
        if is_out:
            self.out_tokens.append(tok)
        return tok

    def barrier(self):
        toks = []
        for s in self.engs:
            if self.cnt[s] > 0:
                toks.append((self.sem[s], self.cnt[s], s, False))
        for q in self.dq:
            j = self.dcnt[q]
            nd = self.NDQ[q]
            for i in range(max(0, j - nd), j):
                toks.append((self.dsem[q][i % nd], 16 * (i // nd + 1), q, True))
        for s in self.engs:
            for t in toks:
                if (not t[3]) and t[2] == s:
                    continue
                self._need(s, t)

    def finish(self):
        for tok in self.out_tokens:
            self._need("sp", tok)


class Rot:
    def __init__(self, tiles, name, keys=None):
        self.tiles = tiles
        self.name = name
        self.keys = keys
        self.i = 0

    def next(self):
        j = self.i % len(self.tiles)
        self.i += 1
        return self.tiles[j], (self.keys[j] if self.keys else "%s#%d" % (self.name, j))


def build(T_OWN, debug=False):
    KSKIP = os.environ.get('K_SKIP', '')
    T_ALL = 2 * T_OWN
    NT_OWN = T_OWN // 512
    NT_ALL = T_ALL // 512
    nc = bass.Bass("TRN2", target_bir_lowering=False)

    def din(name, shape, dt=F32):
        return nc.dram_tensor(name, shape, dt, kind="ExternalInput").ap()

    def dscr(name, shape, dt=F32):
        return nc.dram_tensor(name, shape, dt, kind="ExternalOutput" if debug else "Internal").ap()

    xT = din("xT", [D, T_ALL])
    pos = din("pos", [1, T_ALL], I32)
    cT = din("cT", [128, 8])
    w_ada = din("w_ada", [D, 6144])
    w_fada = din("w_fada", [D, 2048])
    badaT = din("badaT", [128, 64])
    gvecT = din("gvecT", [128, 32])
    w_all = din("w_all", [D, NW1 + NW2 + NW3])
    wa_A = din("wa_A", [16, 512]); ba_A = din("ba_A", [1, 512])
    wa_B = din("wa_B", [16, 512]); ba_B = din("ba_B", [1, 512])
    lqk = din("lqk", [1, 256])
    dng = din("dng", [1, 1024])
    w_gp = din("w_gp", [D, D]); w_dp = din("w_dp", [D, D]); w_out = din("w_out", [D, D])
    w_pq = din("w_pq", [D, 2048])
    skT = din("skT", [128, 2048])
    peer_u = din("peer_u", [16384, D]); peer_v = din("peer_v", [16384, D])
    ident_d = din("ident", [128, 128])
    tri_d = din("tri4", [128, 512])
    ropec_d = din("ropec", [128, 2])
    pconst_d = din("pconst", [128, 4096], I32)
    pconstf_d = din("pconstf", [128, 64])
    pcs_d = din("pcs", [128, 8], I32)
    out = nc.dram_tensor("out", [T_OWN, D], F32, kind="ExternalOutput").ap()

    S_hT = dscr("S_hT", [D, T_ALL], BF16)
    S_rows = dscr("S_rows", [1, 40 * 128])
    S_gqT = dscr("S_gqT", [512, T_OWN]); S_gkT = dscr("S_gkT", [512, T_OWN])
    S_grT = dscr("S_grT", [1024, T_OWN], BF16)
    S_lrA = dscr("S_lrA", [16, T_OWN]); S_lrB = dscr("S_lrB", [16, T_ALL])
    S_gk = dscr("S_gk", [T_ALL, 512]); S_gv = dscr("S_gv", [T_ALL, 1024], BF16)
    S_qT = dscr("S_qT", [1024, T_OWN], BF16); S_kT = dscr("S_kT", [1024, T_ALL], BF16)
    S_dv = dscr("S_dv", [T_ALL, 1024], BF16)
    S_saT = dscr("S_saT", [1024, T_OWN], BF16); S_sbT = dscr("S_sbT", [1024, T_OWN], BF16)
    S_oT = dscr("S_oT", [1024, T_OWN])
    S_odT = dscr("S_odT", [1024, T_OWN], BF16)
    S_x1 = dscr("S_x1", [T_OWN, D])
    S_idx = dscr("S_idx", [T_OWN, 128], I32)
    S_gw = dscr("S_gw", [T_OWN, 128])

    with ExitStack() as st:
        S = Sched(nc, st)

        uniq = [0]

        def sb(name, shape, dt=F32, stack=st):
            uniq[0] += 1
            return stack.enter_context(nc.sbuf_tensor("s%d_%s" % (uniq[0], name), shape, dt))

        def rot(name, shape, dt, n, stack=st):
            return Rot([sb("%s%d" % (name, i), shape, dt, stack) for i in range(n)], name)

        pb = [st.enter_context(nc.psum_tensor("pb%d" % i, [128, 512], F32)) for i in range(7)]
        ptb = st.enter_context(nc.psum_tensor("ptb", [128, 1024], BF16))
        PBK = ["pb%d" % i for i in range(7)]

        ident = sb("ident", [128, 128]); tri4 = sb("tri4", [128, 512]); ropec = sb("ropec", [128, 2])
        ident_bf = sb("ident_bf", [128, 128], BF16)
        ones_bf = sb("ones_bf", [128, 128], BF16); ones_f = sb("ones_f", [1, 128])
        S.dma("sp", lambda e: e.dma_start(out=ident[:], in_=ident_d), w=["ident"])
        S.dma("sp", lambda e: e.dma_start(out=tri4[:], in_=tri_d), w=["tri4"])
        S.dma("sp", lambda e: e.dma_start(out=ropec[:], in_=ropec_d), w=["ropec"])
        S.op("dve", lambda e: e.tensor_copy(out=ident_bf[:], in_=ident[:]), r=["ident"], w=["ident_bf"])
        S.op("pool", lambda e: e.memset(ones_bf[:], 1.0), w=["ones_bf"])
        S.op("pool", lambda e: e.memset(ones_f[:], 1.0), w=["ones_f"])
        triA, triB, striA, striB = (tri4[:, i * 128:(i + 1) * 128] for i in range(4))

        modT = sb("modT", [128, 64]); gvec = sb("gvec", [128, 32]); avec = sb("avec", [128, 24])
        lam = sb("lam", [128, 1]); dngr = sb("dngr", [128, 1024])
        sh1 = lambda kc: modT[:, kc:kc + 1]
        gt1 = lambda kc: modT[:, 16 + kc:17 + kc]

        with ExitStack() as p0:
            cTt = sb("cTt", [128, 8], F32, p0); cact = sb("cact", [128, 8], F32, p0)
            badat = sb("badat", [128, 64], F32, p0)
            wa_r = rot("wada", [128, 6144], F32, 2, p0)
            wf_r = rot("wfada", [128, 2048], F32, 2, p0)
            lq = sb("lq", [128, 256], F32, p0); lj = sb("lj", [128, 64], F32, p0); l2 = sb("l2", [128, 2], F32, p0)
            rows = sb("rows", [128, 40], F32, p0)
            S.dma("sp", lambda e: e.dma_start(out=cTt[:], in_=cT), w=["cTt"])
            S.dma("sp", lambda e: e.dma_start(out=badat[:], in_=badaT), w=["badat"])
            S.dma("sp", lambda e: e.dma_start(out=gvec[:], in_=gvecT), w=["gvec"])
            S.dma("sp", lambda e: e.dma_start(out=lq[:], in_=lqk.partition_broadcast(128)), w=["lq"])
            S.dma("sp", lambda e: e.dma_start(out=dngr[:], in_=dng.partition_broadcast(128)), w=["dngr"])
            S.op("act", lambda e: e.activation(out=cact[:], in_=cTt[:], func=AF.Silu), r=["cTt"], w=["cact"])
            mps = pb[0]
            for kc in range(8):
                wt, wk = wa_r.next(); ft, fk = wf_r.next()
                S.dma("sp", lambda e: e.dma_start(out=wt[:], in_=w_ada[kc * 128:(kc + 1) * 128, :]), w=[wk])
                S.dma("act", lambda e: e.dma_start(out=ft[:], in_=w_fada[kc * 128:(kc + 1) * 128, :]), w=[fk])
                for j in range(48):
                    S.op("pe", lambda e: e.matmul(mps[:, j:j + 1], lhsT=wt[:, j * 128:(j + 1) * 128], rhs=cact[:, kc:kc + 1], start=(kc == 0 and j == 0), stop=False), r=[wk, "cact"], w=["pb0"])
                for j in range(16):
                    S.op("pe", lambda e: e.matmul(mps[:, 48 + j:49 + j], lhsT=ft[:, j * 128:(j + 1) * 128], rhs=cact[:, kc:kc + 1], start=False, stop=(kc == 7 and j == 15)), r=[fk, "cact"], w=["pb0"])
            S.op("dve", lambda e: e.tensor_tensor(out=modT[:], in0=mps[:, 0:64], in1=badat[:], op=ALU.add), r=["pb0", "badat"], w=["modT"])
            for i, (c0, g0) in enumerate(((8, 0), (32, 8), (56, 16))):
                S.op("dve", lambda e: e.scalar_tensor_tensor(out=avec[:, i * 8:(i + 1) * 8], in0=modT[:, c0:c0 + 8], scalar=1.0, in1=gvec[:, g0:g0 + 8], op0=ALU.add, op1=ALU.mult), r=["modT", "gvec"], w=["avec"])
            for i in range(2):
                S.op("dve", lambda e: e.scalar_tensor_tensor(out=lj[:], in0=lq[:, i * 128:i * 128 + 64], scalar=1.0, in1=lq[:, i * 128 + 64:i * 128 + 128], op0=ALU.mult, op1=ALU.mult, accum_out=l2[:, i:i + 1]), r=["lq"], w=["lj", "l2"])
            S.op("act", lambda e: e.activation(out=l2[:], in_=l2[:], func=AF.Exp), r=["l2"], w=["l2"])
            S.op("dve", lambda e: e.scalar_tensor_tensor(out=lam[:], in0=l2[:, 0:1], scalar=0.2, in1=l2[:, 1:2], op0=ALU.add, op1=ALU.subtract), r=["l2"], w=["lam"])
            S.op("dve", lambda e: e.tensor_scalar(out=dngr[:], in0=dngr[:], scalar1=0.8, scalar2=None, op0=ALU.mult), r=["dngr"], w=["dngr"])
            S.op("dve", lambda e: e.tensor_copy(out=rows[:, 0:8], in_=avec[:, 8:16]), r=["avec"], w=["rows"])
            S.op("dve", lambda e: e.tensor_copy(out=rows[:, 8:16], in_=modT[:, 24:32]), r=["modT"], w=["rows"])
            S.op("dve", lambda e: e.tensor_copy(out=rows[:, 16:24], in_=modT[:, 40:48]), r=["modT"], w=["rows"])
            S.op("dve", lambda e: e.tensor_copy(out=rows[:, 24:32], in_=avec[:, 16:24]), r=["avec"], w=["rows"])
            S.op("dve", lambda e: e.tensor_copy(out=rows[:, 32:40], in_=modT[:, 48:56]), r=["modT"], w=["rows"])
            S.dma("sp", lambda e: e.dma_start(out=S_rows.rearrange("o (j p) -> p (o j)", p=128), in_=rows[:], allow_slow_non_contiguous=True), r=["rows"])
            S.barrier()
            if os.environ.get('K_STOP') == '1':
                S.finish(); return nc

        def norm_mod(xt, xk, hT, hk, a_col, sh_col, ntok, stk_rs, ps, psk):
            sq, sqk = stk_rs["sq"].next()
            rs, rsk = stk_rs["rs"].next()
            S.op("act", lambda e: e.activation(out=sq[:, :, 0:ntok], in_=xt[:, :, 0:ntok], func=AF.Square), r=[xk], w=[sqk])
            for kc in range(8):
                S.op("pe", lambda e: e.matmul(ps[:, 0:ntok], lhsT=ones_bf[:], rhs=sq[:, kc, 0:ntok], start=(kc == 0), stop=(kc == 7)), r=["ones_bf", sqk], w=[psk])
            S.op("act", lambda e: e.activation(out=rs[:, 0:ntok], in_=ps[:, 0:ntok], func=AF.Sqrt, bias=EPS, scale=1.0 / D), r=[psk], w=[rsk])
            S.op("dve", lambda e: e.reciprocal(out=rs[:, 0:ntok], in_=rs[:, 0:ntok]), r=[rsk], w=[rsk])
            for kc in range(8):
                S.op("dve", lambda e: e.scalar_tensor_tensor(out=xt[:, kc, 0:ntok], in0=xt[:, kc, 0:ntok], scalar=a_col(kc), in1=rs[:, 0:ntok], op0=ALU.mult, op1=ALU.mult), r=[xk, rsk, "avec"], w=[xk])
            for kc in range(8):
                S.op("act", lambda e: e.activation(out=hT[:, kc, 0:ntok], in_=xt[:, kc, 0:ntok], func=AF.Identity, bias=sh_col(kc), scale=1.0), r=[xk, "modT"], w=[hk])

        xT_v = xT.rearrange("(k p) t -> p k t", p=128)
        hT_v = S_hT.rearrange("(k p) t -> p k t", p=128)
        with ExitStack() as pa:
            x_r = rot("xa", [128, 8, 512], F32, 2, pa)
            h_r = rot("ha", [128, 8, 512], BF16, 2, pa)
            rs_r = {"sq": rot("sqa", [128, 8, 512], BF16, 2, pa), "rs": rot("rsa", [128, 512], F32, 2, pa)}
            for tt in range(NT_ALL):
                xt, xk = x_r.next(); ht, hk = h_r.next()
                t0 = tt * 512
                S.dma("sp", lambda e: e.dma_start(out=xt[:], in_=xT_v[:, :, t0:t0 + 512]), w=[xk])
                norm_mod(xt, xk, ht, hk, lambda kc: avec[:, kc:kc + 1], sh1, 512, rs_r, pb[tt % 2], PBK[tt % 2])
                S.dma("act", lambda e: e.dma_start(out=hT_v[:, :, t0:t0 + 512], in_=ht[:]), r=[hk])
            S.barrier()
            if os.environ.get('K_STOP') == '2':
                S.finish(); return nc

        psrot = Rot(pb[0:6], "pb", PBK[0:6])

        psrotD = Rot(pb[0:4], "pbD", PBK[0:4])
        psrotA = Rot(pb[4:7], "pbA", PBK[4:7])
        AFFINITY = not os.environ.get("K_NOAFF")

        def nps(eng="dve"):
            if not AFFINITY:
                return psrot.next()
            return (psrotD if eng == "dve" else psrotA).next()

        wst_r = rot("wst", [128, 1024], F32, 3)
        lwc = [0]

        def load_w(Wt, wkey, c0, n, src=None, kcs=8):
            src = w_all if src is None else src
            for kc in range(kcs):
                for p0 in range(0, n, 1024):
                    pn = min(1024, n - p0)
                    wst, wstk = wst_r.next()
                    q = "sp" if lwc[0] % 2 == 0 else "act"
                    S.dma(q, lambda e: e.dma_start(out=wst[:, 0:pn], in_=src[kc * 128:(kc + 1) * 128, c0 + p0:c0 + p0 + pn]), w=[wstk])
                    eng = "pool" if (lwc[0] % 2 == 0 and not os.environ.get("K_NOPOOL")) else "dve"
                    lwc[0] += 1
                    dst = Wt[:, kc, p0:p0 + pn] if kcs > 1 else Wt[:, p0:p0 + pn]
                    S.op(eng, lambda e: e.tensor_copy(out=dst, in_=wst[:, 0:pn]), r=[wstk], w=[wkey])

        def fm_chunk(ps, psk, Wt, wkey, wc, n, ht, hk):
            for kc in range(8):
                S.op("pe", lambda e: e.matmul(ps[0:n, :], lhsT=Wt[:, kc, wc:wc + n], rhs=ht[:, kc, :], start=(kc == 0), stop=(kc == 7)), r=[wkey, hk], w=[psk])

        def tm_chunk(ps, psk, Wt, wkey, wc, n, ht, hk, sub):
            for kc in range(8):
                S.op("pe", lambda e: e.matmul(ps[:, 0:n], lhsT=ht[:, kc, sub * 128:(sub + 1) * 128], rhs=Wt[:, kc, wc:wc + n], start=(kc == 0), stop=(kc == 7)), r=[wkey, hk], w=[psk])

        stq = ["sp", "act"]
        stc = [0]

        def store(dst, src, key):
            q = stq[stc[0] % 2] if not os.environ.get("K_Q1") else "sp"; stc[0] += 1
            S.dma(q, lambda e: e.dma_start(out=dst, in_=src), r=[key])

        with ExitStack() as pp:
            W1 = sb("W1", [128, 8, NW1], BF16, pp)
            load_w(W1, "W1", 0, NW1)
            h_r = rot("hb", [128, 8, 512], BF16, 2, pp)
            sf_r = rot("sf", [128, 512], F32, 4, pp)
            sh_r = rot("sh", [128, 512], BF16, 4, pp)
            for tt in range(NT_ALL):
                own = tt < NT_OWN
                t0 = tt * 512
                ht, hk = h_r.next()
                S.dma("sp", lambda e: e.dma_start(out=ht[:], in_=hT_v[:, :, t0:t0 + 512]), w=[hk])
                if own and 'f' not in KSKIP:
                    for c in (range(8) if 'g' not in KSKIP else ()):
                        ps, psk = nps(); fm_chunk(ps, psk, W1, "W1", c * 128, 128, ht, hk)
                        sf, sfk = sf_r.next()
                        S.op("dve", lambda e: e.tensor_copy(out=sf[:], in_=ps[:]), r=[psk], w=[sfk])
                        dst = (S_gqT if c < 4 else S_gkT)[(c % 4) * 128:(c % 4 + 1) * 128, t0:t0 + 512]
                        store(dst, sf[:], sfk)
                    for c in (range(8) if 'r' not in KSKIP else ()):
                        ps, psk = nps("act" if os.environ.get("K_GRF") in ("exp", "ident") else "dve"); fm_chunk(ps, psk, W1, "W1", 2048 + c * 128, 128, ht, hk)
                        sh, shk = sh_r.next()
                        if os.environ.get("K_GRF") == "exp":
                            S.op("act", lambda e: e.activation(out=sh[:], in_=ps[:], func=AF.Exp), r=[psk], w=[shk])
                        elif os.environ.get("K_GRF") == "ident":
                            S.op("act", lambda e: e.activation(out=sh[:], in_=ps[:], func=AF.Identity), r=[psk], w=[shk])
                        else:
                            sf, sfk = sf_r.next()
                            S.op("dve", lambda e: e.tensor_copy(out=sf[:], in_=ps[:]), r=[psk], w=[sfk])
                            S.op("act", lambda e: e.activation(out=sh[:], in_=sf[:], func=AF.Silu), r=[sfk], w=[shk])
                        store(S_grT[c * 128:(c + 1) * 128, t0:t0 + 512], sh[:], shk)
                    if "l" not in KSKIP:
                        ps, psk = nps(); fm_chunk(ps, psk, W1, "W1", 3072, 16, ht, hk)
                        sf, sfk = sf_r.next()
                        S.op("dve", lambda e: e.tensor_copy(out=sf[0:16, :], in_=ps[0:16, :]), r=[psk], w=[sfk])
                        store(S_lrA[:, t0:t0 + 512], sf[0:16, :], sfk)
                if "l" not in KSKIP:
                    ps, psk = nps(); fm_chunk(ps, psk, W1, "W1", 3088, 16, ht, hk)
                    sf, sfk = sf_r.next()
                    S.op("dve", lambda e: e.tensor_copy(out=sf[0:16, :], in_=ps[0:16, :]), r=[psk], w=[sfk])
                    store(S_lrB[:, t0:t0 + 512], sf[0:16, :], sfk)
                for sub in (range(4) if "t" not in KSKIP else ()):
                    r0 = t0 + sub * 128
                    ps, psk = nps(); tm_chunk(ps, psk, W1, "W1", 512, 512, ht, hk, sub)
                    sf, sfk = sf_r.next()
                    S.op("dve", lambda e: e.tensor_copy(out=sf[:], in_=ps[:]), r=[psk], w=[sfk])
                    store(S_gk[r0:r0 + 128, :], sf[:], sfk)
                    for c in range(2):
                        ps, psk = nps("act"); tm_chunk(ps, psk, W1, "W1", 1024 + c * 512, 512, ht, hk, sub)
                        sh, shk = sh_r.next()
                        S.op("act", lambda e: e.activation(out=sh[:], in_=ps[:], func=AF.Identity), r=[psk], w=[shk])
                        store(S_gv[r0:r0 + 128, c * 512:(c + 1) * 512], sh[:], shk)
            S.barrier()
            if os.environ.get('K_STOP') == '3':
                S.finish(); return nc

        with ExitStack() as pp:
            W2 = sb("W2", [128, 8, NW2], BF16, pp)
            load_w(W2, "W2", NW1, NW2)
            h_r = rot("hb", [128, 8, 512], BF16, 2, pp)
            sh_r = rot("sh", [128, 512], BF16, 4, pp)
            posi = sb("posi", [128, 512], I32, pp); ang = sb("ang", [128, 512], F32, pp)
            kf = sb("kf", [128, 512], F32, pp)
            cos_t = sb("cos_t", [128, 512], F32, pp); sin_t = sb("sin_t", [128, 512], F32, pp)
            t1_r = rot("t1", [128, 512], F32, 2, pp); t2_r = rot("t2", [128, 512], F32, 2, pp)
            for tt in range(NT_ALL):
                own = tt < NT_OWN
                t0 = tt * 512
                ht, hk = h_r.next()
                S.dma("sp", lambda e: e.dma_start(out=ht[:], in_=hT_v[:, :, t0:t0 + 512]), w=[hk])
                S.dma("sp", lambda e: e.dma_start(out=posi[:], in_=pos[:, t0:t0 + 512].partition_broadcast(128)), w=["posi"])
                S.op("dve", lambda e: e.tensor_copy(out=ang[:], in_=posi[:]), r=["posi"], w=["ang"])
                S.op("dve", lambda e: e.tensor_scalar(out=ang[:], in0=ang[:], scalar1=ropec[:, 0:1], scalar2=None, op0=ALU.mult), r=["ang", "ropec"], w=["ang"])
                S.op("dve", lambda e: e.tensor_scalar(out=kf[:], in0=ang[:], scalar1=1.0 / TWO_PI, scalar2=None, op0=ALU.mult), r=["ang"], w=["kf"])
                S.op("dve", lambda e: e.tensor_copy(out=posi[:], in_=kf[:]), r=["kf"], w=["posi"])
                S.op("dve", lambda e: e.tensor_copy(out=kf[:], in_=posi[:]), r=["posi"], w=["kf"])
                S.op("dve", lambda e: e.scalar_tensor_tensor(out=ang[:], in0=kf[:], scalar=-TWO_PI, in1=ang[:], op0=ALU.mult, op1=ALU.add), r=["kf", "ang"], w=["ang"])
                S.op("dve", lambda e: e.tensor_scalar(out=ang[:], in0=ang[:], scalar1=math.pi, scalar2=-math.pi, op0=ALU.min, op1=ALU.max), r=["ang"], w=["ang"])
                S.op("act", lambda e: e.activation(out=sin_t[:], in_=ang[:], func=AF.Sin, scale=ropec[:, 1:2]), r=["ang", "ropec"], w=["sin_t"])
                S.op("act", lambda e: e.activation(out=kf[:], in_=ang[:], func=AF.Abs), r=["ang"], w=["kf"])
                S.op("act", lambda e: e.activation(out=cos_t[:], in_=kf[:], func=AF.Sin, bias=math.pi / 2, scale=-1.0), r=["kf"], w=["cos_t"])
                for which in ((0, 1) if own else (1,)):
                    base = which * 2048
                    for h in range(8):
                        ps, psk = nps(); fm_chunk(ps, psk, W2, "W2", base + h * 128, 128, ht, hk)
                        ps2, psk2 = nps(); fm_chunk(ps2, psk2, W2, "W2", base + 1024 + h * 128, 128, ht, hk)
                        t1, t1k = t1_r.next(); t2, t2k = t2_r.next()
                        S.op("dve", lambda e: e.tensor_tensor(out=t1[:], in0=ps[:], in1=cos_t[:], op=ALU.mult), r=[psk, "cos_t"], w=[t1k])
                        S.op("dve", lambda e: e.tensor_tensor(out=t2[:], in0=ps2[:], in1=sin_t[:], op=ALU.mult), r=[psk2, "sin_t"], w=[t2k])
                        sh, shk = sh_r.next()
                        S.op("pool", lambda e: e.tensor_tensor(out=sh[:], in0=t1[:], in1=t2[:], op=ALU.add), r=[t1k, t2k], w=[shk])
                        dst = (S_qT if which == 0 else S_kT)[h * 128:(h + 1) * 128, t0:t0 + 512]
                        store(dst, sh[:], shk)
                for sub in range(4):
                    r0 = t0 + sub * 128
                    for c in range(2):
                        ps, psk = nps("act"); tm_chunk(ps, psk, W2, "W2", 4096 + c * 512, 512, ht, hk, sub)
                        sh, shk = sh_r.next()
                        S.op("act", lambda e: e.activation(out=sh[:], in_=ps[:], func=AF.Identity), r=[psk], w=[shk])
                        store(S_dv[r0:r0 + 128, c * 512:(c + 1) * 512], sh[:], shk)
            S.barrier()
            if os.environ.get('K_STOP') == '4':
                S.finish(); return nc

        with ExitStack() as pp:
            W3 = sb("W3", [128, 8, NW3], BF16, pp)
            sf3_r = rot("sf3", [128, 512], F32, 4, pp)
            load_w(W3, "W3", NW1 + NW2, NW3)
            h_r = rot("hb", [128, 8, 512], BF16, 2, pp)
            sh_r = rot("sh", [128, 512], BF16, 4, pp)
            for tt in range(NT_OWN):
                t0 = tt * 512
                ht, hk = h_r.next()
                S.dma("sp", lambda e: e.dma_start(out=ht[:], in_=hT_v[:, :, t0:t0 + 512]), w=[hk])
                for c in range(16):
                    ps, psk = nps(); fm_chunk(ps, psk, W3, "W3", c * 128, 128, ht, hk)
                    sh, shk = sh_r.next()
                    sf, sfk = sf3_r.next()
                    S.op("dve", lambda e: e.tensor_copy(out=sf[:], in_=ps[:]), r=[psk], w=[sfk])
                    S.op("act", lambda e: e.activation(out=sh[:], in_=sf[:], func=AF.Sigmoid), r=[sfk], w=[shk])
                    dst = (S_saT if c < 8 else S_sbT)[(c % 8) * 128:(c % 8 + 1) * 128, t0:t0 + 512]
                    store(dst, sh[:], shk)
            S.barrier()
            if os.environ.get('K_STOP') == '5':
                S.finish(); return nc

        DKS = 128 ** -0.5
        oT_v = S_oT.rearrange("(c p) t -> p c t", p=128)
        gqT_v = S_gqT.rearrange("(h p) t -> p h t", p=128)
        gkT_v = S_gkT.rearrange("(h p) t -> p h t", p=128)
        with ExitStack() as pc:
            S32 = sb("S32", [128, 4, 256], F32, pc); Sbf = sb("Sbf", [128, 4, 256], BF16, pc)
            wa_t = sb("wa_t", [16, 512], F32, pc); ba_t = sb("ba_t", [1, 512], F32, pc)
            lr_r = rot("lr", [16, 128], F32, 2, pc)
            gk_r = rot("gkt", [128, 512], F32, 2, pc); gv_r = rot("gvt", [128, 1024], BF16, 2, pc)
            gq_r = rot("gqT", [128, 4, 128], F32, 2, pc); gkT_r = rot("gkT", [128, 4, 128], F32, 2, pc)
            sp_r = rot("spt", [128, 512], F32, 2, pc)
            eb_r = rot("expb", [128, 512], F32, 2, pc); enb_r = rot("expnb", [128, 512], F32, 2, pc)
            qe_r = rot("qe", [128, 4, 128], BF16, 2, pc); ke_r = rot("ke", [128, 4, 128], BF16, 2, pc)
            ee_r = rot("eend", [128, 512], F32, 2, pc); kend_r = rot("kend", [128, 512], BF16, 2, pc)
            attm_r = rot("attm", [128, 4, 128], BF16, 2, pc)
            osb_r = rot("osb", [128, 8, 128], F32, 2, pc); oa_r = rot("oa", [128, 8, 128], F32, 2, pc)
            for di in range(2):
                tri = triA if di == 0 else triB
                stri = striA if di == 0 else striB
                S.dma("sp", lambda e: e.dma_start(out=wa_t[:], in_=(wa_A if di == 0 else wa_B)), w=["wa_t"])
                S.dma("sp", lambda e: e.dma_start(out=ba_t[:], in_=(ba_A if di == 0 else ba_B)), w=["ba_t"])
                S.op("dve", lambda e: e.memset(S32[:], 0.0), w=["S32"])
                S.op("pool", lambda e: e.memset(Sbf[:], 0.0), w=["Sbf"])
                S_lr = S_lrA if di == 0 else S_lrB
                tiles = list(range(T_OWN // 128)) if di == 0 else list(range(T_ALL // 128 - 1, -1, -1))
                order = (0, 1) if di == 0 else (1, 0)
                for ti in tiles:
                    t0 = ti * 128
                    wout = ti < T_OWN // 128
                    lr, lrk = lr_r.next(); gk, gkk = gk_r.next(); gv, gvk = gv_r.next()
                    S.dma("sp", lambda e: e.dma_start(out=lr[:], in_=S_lr[:, t0:t0 + 128]), w=[lrk])
                    S.dma("act", lambda e: e.dma_start(out=gk[:], in_=S_gk[t0:t0 + 128, :]), w=[gkk])
                    S.dma("sp", lambda e: e.dma_start(out=gv[:], in_=S_gv[t0:t0 + 128, :]), w=[gvk])
                    if wout:
                        gq, gqk = gq_r.next(); gkT, gkTk = gkT_r.next()
                        S.dma("act", lambda e: e.dma_start(out=gq[:], in_=gqT_v[:, :, t0:t0 + 128]), w=[gqk])
                        S.dma("sp", lambda e: e.dma_start(out=gkT[:], in_=gkT_v[:, :, t0:t0 + 128]), w=[gkTk])
                    S.op("pe", lambda e: e.matmul(pb[0][:, :], lhsT=lr[0:16, :], rhs=wa_t[0:16, :], start=True, stop=False), r=[lrk, "wa_t"], w=["pb0"])
                    S.op("pe", lambda e: e.matmul(pb[0][:, :], lhsT=ones_f[0:1, 0:128], rhs=ba_t[0:1, :], start=False, stop=True), r=["ones_f", "ba_t"], w=["pb0"])
                    sp, spk = sp_r.next()
                    S.op("act", lambda e: e.activation(out=sp[:], in_=pb[0][:, :], func=AF.Exp, scale=-1.0), r=["pb0"], w=[spk])
                    S.op("act", lambda e: e.activation(out=sp[:], in_=sp[:], func=AF.Ln, bias=1.0, scale=1.0), r=[spk], w=[spk])
                    for h in range(4):
                        S.op("pe", lambda e: e.matmul(pb[1][:, h * 128:(h + 1) * 128], lhsT=sp[:, h * 128:(h + 1) * 128], rhs=tri, start=True, stop=True), r=[spk, "tri4"], w=["pb1"])
                    eb, ebk = eb_r.next()
                    S.op("act", lambda e: e.activation(out=eb[:], in_=pb[1][:, :], func=AF.Exp, scale=-1.0 / 16), r=["pb1"], w=[ebk])
                    if wout:
                        enb, enbk = enb_r.next()
                        S.op("act", lambda e: e.activation(out=enb[:], in_=pb[1][:, :], func=AF.Exp, scale=1.0 / 16), r=["pb1"], w=[enbk])
                        qe, qek = qe_r.next(); ke, kek = ke_r.next()
                        S.op("dve", lambda e: e.scalar_tensor_tensor(out=qe[:], in0=gq[:], scalar=DKS, in1=eb[:].rearrange("p (h t) -> p h t", h=4), op0=ALU.mult, op1=ALU.mult), r=[gqk, ebk], w=[qek])
                        S.op("pool", lambda e: e.tensor_tensor(out=ke[:], in0=gkT[:], in1=enb[:].rearrange("p (h t) -> p h t", h=4), op=ALU.mult), r=[gkTk, enbk], w=[kek])
                    S.op("pe", lambda e: e.matmul(pb[0][:, :], lhsT=stri, rhs=sp[:], start=True, stop=True), r=[spk, "tri4"], w=["pb0"])
                    ee, eek = ee_r.next(); kend, kendk = kend_r.next()
                    S.op("act", lambda e: e.activation(out=ee[:], in_=pb[0][:, :], func=AF.Exp, scale=-1.0 / 16), r=["pb0"], w=[eek])
                    S.op("dve", lambda e: e.tensor_tensor(out=kend[:], in0=gk[:], in1=ee[:], op=ALU.mult), r=[gkk, eek], w=[kendk])
                    if wout:
                        for h in range(4):
                            S.op("pe", lambda e: e.matmul(pb[3][:, h * 128:(h + 1) * 128], lhsT=ke[:, h, :], rhs=qe[:, h, :], start=True, stop=True), r=[kek, qek], w=["pb3"])
                        attm, attmk = attm_r.next()
                        S.op("dve", lambda e: e.tensor_tensor(out=attm[:], in0=pb[3][:, :].rearrange("p (h t) -> p h t", h=4), in1=tri.unsqueeze(1).to_broadcast([128, 4, 128]), op=ALU.mult), r=["pb3", "tri4"], w=[attmk])
                        for ch in range(8):
                            h, dc = ch // 2, ch % 2
                            ob = pb[4 + ch // 4]; obk = PBK[4 + ch // 4]
                            S.op("pe", lambda e: e.matmul(ob[:, (ch % 4) * 128:(ch % 4 + 1) * 128], lhsT=gv[:, h * 256 + dc * 128:h * 256 + (dc + 1) * 128], rhs=attm[:, h, :], start=(ch % 4 == 0), stop=False), r=[gvk, attmk], w=[obk])
                    for ci, chunk in enumerate(order):
                        r0 = chunk * 64
                        if wout:
                            for ch in range(8):
                                h, dc = ch // 2, ch % 2
                                ob = pb[4 + ch // 4]; obk = PBK[4 + ch // 4]
                                c0 = (ch % 4) * 128 + r0
                                S.op("pe", lambda e: e.matmul(ob[:, c0:c0 + 64], lhsT=Sbf[:, h, dc * 128:(dc + 1) * 128], rhs=qe[:, h, r0:r0 + 64], start=False, stop=(ci == 1 and ch % 4 == 3)), r=["Sbf", qek], w=[obk])
                        for h in range(4):
                            ub = pb[2] if h < 2 else pb[6]
                            ubk = "pb2" if h < 2 else "pb6"
                            S.op("pe", lambda e: e.matmul(ub[:, (h % 2) * 256:(h % 2 + 1) * 256], lhsT=kend[r0:r0 + 64, h * 128:(h + 1) * 128], rhs=gv[r0:r0 + 64, h * 256:(h + 1) * 256], start=True, stop=True), r=[kendk, gvk], w=[ubk])
                        dcol = (r0 + 63) if di == 0 else r0
                        for h in range(4):
                            ub = pb[2] if h < 2 else pb[6]
                            ubk = "pb2" if h < 2 else "pb6"
                            S.op("dve", lambda e: e.scalar_tensor_tensor(out=S32[:, h, :], in0=S32[:, h, :], scalar=eb[:, h * 128 + dcol:h * 128 + dcol + 1], in1=ub[:, (h % 2) * 256:(h % 2 + 1) * 256], op0=ALU.mult, op1=ALU.add), r=["S32", ebk, ubk], w=["S32"])
                        S.op("act", lambda e: e.activation(out=Sbf[:], in_=S32[:], func=AF.Identity), r=["S32"], w=["Sbf"])
                    if wout:
                        osb, osbk = osb_r.next()
                        if di == 0:
                            for half in range(2):
                                S.op("dve", lambda e: e.tensor_copy(out=osb[:, half * 4:(half + 1) * 4, :], in_=pb[4 + half][:, :].rearrange("p (c t) -> p c t", c=4)), r=[PBK[4 + half]], w=[osbk])
                        else:
                            oa, oak = oa_r.next()
                            S.dma("sp", lambda e: e.dma_start(out=oa[:], in_=oT_v[:, :, t0:t0 + 128]), w=[oak])
                            for half in range(2):
                                S.op("dve", lambda e: e.tensor_tensor(out=osb[:, half * 4:(half + 1) * 4, :], in0=pb[4 + half][:, :].rearrange("p (c t) -> p c t", c=4), in1=oa[:, half * 4:(half + 1) * 4, :], op=ALU.add), r=[PBK[4 + half], oak], w=[osbk])
                        S.dma("act", lambda e: e.dma_start(out=oT_v[:, :, t0:t0 + 128], in_=osb[:]), r=[osbk])
                S.barrier()
                if os.environ.get('K_STOP') == '6':
                    S.finish(); return nc

        NKB = T_ALL // 128
        NQG = T_OWN // 512
        dv_v = S_dv.rearrange("(kb p) f -> p kb f", p=128)
        with ExitStack() as pd:
            kT_r = rot("kT", [128, T_ALL], BF16, 2, pd); qT_r = rot("qT", [128, T_OWN], BF16, 2, pd)
            va_r = rot("vaug", [128, NKB, 129], BF16, 2, pd)
            pT_r = rot("pT", [128, 512], BF16, 4, pd)
            rz = sb("rz", [128, 4], F32, pd)
            osb_r = rot("dosb", [128, 8 * 129], F32, 2, pd)
            t1_r = rot("dt1", [128, 128], F32, 2, pd)
            od_r = rot("od", [128, 4, 128], BF16, 2, pd); junk = sb("djunk", [128, 128], F32, pd)
            odT_r = rot("odT", [128, 512], BF16, 2, pd)
            for vt in va_r.tiles:
                S.op("pool", lambda e: e.memset(vt[:, :, 128:129], 1.0), w=["vaug#0", "vaug#1"])
            sps = [(pb[i], PBK[i]) for i in range(3)]

            def oreg(r):
                b = 3 + r // 3
                return pb[b][:, (r % 3) * 129:(r % 3) * 129 + 129], PBK[b]

            heads = {}

            def load_head(h):
                if h in heads or h >= 8:
                    return
                kT, kTk = kT_r.next(); qT, qTk = qT_r.next(); va, vak = va_r.next()
                S.dma("sp", lambda e: e.dma_start(out=kT[:], in_=S_kT[h * 128:(h + 1) * 128, :]), w=[kTk])
                S.dma("sp", lambda e: e.dma_start(out=qT[:], in_=S_qT[h * 128:(h + 1) * 128, :]), w=[qTk])
                S.dma("sp", lambda e: e.dma_start(out=va[:, :, 0:128], in_=dv_v[:, :, h * 128:(h + 1) * 128]), w=[vak])
                heads[h] = (kT, kTk, qT, qTk, va, vak)

            o4_r = rot("do4", [128, 4, 128], F32, 2, pd)
            ssq = sb("dssq", [128, 4], F32, pd)

            def epilogue(h, qg):
                q0 = qg * 512
                osb, osbk = osb_r.next()
                for b in range(3):
                    ncol = 387 if b < 2 else 258
                    S.op("dve", lambda e: e.tensor_copy(out=osb[:, b * 387:b * 387 + ncol], in_=pb[3 + b][:, 0:ncol]), r=[PBK[3 + b]], w=[osbk])
                odT, odTk = odT_r.next()
                o4, o4k = o4_r.next()
                S.op("dve", lambda e: e.memset(ssq[:], 0.0), w=["dssq"])
                for qi in range(4):
                    o0 = osb[:, qi * 129:(qi + 1) * 129]; o1 = osb[:, (4 + qi) * 129:(5 + qi) * 129]
                    S.op("dve", lambda e: e.reciprocal(out=rz[:, 0:1], in_=o0[:, 128:129]), r=[osbk], w=["rz"])
                    S.op("dve", lambda e: e.reciprocal(out=rz[:, 1:2], in_=o1[:, 128:129]), r=[osbk], w=["rz"])
                    S.op("dve", lambda e: e.tensor_tensor(out=rz[:, 2:3], in0=rz[:, 1:2], in1=lam[:], op=ALU.mult), r=["rz", "lam"], w=["rz"])
                    t1, t1k = t1_r.next()
                    S.op("dve", lambda e: e.tensor_scalar(out=t1[:], in0=o1[:, 0:128], scalar1=rz[:, 2:3], scalar2=None, op0=ALU.mult), r=[osbk, "rz"], w=[t1k])
                    S.op("dve", lambda e: e.scalar_tensor_tensor(out=o4[:, qi, :], in0=o0[:, 0:128], scalar=rz[:, 0:1], in1=t1[:], op0=ALU.mult, op1=ALU.subtract), r=[osbk, "rz", t1k], w=[o4k])
                    S.op("dve", lambda e: e.scalar_tensor_tensor(out=junk[:], in0=o4[:, qi, :], scalar=1.0, in1=o4[:, qi, :], op0=ALU.mult, op1=ALU.mult, accum_out=ssq[:, qi:qi + 1]), r=[o4k], w=["djunk", "dssq"])
                S.op("dve", lambda e: e.tensor_scalar(out=ssq[:], in0=ssq[:], scalar1=1.0 / 128, scalar2=EPS, op0=ALU.mult, op1=ALU.add), r=["dssq"], w=["dssq"])
                S.op("act", lambda e: e.activation(out=ssq[:], in_=ssq[:], func=AF.Sqrt), r=["dssq"], w=["dssq"])
                S.op("dve", lambda e: e.reciprocal(out=ssq[:], in_=ssq[:]), r=["dssq"], w=["dssq"])
                od, odk = od_r.next()
                for qi in range(4):
                    S.op("dve", lambda e: e.scalar_tensor_tensor(out=od[:, qi, :], in0=o4[:, qi, :], scalar=ssq[:, qi:qi + 1], in1=dngr[:, h * 128:(h + 1) * 128], op0=ALU.mult, op1=ALU.mult), r=[o4k, "dssq", "dngr"], w=[odk])

                def part_b():
                    for qi in range(4):
                        S.op("pe", lambda e: e.transpose(ptb[:, qi * 128:(qi + 1) * 128], od[:, qi, :], ident_bf[:]), r=[odk, "ident_bf"], w=["ptb"])
                    S.op("dve", lambda e: e.tensor_copy(out=odT[:], in_=ptb[:, 0:512]), r=["ptb"], w=[odTk])
                    S.dma("sp", lambda e: e.dma_start(out=S_odT[h * 128:(h + 1) * 128, q0:q0 + 512], in_=odT[:]), r=[odTk])
                return part_b

            steps = [(h, qg, kb, m) for h in range(8) for qg in range(NQG) for kb in range(NKB) for m in range(2)]
            LA = 2
            info = {}

            def emit_qk(si):
                h, qg, kb, m = steps[si]
                if h not in heads:
                    load_head(h)
                kT, kTk, qT, qTk, va, vak = heads[h]
                sp_, spk_ = sps[si % 3]
                q0 = qg * 512
                S.op("pe", lambda e: e.matmul(sp_[:, :], lhsT=kT[m * 64:(m + 1) * 64, kb * 128:(kb + 1) * 128], rhs=qT[m * 64:(m + 1) * 64, q0:q0 + 512], start=True, stop=True), r=[kTk, qTk], w=[spk_])
                pT, pTk = pT_r.next()
                S.op("act", lambda e: e.activation(out=pT[:], in_=sp_[:, :], func=AF.Exp, scale=0.125), r=[spk_], w=[pTk])
                info[si] = (pT, pTk)

            def emit_av(si):
                h, qg, kb, m = steps[si]
                if qg == 0 and kb == 0 and m == 0:
                    load_head(h + 1)
                kT, kTk, qT, qTk, va, vak = heads[h]
                pT, pTk = info.pop(si)
                for qi in range(4):
                    r = m * 4 + qi
                    oap, obk = oreg(r)
                    S.op("pe", lambda e: e.matmul(oap, lhsT=pT[:, qi * 128:(qi + 1) * 128], rhs=va[:, kb, :], start=(kb == 0 and r % 3 == 0), stop=(kb == NKB - 1 and (r % 3 == 2 or r == 7))), r=[pTk, vak], w=[obk])
                if kb == NKB - 1 and m == 1:
                    deferred.append((si + 24, epilogue(h, qg)))

            deferred = []
            for si in range(len(steps) + LA):
                if si < len(steps):
                    emit_qk(si)
                if si - LA >= 0:
                    emit_av(si - LA)
                while deferred and deferred[0][0] <= si:
                    deferred.pop(0)[1]()
            for _, fn in deferred:
                fn()
            S.barrier()

        NTE = T_OWN // 256
        cv = lambda ap: ap.rearrange("(c p) t -> p c t", p=128)
        grT_v = cv(S_grT); odT_v = cv(S_odT); saT_v = cv(S_saT); sbT_v = cv(S_sbT)
        with ExitStack() as pe_:
            Wgp = sb("Wgp", [128, 8, 1024], BF16, pe_); Wdp = sb("Wdp", [128, 8, 1024], BF16, pe_); Wo = sb("Wo", [128, 8, 1024], BF16, pe_)
            load_w(Wgp, "Wgp", 0, 1024, w_gp); load_w(Wdp, "Wdp", 0, 1024, w_dp); load_w(Wo, "Wo", 0, 1024, w_out)
            o_r = rot("eo", [128, 8, 256], F32, 2, pe_); gr_r = rot("egr", [128, 8, 256], BF16, 2, pe_)
            sq_r = rot("esq", [128, 8, 256], BF16, 1, pe_); rsd = sb("ersd", [128, 4, 256], F32, pe_)
            og_r = rot("eog", [128, 8, 256], BF16, 1, pe_); od_r2 = rot("eod", [128, 8, 256], BF16, 2, pe_)
            sa_r = rot("esa", [128, 8, 256], BF16, 2, pe_); sb_r = rot("esb", [128, 8, 256], BF16, 2, pe_)
            mg_r = rot("emg", [128, 8, 256], BF16, 1, pe_); m1_r = rot("em1", [128, 256], F32, 2, pe_); m2_r = rot("em2", [128, 256], F32, 2, pe_)
            x_r2 = rot("ex", [128, 8, 256], F32, 2, pe_); x1tm_r = rot("ex1tm", [128, 1024], F32, 2, pe_)
            for te in range(NTE):
                t0 = te * 256
                o_, ok_ = o_r.next(); gr, grk = gr_r.next(); odt, odtk = od_r2.next(); sa, sak = sa_r.next(); sbb, sbk = sb_r.next(); xt, xk = x_r2.next()
                S.dma("sp", lambda e: e.dma_start(out=o_[:], in_=oT_v[:, :, t0:t0 + 256]), w=[ok_])
                S.dma("act", lambda e: e.dma_start(out=gr[:], in_=grT_v[:, :, t0:t0 + 256]), w=[grk])
                S.dma("sp", lambda e: e.dma_start(out=odt[:], in_=odT_v[:, :, t0:t0 + 256]), w=[odtk])
                S.dma("act", lambda e: e.dma_start(out=sa[:], in_=saT_v[:, :, t0:t0 + 256]), w=[sak])
                S.dma("sp", lambda e: e.dma_start(out=sbb[:], in_=sbT_v[:, :, t0:t0 + 256]), w=[sbk])
                S.dma("act", lambda e: e.dma_start(out=xt[:], in_=xT_v[:, :, t0:t0 + 256]), w=[xk])
                sq, sqk = sq_r.next()
                S.op("act", lambda e: e.activation(out=sq[:], in_=o_[:], func=AF.Square), r=[ok_], w=[sqk])
                for h in range(4):
                    bk = pb[4 + h // 2]; bkk = PBK[4 + h // 2]
                    for dc in range(2):
                        S.op("pe", lambda e: e.matmul(bk[:, (h % 2) * 256:(h % 2 + 1) * 256], lhsT=ones_bf[:], rhs=sq[:, h * 2 + dc, :], start=(dc == 0), stop=(dc == 1)), r=["ones_bf", sqk], w=[bkk])
                for half in range(2):
                    S.op("act", lambda e: e.activation(out=rsd[:, half * 2:(half + 1) * 2, :], in_=pb[4 + half][:, :].rearrange("p (h t) -> p h t", h=2), func=AF.Sqrt, bias=EPS, scale=1.0 / 256), r=[PBK[4 + half]], w=["ersd"])
                S.op("dve", lambda e: e.reciprocal(out=rsd[:], in_=rsd[:]), r=["ersd"], w=["ersd"])
                for ch in range(8):
                    S.op("dve", lambda e: e.scalar_tensor_tensor(out=o_[:, ch, :], in0=o_[:, ch, :], scalar=gvec[:, 24 + ch:25 + ch], in1=rsd[:, ch // 2, :], op0=ALU.mult, op1=ALU.mult), r=[ok_, "gvec", "ersd"], w=[ok_])
                og, ogk = og_r.next()
                S.op("pool", lambda e: e.tensor_tensor(out=og[:], in0=o_[:], in1=gr[:], op=ALU.mult), r=[ok_, grk], w=[ogk])
                mg, mgk = mg_r.next()
                for oc in range(8):
                    ps, psk = nps()
                    for kc in range(8):
                        S.op("pe", lambda e: e.matmul(ps[:, 0:256], lhsT=Wgp[:, kc, oc * 128:(oc + 1) * 128], rhs=og[:, kc, :], start=(kc == 0), stop=False), r=["Wgp", ogk], w=[psk])
                    for kc in range(8):
                        S.op("pe", lambda e: e.matmul(ps[:, 256:512], lhsT=Wdp[:, kc, oc * 128:(oc + 1) * 128], rhs=odt[:, kc, :], start=False, stop=(kc == 7)), r=["Wdp", odtk], w=[psk])
                    m1, m1k = m1_r.next(); m2, m2k = m2_r.next()
                    S.op("dve", lambda e: e.tensor_tensor(out=m1[:], in0=ps[:, 0:256], in1=sa[:, oc, :], op=ALU.mult), r=[psk, sak], w=[m1k])
                    S.op("dve", lambda e: e.tensor_tensor(out=m2[:], in0=ps[:, 256:512], in1=sbb[:, oc, :], op=ALU.mult), r=[psk, sbk], w=[m2k])
                    S.op("pool", lambda e: e.tensor_tensor(out=mg[:, oc, :], in0=m1[:], in1=m2[:], op=ALU.add), r=[m1k, m2k], w=[mgk])
                for oc in range(8):
                    ps, psk = nps()
                    for kc in range(8):
                        S.op("pe", lambda e: e.matmul(ps[:, 0:256], lhsT=Wo[:, kc, oc * 128:(oc + 1) * 128], rhs=mg[:, kc, :], start=(kc == 0), stop=(kc == 7)), r=["Wo", mgk], w=[psk])
                    S.op("dve", lambda e: e.scalar_tensor_tensor(out=xt[:, oc, :], in0=ps[:, 0:256], scalar=gt1(oc), in1=xt[:, oc, :], op0=ALU.mult, op1=ALU.add), r=[psk, "modT", xk], w=[xk])
                for sub in range(2):
                    x1tm, x1tmk = x1tm_r.next()
                    for half in range(2):
                        ps, psk = nps("act")
                        for k4 in range(4):
                            kc = half * 4 + k4
                            S.op("pe", lambda e: e.matmul(ps[:, k4 * 128:(k4 + 1) * 128], lhsT=xt[:, kc, sub * 128:(sub + 1) * 128], rhs=ident[:], start=True, stop=True), r=[xk, "ident"], w=[psk])
                        S.op("act", lambda e: e.activation(out=x1tm[:, half * 512:(half + 1) * 512], in_=ps[:, :], func=AF.Identity), r=[psk], w=[x1tmk])
                    store(S_x1[t0 + sub * 128:t0 + (sub + 1) * 128, :], x1tm[:], x1tmk)
            S.barrier()
            if os.environ.get('K_STOP') == '8':
                S.finish(); return nc

        NTF = T_OWN // 128
        with ExitStack() as pf:
            Wq = sb("Wq", [128, 8, 2048], BF16, pf); skb = sb("skb", [128, 2048], BF16, pf)
            load_w(Wq, "Wq", 0, 2048, w_pq)
            load_w(skb, "skb", 0, 2048, skT, kcs=1)
            rowsb = sb("rowsb", [128, 5120], F32, pf)
            S.dma("sp", lambda e: e.dma_start(out=rowsb[:], in_=S_rows.partition_broadcast(128)), w=["rowsb"])
            a2row, sh2row, gt2row, afrow, fshrow = (rowsb[:, i * 1024:(i + 1) * 1024] for i in range(5))
            pci = sb("pci", [128, 384], I32, pf); pcs = sb("pcs", [128, 8], I32, pf); pcf = sb("pcf", [128, 64], F32, pf)
            S.dma("sp", lambda e: e.dma_start(out=pci[:, 0:128], in_=pconst_d[:, 0:128]), w=["pci"])
            S.dma("sp", lambda e: e.dma_start(out=pci[:, 128:384], in_=pconst_d[:, 2048:2304]), w=["pci"])
            S.dma("sp", lambda e: e.dma_start(out=pcs[:], in_=pcs_d), w=["pcs"])
            S.dma("sp", lambda e: e.dma_start(out=pcf[:], in_=pconstf_d), w=["pcf"])
            niota = pci[:, 0:128].unsqueeze(1).to_broadcast([128, 16, 128]); ciota = pci[:, 128:384].unsqueeze(1).to_broadcast([128, 8, 256])
            cst = lambda j, n: pcs[:, j:j + 1].to_broadcast([128, n])
            x1_r = rot("fx1", [128, 1024], F32, 2, pf)
            h2 = sb("fh2", [128, 1024], F32, pf); h2b = sb("fh2b", [128, 1024], BF16, pf); h2T = sb("fh2T", [128, 8, 128], BF16, pf)
            qpT = sb("fqpT", [128, 16, 128], BF16, pf)
            sc = sb("fsc", [128, 2048], F32, pf); sc2 = sb("fsc2", [128, 2048], F32, pf)
            cand = sb("fcand", [128, 2048], F32, pf); tmpb = sb("ftmp", [128, 2048], F32, pf)
            v16 = sb("fv16", [128, 256], F32, pf); i16i = sb("fi16i", [128, 256], I32, pf); i16f = sb("fi16f", [128, 256], F32, pf)
            f16 = sb("ff16", [128, 128], F32, pf); cpi = sb("fcpi", [128, 128], I32, pf); cpj = sb("fcpj", [128, 128], I32, pf)
            aif = sb("faif", [128, 128], F32, pf); bjf = sb("fbjf", [128, 128], F32, pf)
            iA = sb("fiA", [128, 128], F32, pf); iB = sb("fiB", [128, 128], F32, pf)
            eif = sb("feif", [128, 128], F32, pf); eii = sb("feii", [128, 128], I32, pf)
            gm = sb("fgm", [128, 8], F32, pf); ge = sb("fge", [128, 128], F32, pf); gs = sb("fgs", [128, 8], F32, pf)
            gw = sb("fgw", [128, 128], F32, pf); aa = sb("faa", [128, 128], F32, pf); ww = sb("fww", [128, 128], F32, pf)
            st4 = sb("fst4", [128, 4], F32, pf)
            g_r = rot("fgat", [128, 1024], F32, 8, pf)
            junk = sb("fjunk", [128, 1024], F32, pf); acc = sb("facc", [128, 1024], F32, pf)
            x2 = sb("fx2", [128, 1024], F32, pf); ot_r = rot("fout", [128, 1024], F32, 2, pf)
            v16v = v16[:].rearrange("p (h q k) -> p h q k", h=8, q=2)
            i16v = i16f[:].rearrange("p (h q k) -> p h q k", h=8, q=2)
            B4 = [128, 8, 16, 16]
            r4 = lambda t: t[:].rearrange("p (h a b) -> p h a b", h=8, a=16)
            r3 = lambda t: t[:].rearrange("p (h k) -> p h k", h=8)
            for tf in range(NTF):
                t0 = tf * 128
                x1, x1k = x1_r.next()
                S.dma("sp", lambda e: e.dma_start(out=x1[:], in_=S_x1[t0:t0 + 128, :]), w=[x1k])
                S.op("act", lambda e: e.activation(out=junk[:], in_=x1[:], func=AF.Square, accum_out=st4[:, 0:1]), r=[x1k], w=["fjunk", "fst4"])
                S.op("act", lambda e: e.activation(out=st4[:, 1:2], in_=st4[:, 0:1], func=AF.Sqrt, bias=EPS, scale=1.0 / D), r=["fst4"], w=["fst4"])
                S.op("dve", lambda e: e.reciprocal(out=st4[:, 1:2], in_=st4[:, 1:2]), r=["fst4"], w=["fst4"])
                S.op("dve", lambda e: e.scalar_tensor_tensor(out=h2[:], in0=x1[:], scalar=st4[:, 1:2], in1=a2row, op0=ALU.mult, op1=ALU.mult), r=[x1k, "fst4", "rowsb"], w=["fh2"])
                S.op("dve", lambda e: e.tensor_tensor(out=h2[:], in0=h2[:], in1=sh2row, op=ALU.add), r=["fh2", "rowsb"], w=["fh2"])
                S.op("act", lambda e: e.activation(out=h2b[:], in_=h2[:], func=AF.Identity), r=["fh2"], w=["fh2b"])
                for kc in range(8):
                    S.op("pe", lambda e: e.transpose(ptb[:, kc * 128:(kc + 1) * 128], h2b[:, kc * 128:(kc + 1) * 128], ident_bf[:]), r=["fh2b", "ident_bf"], w=["ptb"])
                S.op("dve", lambda e: e.tensor_copy(out=h2T[:].rearrange("p k t -> p (k t)"), in_=ptb[:, :]), r=["ptb"], w=["fh2T"])
                for g4 in range(4):
                    ps, psk = nps("act")
                    for q in range(4):
                        hp = g4 * 4 + q
                        for kc in range(8):
                            S.op("pe", lambda e: e.matmul(ps[:, q * 128:(q + 1) * 128], lhsT=Wq[:, kc, hp * 128:(hp + 1) * 128], rhs=h2T[:, kc, :], start=(kc == 0), stop=(kc == 7)), r=["Wq", "fh2T"], w=[psk])
                    S.op("act", lambda e: e.activation(out=qpT[:, g4 * 4:(g4 + 1) * 4, :], in_=ps[:, :].rearrange("p (q t) -> p q t", q=4), func=AF.Identity), r=[psk], w=["fqpT"])
                for g4 in range(4):
                    ps, psk = nps()
                    for q in range(4):
                        hp = g4 * 4 + q
                        S.op("pe", lambda e: e.matmul(ps[:, q * 128:(q + 1) * 128], lhsT=qpT[:, hp, :], rhs=skb[:, hp * 128:(hp + 1) * 128], start=True, stop=True), r=["fqpT", "skb"], w=[psk])
                    S.op("dve", lambda e: e.tensor_copy(out=sc[:, g4 * 512:(g4 + 1) * 512], in_=ps[:, :]), r=[psk], w=["fsc"])
                sci = sc[:].bitcast(I32)
                S.op("dve", lambda e: e.tensor_tensor(out=sci, in0=sci, in1=cst(0, 2048), op=ALU.bitwise_and), r=["fsc", "pcs"], w=["fsc"])
                S.op("dve", lambda e: e.tensor_tensor(out=sci.rearrange("p (g n) -> p g n", g=16), in0=sci.rearrange("p (g n) -> p g n", g=16), in1=niota, op=ALU.bitwise_or), r=["fsc", "pci"], w=["fsc"])
                for hp in range(16):
                    sl = slice(hp * 128, (hp + 1) * 128)
                    S.op("dve", lambda e: e.max(out=v16[:, hp * 16:hp * 16 + 8], in_=sc[:, sl]), r=["fsc"], w=["fv16"])
                    S.op("dve", lambda e: e.match_replace(out=sc2[:, sl], in_to_replace=v16[:, hp * 16:hp * 16 + 8], in_values=sc[:, sl], imm_value=-1e30), r=["fsc", "fv16"], w=["fsc2"])
                    S.op("dve", lambda e: e.max(out=v16[:, hp * 16 + 8:hp * 16 + 16], in_=sc2[:, sl]), r=["fsc2"], w=["fv16"])
                S.op("dve", lambda e: e.tensor_tensor(out=i16i[:], in0=v16[:].bitcast(I32), in1=cst(2, 256), op=ALU.bitwise_and), r=["fv16", "pcs"], w=["fi16i"])
                S.op("dve", lambda e: e.tensor_copy(out=i16f[:], in_=i16i[:]), r=["fi16i"], w=["fi16f"])
                S.op("dve", lambda e: e.tensor_tensor(out=r4(cand), in0=v16v[:, :, 0, :].unsqueeze(3).to_broadcast(B4), in1=v16v[:, :, 1, :].unsqueeze(2).to_broadcast(B4), op=ALU.add), r=["fv16"], w=["fcand"])
                cai = cand[:].bitcast(I32)
                S.op("dve", lambda e: e.tensor_tensor(out=cai, in0=cai, in1=cst(1, 2048), op=ALU.bitwise_and), r=["fcand", "pcs"], w=["fcand"])
                S.op("dve", lambda e: e.tensor_tensor(out=cai.rearrange("p (g n) -> p g n", g=8), in0=cai.rearrange("p (g n) -> p g n", g=8), in1=ciota, op=ALU.bitwise_or), r=["fcand", "pci"], w=["fcand"])
                for h in range(8):
                    sl = slice(h * 256, (h + 1) * 256)
                    S.op("dve", lambda e: e.max(out=f16[:, h * 16:h * 16 + 8], in_=cand[:, sl]), r=["fcand"], w=["ff16"])
                    S.op("dve", lambda e: e.match_replace(out=tmpb[:, sl], in_to_replace=f16[:, h * 16:h * 16 + 8], in_values=cand[:, sl], imm_value=-1e30), r=["fcand", "ff16"], w=["ftmp"])
                    S.op("dve", lambda e: e.max(out=f16[:, h * 16 + 8:h * 16 + 16], in_=tmpb[:, sl]), r=["ftmp"], w=["ff16"])
                S.op("dve", lambda e: e.tensor_tensor(out=cpi[:], in0=f16[:].bitcast(I32), in1=cst(3, 128), op=ALU.bitwise_and), r=["ff16", "pcs"], w=["fcpi"])
                S.op("dve", lambda e: e.tensor_tensor(out=cpj[:], in0=cpi[:], in1=cst(5, 128), op=ALU.bitwise_and), r=["fcpi", "pcs"], w=["fcpj"])
                S.op("dve", lambda e: e.tensor_tensor(out=cpi[:], in0=cpi[:], in1=cst(4, 128), op=ALU.arith_shift_right), r=["fcpi", "pcs"], w=["fcpi"])
                S.op("dve", lambda e: e.tensor_copy(out=aif[:], in_=cpi[:]), r=["fcpi"], w=["faif"])
                S.op("dve", lambda e: e.tensor_copy(out=bjf[:], in_=cpj[:]), r=["fcpj"], w=["fbjf"])
                io4 = pcf[:, 0:16].unsqueeze(1).unsqueeze(1).to_broadcast(B4)
                for which, (sel, dst_) in enumerate(((aif, iA), (bjf, iB))):
                    S.op("dve", lambda e: e.tensor_tensor(out=r4(sc2), in0=io4, in1=r3(sel).unsqueeze(3).to_broadcast(B4), op=ALU.is_equal), r=["pcf", "faif", "fbjf"], w=["fsc2"])
                    S.op("dve", lambda e: e.tensor_tensor(out=r4(sc2), in0=r4(sc2), in1=i16v[:, :, which, :].unsqueeze(2).to_broadcast(B4), op=ALU.mult), r=["fsc2", "fi16f"], w=["fsc2"])
                    S.op("dve", lambda e: e.tensor_reduce(out=dst_[:], in_=sc2[:].rearrange("p (k i) -> p k i", i=16), axis=AX.X, op=ALU.add), r=["fsc2"], w=["fiA", "fiB"])
                S.op("dve", lambda e: e.scalar_tensor_tensor(out=eif[:], in0=iA[:], scalar=128.0, in1=iB[:], op0=ALU.mult, op1=ALU.add), r=["fiA", "fiB"], w=["feif"])
                S.op("dve", lambda e: e.tensor_copy(out=eii[:], in_=eif[:]), r=["feif"], w=["feii"])
                S.op("dve", lambda e: e.tensor_reduce(out=gm[:], in_=r3(f16), axis=AX.X, op=ALU.max), r=["ff16"], w=["fgm"])
                S.op("dve", lambda e: e.tensor_tensor(out=r3(ge), in0=r3(f16), in1=gm[:].unsqueeze(2).to_broadcast([128, 8, 16]), op=ALU.subtract), r=["ff16", "fgm"], w=["fge"])
                S.op("act", lambda e: e.activation(out=ge[:], in_=ge[:], func=AF.Exp), r=["fge"], w=["fge"])
                S.op("dve", lambda e: e.tensor_reduce(out=gs[:], in_=r3(ge), axis=AX.X, op=ALU.add), r=["fge"], w=["fgs"])
                S.op("dve", lambda e: e.reciprocal(out=gs[:], in_=gs[:]), r=["fgs"], w=["fgs"])
                S.op("dve", lambda e: e.tensor_tensor(out=r3(gw), in0=r3(ge), in1=gs[:].unsqueeze(2).to_broadcast([128, 8, 16]), op=ALU.mult), r=["fge", "fgs"], w=["fgw"])
                S.op("dve", lambda e: e.memset(aa[:], 0.0), w=["faa"])
                for k in range(128):
                    gt_, gtk = g_r.next()
                    S.dma("pool", lambda e: e.indirect_dma_start(out=gt_[:], out_offset=None, in_=peer_u, in_offset=bass.IndirectOffsetOnAxis(ap=eii[:, k:k + 1], axis=0)), r=["feii"], w=[gtk])
                    S.op("dve", lambda e: e.scalar_tensor_tensor(out=junk[:], in0=gt_[:], scalar=1.0, in1=h2[:], op0=ALU.mult, op1=ALU.mult, accum_out=aa[:, k:k + 1]), r=[gtk, "fh2"], w=["fjunk", "faa"])
                S.op("act", lambda e: e.activation(out=ww[:], in_=aa[:], func=AF.Gelu), r=["faa"], w=["fww"])
                S.op("dve", lambda e: e.tensor_tensor(out=ww[:], in0=ww[:], in1=gw[:], op=ALU.mult), r=["fww", "fgw"], w=["fww"])
                S.op("dve", lambda e: e.memset(acc[:], 0.0), w=["facc"])
                for k in range(128):
                    gt_, gtk = g_r.next()
                    S.dma("pool", lambda e: e.indirect_dma_start(out=gt_[:], out_offset=None, in_=peer_v, in_offset=bass.IndirectOffsetOnAxis(ap=eii[:, k:k + 1], axis=0)), r=["feii"], w=[gtk])
                    S.op("dve", lambda e: e.scalar_tensor_tensor(out=acc[:], in0=gt_[:], scalar=ww[:, k:k + 1], in1=acc[:], op0=ALU.mult, op1=ALU.add), r=[gtk, "fww", "facc"], w=["facc"])
                S.op("dve", lambda e: e.tensor_tensor(out=acc[:], in0=acc[:], in1=gt2row, op=ALU.mult), r=["facc", "rowsb"], w=["facc"])
                S.op("dve", lambda e: e.tensor_tensor(out=x2[:], in0=acc[:], in1=x1[:], op=ALU.add), r=["facc", x1k], w=["fx2"])
                S.op("act", lambda e: e.activation(out=junk[:], in_=x2[:], func=AF.Square, accum_out=st4[:, 2:3]), r=["fx2"], w=["fjunk", "fst4"])
                S.op("act", lambda e: e.activation(out=st4[:, 3:4], in_=st4[:, 2:3], func=AF.Sqrt, bias=EPS, scale=1.0 / D), r=["fst4"], w=["fst4"])
                S.op("dve", lambda e: e.reciprocal(out=st4[:, 3:4], in_=st4[:, 3:4]), r=["fst4"], w=["fst4"])
                ot, otk = ot_r.next()
                S.op("dve", lambda e: e.scalar_tensor_tensor(out=ot[:], in0=x2[:], scalar=st4[:, 3:4], in1=afrow, op0=ALU.mult, op1=ALU.mult), r=["fx2", "fst4", "rowsb"], w=[otk])
                S.op("dve", lambda e: e.tensor_tensor(out=ot[:], in0=ot[:], in1=fshrow, op=ALU.add), r=[otk, "rowsb"], w=[otk])
                S.dma("sp", lambda e: e.dma_start(out=out[t0:t0 + 128, :], in_=ot[:]), r=[otk], is_out=True)
            S.barrier()

        S.finish()
    return nc


def PHASES_AFTER_B(nc, S, st, L):
    pass


def _consts():
    p = np.arange(128)
    ident = np.eye(128, dtype=np.float32)
    j = p[:, None]; c = p[None, :]
    same = (j // 64) == (c // 64)
    triA = (same & (j <= c)).astype(np.float32)
    triB = (same & (j >= c)).astype(np.float32)
    striA = (same & (j > c)).astype(np.float32)
    striB = (same & (j < c)).astype(np.float32)
    tri4 = np.concatenate([triA, triB, striA, striB], axis=1)
    f = (p % 32).astype(np.float64)
    inv = (10000.0 ** (-(2.0 * f) / 64.0)).astype(np.float32)
    sign = np.where((p % 64) < 32, -1.0, 1.0).astype(np.float32)
    ropec = np.stack([inv, sign], axis=1).astype(np.float32)
    return ident, tri4, ropec


def make_in_maps(inputs, T_OWN=None):
    x = np.asarray(inputs["x"]); B, S_, _ = x.shape
    T_OWN = S_ // 2
    ident, tri4, ropec = _consts()
    w_in = np.asarray(inputs["w_in"])[0]
    cols = lambda a, b: w_in[:, a:b]
    perm = np.arange(1024).reshape(16, 64)
    perm = np.concatenate([perm[:, 32:], perm[:, :32]], axis=1).reshape(-1)
    dq = cols(3104, 4128); dk = cols(4128, 5152)
    pconst = np.zeros((128, 4096), np.int32)
    pconst[:, 0:2048] = (np.arange(2048) % 128)[None, :]
    pconst[:, 2048:4096] = (np.arange(2048) % 256)[None, :]
    pconstf = np.zeros((128, 64), np.float32)
    pconstf[:, 0:16] = np.arange(16, dtype=np.float32)[None, :]
    pcs = np.tile(np.array([-128, -256, 127, 255, 4, 15, 0, 0], np.int32)[None, :], (128, 1))
    common = {
        "w_ada": np.ascontiguousarray(inputs["w_ada"][0]),
        "w_fada": np.ascontiguousarray(inputs["w_final_ada"]),
        "lqk": np.concatenate([inputs["diff_lq1"][0], inputs["diff_lk1"][0], inputs["diff_lq2"][0], inputs["diff_lk2"][0]])[None, :].astype(np.float32),
        "dng": np.ascontiguousarray(inputs["diff_norm_g"]),
        "w_gp": np.ascontiguousarray(inputs["w_gla_proj"][0]), "w_dp": np.ascontiguousarray(inputs["w_diff_proj"][0]),
        "w_out": np.ascontiguousarray(inputs["w_out"][0]), "w_pq": np.ascontiguousarray(inputs["peer_wq"][0]),
        "skT": np.ascontiguousarray(np.transpose(inputs["peer_subkeys"][0].reshape(16, 128, 128), (2, 0, 1)).reshape(128, 2048)),
        "peer_u": np.ascontiguousarray(inputs["peer_u"][0]), "peer_v": np.ascontiguousarray(inputs["peer_v"][0]),
        "ident": ident, "tri4": tri4, "ropec": ropec, "pconst": pconst, "pconstf": pconstf, "pcs": pcs,
        "gvecT": np.ascontiguousarray(np.concatenate([inputs["norm1_g"][0].reshape(8, 128).T, inputs["norm2_g"][0].reshape(8, 128).T,
                                                       inputs["normf_g"].reshape(8, 128).T, inputs["gla_norm_g"][0].reshape(8, 128).T], axis=1)),
        "badaT": np.ascontiguousarray(np.concatenate([inputs["b_ada"][0].reshape(48, 128).T, inputs["b_final_ada"].reshape(16, 128).T], axis=1)),
    }
    w_all_hf = []
    for hf in range(2):
        lrA = cols(3072, 3088) if hf == 0 else cols(3088, 3104)
        lrB = cols(3088, 3104) if hf == 0 else cols(3072, 3088)
        w_all_hf.append(np.ascontiguousarray(np.concatenate(
            [cols(0, 3072), lrA, lrB, dq, dq[:, perm], dk, dk[:, perm], cols(5152, 6176), cols(6176, 8224)], axis=1)))
    fw = (inputs["gla_wa_fw"][0], inputs["gla_ba_fw"]); bw = (inputs["gla_wa_bw"][0], inputs["gla_ba_bw"])
    in_maps = []
    for b in range(B):
        for hf in range(2):
            xb = x[b]; pb_ = np.asarray(inputs["positions"])[b]
            if hf == 1:
                xb = xb[::-1]; pb_ = pb_[::-1]
            A, Bd = (fw, bw) if hf == 0 else (bw, fw)
            m = dict(common)
            m.update({
                "xT": np.ascontiguousarray(xb.T), "pos": np.ascontiguousarray(pb_[None, :].astype(np.int32)),
                "cT": np.ascontiguousarray(np.asarray(inputs["c"])[b].reshape(8, 128).T),
                "w_all": w_all_hf[hf],
                "wa_A": np.ascontiguousarray(A[0]), "ba_A": np.ascontiguousarray(A[1].reshape(1, 512)),
                "wa_B": np.ascontiguousarray(Bd[0]), "ba_B": np.ascontiguousarray(Bd[1].reshape(1, 512)),
            })
            in_maps.append(m)
    return in_maps, B, T_OWN


def kernel(**inputs):
    inputs = {k: np.asarray(v) for k, v in inputs.items()}
    in_maps, B, T_OWN = make_in_maps(inputs)
    nc = build(T_OWN)
    res = run_bass_kernel_spmd(nc, in_maps, core_ids=list(range(len(in_maps))))
    S_ = 2 * T_OWN
    out = np.zeros((B, S_, D), np.float32)
    for b in range(B):
        for hf in range(2):
            o = res.results[b * 2 + hf]["out"]
            if hf == 0:
                out[b, :T_OWN] = o
            else:
                out[b, T_OWN:] = o[::-1]
    return out
```

```python
import math, os
from contextlib import ExitStack
import numpy as np
import concourse.bass as bass
import concourse.mybir as mybir
from concourse.bass_utils import run_bass_kernel_spmd

F32 = mybir.dt.float32
BF16 = mybir.dt.bfloat16
I32 = mybir.dt.int32
AF = mybir.ActivationFunctionType
ALU = mybir.AluOpType
AX = mybir.AxisListType

D = 1024
NW1 = 3104
NW2 = 5120
NW3 = 2048
EPS = 1e-6
TWO_PI = 2.0 * math.pi


class Sched:
    NDSEM = 6
    NDQ = {"sp": 6, "act": 6, "pool": 12}

    def __init__(self, nc, stack):
        self.nc = nc
        self.engs = {"pe": nc.tensor, "dve": nc.vector, "act": nc.scalar, "pool": nc.gpsimd, "sp": nc.sync}
        self.sem = {k: stack.enter_context(nc.semaphore("s_" + k)) for k in self.engs}
        self.cnt = {k: 0 for k in self.engs}
        self.dq = ["sp", "act", "pool"]
        self.dsem = {q: [stack.enter_context(nc.semaphore("d_%s%d" % (q, i))) for i in range(self.NDQ[q])] for q in self.dq}
        self.dcnt = {q: 0 for q in self.dq}
        self.waited = {}
        self.lastw = {}
        self.readers = {}
        self.out_tokens = []

    def _need(self, stream, tok):
        sem, val, tstream, isdma = tok
        if (not isdma) and tstream == stream and stream == "pe":
            return
        key = (stream, id(sem))
        if self.waited.get(key, 0) >= val:
            return
        self.waited[key] = val
        self.engs[stream].wait_ge(sem, val)

    def _deps(self, stream, r, w):
        for k in r:
            t = self.lastw.get(k)
            if t is not None:
                self._need(stream, t)
        for k in w:
            t = self.lastw.get(k)
            if t is not None:
                self._need(stream, t)
            for t in self.readers.get(k, {}).values():
                self._need(stream, t)

    def _commit(self, tok, r, w):
        for k in r:
            self.readers.setdefault(k, {})[(tok[2], tok[3], id(tok[0]))] = tok
        for k in w:
            self.lastw[k] = tok
            self.readers[k] = {}

    def op(self, stream, fn, r=(), w=()):
        self._deps(stream, r, w)
        ins = fn(self.engs[stream])
        self.cnt[stream] += 1
        ins.then_inc(self.sem[stream], 1)
        tok = (self.sem[stream], self.cnt[stream], stream, False)
        self._commit(tok, r, w)
        return tok

    def dma(self, q, fn, r=(), w=(), is_out=False):
        j = self.dcnt[q]
        nd = self.NDQ[q]
        sem = self.dsem[q][j % nd]
        if j >= nd:
            self._need(q, (sem, 16 * (j // nd), q, True))
        self._deps(q, r, w)
        ins = fn(self.engs[q])
        ins.then_inc(sem, 16)
        self.dcnt[q] += 1
        tok = (sem, 16 * (j // nd + 1), q, True)
        self._commit(tok, r, w)
        if is_out:
            self.out_tokens.append(tok)
        return tok

    def barrier(self):
        toks = []
        for s in self.engs:
            if self.cnt[s] > 0:
                toks.append((self.sem[s], self.cnt[s], s, False))
        for q in self.dq:
            j = self.dcnt[q]
            nd = self.NDQ[q]
            for i in range(max(0, j - nd), j):
                toks.append((self.dsem[q][i % nd], 16 * (i // nd + 1), q, True))
        for s in self.engs:
            for t in toks:
                if (not t[3]) and t[2] == s:
                    continue
                self._need(s, t)

    def finish(self):
        for tok in self.out_tokens:
            self._need("sp", tok)


class Rot:
    def __init__(self, tiles, name, keys=None):
        self.tiles = tiles
        self.name = name
        self.keys = keys
        self.i = 0

    def next(self):
        j = self.i % len(self.tiles)
        self.i += 1
        return self.tiles[j], (self.keys[j] if self.keys else "%s#%d" % (self.name, j))


def build(T_OWN, debug=False):
    KSKIP = os.environ.get('K_SKIP', '')
    T_ALL = 2 * T_OWN
    NT_OWN = T_OWN // 512
    NT_ALL = T_ALL // 512
    nc = bass.Bass("TRN2", target_bir_lowering=False)

    def din(name, shape, dt=F32):
        return nc.dram_tensor(name, shape, dt, kind="ExternalInput").ap()

    def dscr(name, shape, dt=F32):
        return nc.dram_tensor(name, shape, dt, kind="ExternalOutput" if debug else "Internal").ap()

    xT = din("xT", [D, T_ALL])
    pos = din("pos", [1, T_ALL], I32)
    cT = din("cT", [128, 8])
    w_ada = din("w_ada", [D, 6144])
    w_fada = din("w_fada", [D, 2048])
    badaT = din("badaT", [128, 64])
    gvecT = din("gvecT", [128, 32])
    w_all = din("w_all", [D, NW1 + NW2 + NW3])
    wa_A = din("wa_A", [16, 512]); ba_A = din("ba_A", [1, 512])
    wa_B = din("wa_B", [16, 512]); ba_B = din("ba_B", [1, 512])
    lqk = din("lqk", [1, 256])
    dng = din("dng", [1, 1024])
    w_gp = din("w_gp", [D, D]); w_dp = din("w_dp", [D, D]); w_out = din("w_out", [D, D])
    w_pq = din("w_pq", [D, 2048])
    skT = din("skT", [128, 2048])
    peer_u = din("peer_u", [16384, D]); peer_v = din("peer_v", [16384, D])
    ident_d = din("ident", [128, 128])
    tri_d = din("tri4", [128, 512])
    ropec_d = din("ropec", [128, 2])
    pconst_d = din("pconst", [128, 4096], I32)
    pconstf_d = din("pconstf", [128, 64])
    pcs_d = din("pcs", [128, 8], I32)
    out = nc.dram_tensor("out", [T_OWN, D], F32, kind="ExternalOutput").ap()

    S_hT = dscr("S_hT", [D, T_ALL], BF16)
    S_rows = dscr("S_rows", [1, 40 * 128])
    S_gqT = dscr("S_gqT", [512, T_OWN]); S_gkT = dscr("S_gkT", [512, T_OWN])
    S_grT = dscr("S_grT", [1024, T_OWN], BF16)
    S_lrA = dscr("S_lrA", [16, T_OWN]); S_lrB = dscr("S_lrB", [16, T_ALL])
    S_gk = dscr("S_gk", [T_ALL, 512]); S_gv = dscr("S_gv", [T_ALL, 1024], BF16)
    S_qT = dscr("S_qT", [1024, T_OWN], BF16); S_kT = dscr("S_kT", [1024, T_ALL], BF16)
    S_dv = dscr("S_dv", [T_ALL, 1024], BF16)
    S_saT = dscr("S_saT", [1024, T_OWN], BF16); S_sbT = dscr("S_sbT", [1024, T_OWN], BF16)
    S_oT = dscr("S_oT", [1024, T_OWN])
    S_odT = dscr("S_odT", [1024, T_OWN], BF16)
    S_x1 = dscr("S_x1", [T_OWN, D])
    S_idx = dscr("S_idx", [T_OWN, 128], I32)
    S_gw = dscr("S_gw", [T_OWN, 128])

    with ExitStack() as st:
        S = Sched(nc, st)

        uniq = [0]

        def sb(name, shape, dt=F32, stack=st):
            uniq[0] += 1
            return stack.enter_context(nc.sbuf_tensor("s%d_%s" % (uniq[0], name), shape, dt))

        def rot(name, shape, dt, n, stack=st):
            return Rot([sb("%s%d" % (name, i), shape, dt, stack) for i in range(n)], name)

        pb = [st.enter_context(nc.psum_tensor("pb%d" % i, [128, 512], F32)) for i in range(7)]
        ptb = st.enter_context(nc.psum_tensor("ptb", [128, 1024], BF16))
        PBK = ["pb%d" % i for i in range(7)]

        ident = sb("ident", [128, 128]); tri4 = sb("tri4", [128, 512]); ropec = sb("ropec", [128, 2])
        ident_bf = sb("ident_bf", [128, 128], BF16)
        ones_bf = sb("ones_bf", [128, 128], BF16); ones_f = sb("ones_f", [1, 128])
        S.dma("sp", lambda e: e.dma_start(out=ident[:], in_=ident_d), w=["ident"])
        S.dma("sp", lambda e: e.dma_start(out=tri4[:], in_=tri_d), w=["tri4"])
        S.dma("sp", lambda e: e.dma_start(out=ropec[:], in_=ropec_d), w=["ropec"])
        S.op("dve", lambda e: e.tensor_copy(out=ident_bf[:], in_=ident[:]), r=["ident"], w=["ident_bf"])
        S.op("pool", lambda e: e.memset(ones_bf[:], 1.0), w=["ones_bf"])
        S.op("pool", lambda e: e.memset(ones_f[:], 1.0), w=["ones_f"])
        triA, triB, striA, striB = (tri4[:, i * 128:(i + 1) * 128] for i in range(4))

        modT = sb("modT", [128, 64]); gvec = sb("gvec", [128, 32]); avec = sb("avec", [128, 24])
        lam = sb("lam", [128, 1]); dngr = sb("dngr", [128, 1024])
        sh1 = lambda kc: modT[:, kc:kc + 1]
        gt1 = lambda kc: modT[:, 16 + kc:17 + kc]

        with ExitStack() as p0:
            cTt = sb("cTt", [128, 8], F32, p0); cact = sb("cact", [128, 8], F32, p0)
            badat = sb("badat", [128, 64], F32, p0)
            wa_r = rot("wada", [128, 6144], F32, 2, p0)
            wf_r = rot("wfada", [128, 2048], F32, 2, p0)
            lq = sb("lq", [128, 256], F32, p0); lj = sb("lj", [128, 64], F32, p0); l2 = sb("l2", [128, 2], F32, p0)
            rows = sb("rows", [128, 40], F32, p0)
            S.dma("sp", lambda e: e.dma_start(out=cTt[:], in_=cT), w=["cTt"])
            S.dma("sp", lambda e: e.dma_start(out=badat[:], in_=badaT), w=["badat"])
            S.dma("sp", lambda e: e.dma_start(out=gvec[:], in_=gvecT), w=["gvec"])
            S.dma("sp", lambda e: e.dma_start(out=lq[:], in_=lqk.partition_broadcast(128)), w=["lq"])
            S.dma("sp", lambda e: e.dma_start(out=dngr[:], in_=dng.partition_broadcast(128)), w=["dngr"])
            S.op("act", lambda e: e.activation(out=cact[:], in_=cTt[:], func=AF.Silu), r=["cTt"], w=["cact"])
            mps = pb[0]
            for kc in range(8):
                wt, wk = wa_r.next(); ft, fk = wf_r.next()
                S.dma("sp", lambda e: e.dma_start(out=wt[:], in_=w_ada[kc * 128:(kc + 1) * 128, :]), w=[wk])
                S.dma("act", lambda e: e.dma_start(out=ft[:], in_=w_fada[kc * 128:(kc + 1) * 128, :]), w=[fk])
                for j in range(48):
                    S.op("pe", lambda e: e.matmul(mps[:, j:j + 1], lhsT=wt[:, j * 128:(j + 1) * 128], rhs=cact[:, kc:kc + 1], start=(kc == 0 and j == 0), stop=False), r=[wk, "cact"], w=["pb0"])
                for j in range(16):
                    S.op("pe", lambda e: e.matmul(mps[:, 48 + j:49 + j], lhsT=ft[:, j * 128:(j + 1) * 128], rhs=cact[:, kc:kc + 1], start=False, stop=(kc == 7 and j == 15)), r=[fk, "cact"], w=["pb0"])
            S.op("dve", lambda e: e.tensor_tensor(out=modT[:], in0=mps[:, 0:64], in1=badat[:], op=ALU.add), r=["pb0", "badat"], w=["modT"])
            for i, (c0, g0) in enumerate(((8, 0), (32, 8), (56, 16))):
                S.op("dve", lambda e: e.scalar_tensor_tensor(out=avec[:, i * 8:(i + 1) * 8], in0=modT[:, c0:c0 + 8], scalar=1.0, in1=gvec[:, g0:g0 + 8], op0=ALU.add, op1=ALU.mult), r=["modT", "gvec"], w=["avec"])
            for i in range(2):
                S.op("dve", lambda e: e.scalar_tensor_tensor(out=lj[:], in0=lq[:, i * 128:i * 128 + 64], scalar=1.0, in1=lq[:, i * 128 + 64:i * 128 + 128], op0=ALU.mult, op1=ALU.mult, accum_out=l2[:, i:i + 1]), r=["lq"], w=["lj", "l2"])
            S.op("act", lambda e: e.activation(out=l2[:], in_=l2[:], func=AF.Exp), r=["l2"], w=["l2"])
            S.op("dve", lambda e: e.scalar_tensor_tensor(out=lam[:], in0=l2[:, 0:1], scalar=0.2, in1=l2[:, 1:2], op0=ALU.add, op1=ALU.subtract), r=["l2"], w=["lam"])
            S.op("dve", lambda e: e.tensor_scalar(out=dngr[:], in0=dngr[:], scalar1=0.8, scalar2=None, op0=ALU.mult), r=["dngr"], w=["dngr"])
            S.op("dve", lambda e: e.tensor_copy(out=rows[:, 0:8], in_=avec[:, 8:16]), r=["avec"], w=["rows"])
            S.op("dve", lambda e: e.tensor_copy(out=rows[:, 8:16], in_=modT[:, 24:32]), r=["modT"], w=["rows"])
            S.op("dve", lambda e: e.tensor_copy(out=rows[:, 16:24], in_=modT[:, 40:48]), r=["modT"], w=["rows"])
            S.op("dve", lambda e: e.tensor_copy(out=rows[:, 24:32], in_=avec[:, 16:24]), r=["avec"], w=["rows"])
            S.op("dve", lambda e: e.tensor_copy(out=rows[:, 32:40], in_=modT[:, 48:56]), r=["modT"], w=["rows"])
            S.dma("sp", lambda e: e.dma_start(out=S_rows.rearrange("o (j p) -> p (o j)", p=128), in_=rows[:], allow_slow_non_contiguous=True), r=["rows"])
            S.barrier()
            if os.environ.get('K_STOP') == '1':
                S.finish(); return nc

        def norm_mod(xt, xk, hT, hk, a_col, sh_col, ntok, stk_rs, ps, psk):
            sq, sqk = stk_rs["sq"].next()
            rs, rsk = stk_rs["rs"].next()
            S.op("act", lambda e: e.activation(out=sq[:, :, 0:ntok], in_=xt[:, :, 0:ntok], func=AF.Square), r=[xk], w=[sqk])
            for kc in range(8):
                S.op("pe", lambda e: e.matmul(ps[:, 0:ntok], lhsT=ones_bf[:], rhs=sq[:, kc, 0:ntok], start=(kc == 0), stop=(kc == 7)), r=["ones_bf", sqk], w=[psk])
            S.op("act", lambda e: e.activation(out=rs[:, 0:ntok], in_=ps[:, 0:ntok], func=AF.Sqrt, bias=EPS, scale=1.0 / D), r=[psk], w=[rsk])
            S.op("dve", lambda e: e.reciprocal(out=rs[:, 0:ntok], in_=rs[:, 0:ntok]), r=[rsk], w=[rsk])
            for kc in range(8):
                S.op("dve", lambda e: e.scalar_tensor_tensor(out=xt[:, kc, 0:ntok], in0=xt[:, kc, 0:ntok], scalar=a_col(kc), in1=rs[:, 0:ntok], op0=ALU.mult, op1=ALU.mult), r=[xk, rsk, "avec"], w=[xk])
            for kc in range(8):
                S.op("act", lambda e: e.activation(out=hT[:, kc, 0:ntok], in_=xt[:, kc, 0:ntok], func=AF.Identity, bias=sh_col(kc), scale=1.0), r=[xk, "modT"], w=[hk])

        xT_v = xT.rearrange("(k p) t -> p k t", p=128)
        hT_v = S_hT.rearrange("(k p) t -> p k t", p=128)
        with ExitStack() as pa:
            x_r = rot("xa", [128, 8, 512], F32, 2, pa)
            h_r = rot("ha", [128, 8, 512], BF16, 2, pa)
            rs_r = {"sq": rot("sqa", [128, 8, 512], BF16, 2, pa), "rs": rot("rsa", [128, 512], F32, 2, pa)}
            for tt in range(NT_ALL):
                xt, xk = x_r.next(); ht, hk = h_r.next()
                t0 = tt * 512
                S.dma("sp", lambda e: e.dma_start(out=xt[:], in_=xT_v[:, :, t0:t0 + 512]), w=[xk])
                norm_mod(xt, xk, ht, hk, lambda kc: avec[:, kc:kc + 1], sh1, 512, rs_r, pb[tt % 2], PBK[tt % 2])
                S.dma("act", lambda e: e.dma_start(out=hT_v[:, :, t0:t0 + 512], in_=ht[:]), r=[hk])
            S.barrier()
            if os.environ.get('K_STOP') == '2':
                S.finish(); return nc

        psrot = Rot(pb[0:6], "pb", PBK[0:6])

        psrotD = Rot(pb[0:4], "pbD", PBK[0:4])
        psrotA = Rot(pb[4:7], "pbA", PBK[4:7])
        AFFINITY = not os.environ.get("K_NOAFF")

        def nps(eng="dve"):
            if not AFFINITY:
                return psrot.next()
            return (psrotD if eng == "dve" else psrotA).next()

        wst_r = rot("wst", [128, 1024], F32, 3)
        lwc = [0]

        def load_w(Wt, wkey, c0, n, src=None, kcs=8):
            src = w_all if src is None else src
            for kc in range(kcs):
                for p0 in range(0, n, 1024):
                    pn = min(1024, n - p0)
                    wst, wstk = wst_r.next()
                    q = "sp" if lwc[0] % 2 == 0 else "act"
                    S.dma(q, lambda e: e.dma_start(out=wst[:, 0:pn], in_=src[kc * 128:(kc + 1) * 128, c0 + p0:c0 + p0 + pn]), w=[wstk])
                    eng = "pool" if (lwc[0] % 2 == 0 and not os.environ.get("K_NOPOOL")) else "dve"
                    lwc[0] += 1
                    dst = Wt[:, kc, p0:p0 + pn] if kcs > 1 else Wt[:, p0:p0 + pn]
                    S.op(eng, lambda e: e.tensor_copy(out=dst, in_=wst[:, 0:pn]), r=[wstk], w=[wkey])

        def fm_chunk(ps, psk, Wt, wkey, wc, n, ht, hk):
            for kc in range(8):
                S.op("pe", lambda e: e.matmul(ps[0:n, :], lhsT=Wt[:, kc, wc:wc + n], rhs=ht[:, kc, :], start=(kc == 0), stop=(kc == 7)), r=[wkey, hk], w=[psk])

        def tm_chunk(ps, psk, Wt, wkey, wc, n, ht, hk, sub):
            for kc in range(8):
                S.op("pe", lambda e: e.matmul(ps[:, 0:n], lhsT=ht[:, kc, sub * 128:(sub + 1) * 128], rhs=Wt[:, kc, wc:wc + n], start=(kc == 0), stop=(kc == 7)), r=[wkey, hk], w=[psk])

        stq = ["sp", "act"]
        stc = [0]

        def store(dst, src, key):
            q = stq[stc[0] % 2] if not os.environ.get("K_Q1") else "sp"; stc[0] += 1
            S.dma(q, lambda e: e.dma_start(out=dst, in_=src), r=[key])

        with ExitStack() as pp:
            W1 = sb("W1", [128, 8, NW1], BF16, pp)
            load_w(W1, "W1", 0, NW1)
            h_r = rot("hb", [128, 8, 512], BF16, 2, pp)
            sf_r = rot("sf", [128, 512], F32, 4, pp)
            sh_r = rot("sh", [128, 512], BF16, 4, pp)
            for tt in range(NT_ALL):
                own = tt < NT_OWN
                t0 = tt * 512
                ht, hk = h_r.next()
                S.dma("sp", lambda e: e.dma_start(out=ht[:], in_=hT_v[:, :, t0:t0 + 512]), w=[hk])
                if own and 'f' not in KSKIP:
                    for c in (range(8) if 'g' not in KSKIP else ()):
                        ps, psk = nps(); fm_chunk(ps, psk, W1, "W1", c * 128, 128, ht, hk)
                        sf, sfk = sf_r.next()
                        S.op("dve", lambda e: e.tensor_copy(out=sf[:], in_=ps[:]), r=[psk], w=[sfk])
                        dst = (S_gqT if c < 4 else S_gkT)[(c % 4) * 128:(c % 4 + 1) * 128, t0:t0 + 512]
                        store(dst, sf[:], sfk)
                    for c in (range(8) if 'r' not in KSKIP else ()):
                        ps, psk = nps("act" if os.environ.get("K_GRF") in ("exp", "ident") else "dve"); fm_chunk(ps, psk, W1, "W1", 2048 + c * 128, 128, ht, hk)
                        sh, shk = sh_r.next()
                        if os.environ.get("K_GRF") == "exp":
                            S.op("act", lambda e: e.activation(out=sh[:], in_=ps[:], func=AF.Exp), r=[psk], w=[shk])
                        elif os.environ.get("K_GRF") == "ident":
                            S.op("act", lambda e: e.activation(out=sh[:], in_=ps[:], func=AF.Identity), r=[psk], w=[shk])
                        else:
                            sf, sfk = sf_r.next()
                            S.op("dve", lambda e: e.tensor_copy(out=sf[:], in_=ps[:]), r=[psk], w=[sfk])
                            S.op("act", lambda e: e.activation(out=sh[:], in_=sf[:], func=AF.Silu), r=[sfk], w=[shk])
                        store(S_grT[c * 128:(c + 1) * 128, t0:t0 + 512], sh[:], shk)
                    if "l" not in KSKIP:
                        ps, psk = nps(); fm_chunk(ps, psk, W1, "W1", 3072, 16, ht, hk)
                        sf, sfk = sf_r.next()
                        S.op("dve", lambda e: e.tensor_copy(out=sf[0:16, :], in_=ps[0:16, :]), r=[psk], w=[sfk])
                        store(S_lrA[:, t0:t0 + 512], sf[0:16, :], sfk)
                if "l" not in KSKIP:
                    ps, psk = nps(); fm_chunk(ps, psk, W1, "W1", 3088, 16, ht, hk)
                    sf, sfk = sf_r.next()
                    S.op("dve", lambda e: e.tensor_copy(out=sf[0:16, :], in_=ps[0:16, :]), r=[psk], w=[sfk])
                    store(S_lrB[:, t0:t0 + 512], sf[0:16, :], sfk)
                for sub in (range(4) if "t" not in KSKIP else ()):
                    r0 = t0 + sub * 128
                    ps, psk = nps(); tm_chunk(ps, psk, W1, "W1", 512, 512, ht, hk, sub)
                    sf, sfk = sf_r.next()
                    S.op("dve", lambda e: e.tensor_copy(out=sf[:], in_=ps[:]), r=[psk], w=[sfk])
                    store(S_gk[r0:r0 + 128, :], sf[:], sfk)
                    for c in range(2):
                        ps, psk = nps("act"); tm_chunk(ps, psk, W1, "W1", 1024 + c * 512, 512, ht, hk, sub)
                        sh, shk = sh_r.next()
                        S.op("act", lambda e: e.activation(out=sh[:], in_=ps[:], func=AF.Identity), r=[psk], w=[shk])
                        store(S_gv[r0:r0 + 128, c * 512:(c + 1) * 512], sh[:], shk)
            S.barrier()
            if os.environ.get('K_STOP') == '3':
                S.finish(); return nc

        with ExitStack() as pp:
            W2 = sb("W2", [128, 8, NW2], BF16, pp)
            load_w(W2, "W2", NW1, NW2)
            h_r = rot("hb", [128, 8, 512], BF16, 2, pp)
            sh_r = rot("sh", [128, 512], BF16, 4, pp)
            posi = sb("posi", [128, 512], I32, pp); ang = sb("ang", [128, 512], F32, pp)
            kf = sb("kf", [128, 512], F32, pp)
            cos_t = sb("cos_t", [128, 512], F32, pp); sin_t = sb("sin_t", [128, 512], F32, pp)
            t1_r = rot("t1", [128, 512], F32, 2, pp); t2_r = rot("t2", [128, 512], F32, 2, pp)
            for tt in range(NT_ALL):
                own = tt < NT_OWN
                t0 = tt * 512
                ht, hk = h_r.next()
                S.dma("sp", lambda e: e.dma_start(out=ht[:], in_=hT_v[:, :, t0:t0 + 512]), w=[hk])
                S.dma("sp", lambda e: e.dma_start(out=posi[:], in_=pos[:, t0:t0 + 512].partition_broadcast(128)), w=["posi"])
                S.op("dve", lambda e: e.tensor_copy(out=ang[:], in_=posi[:]), r=["posi"], w=["ang"])
                S.op("dve", lambda e: e.tensor_scalar(out=ang[:], in0=ang[:], scalar1=ropec[:, 0:1], scalar2=None, op0=ALU.mult), r=["ang", "ropec"], w=["ang"])
                S.op("dve", lambda e: e.tensor_scalar(out=kf[:], in0=ang[:], scalar1=1.0 / TWO_PI, scalar2=None, op0=ALU.mult), r=["ang"], w=["kf"])
                S.op("dve", lambda e: e.tensor_copy(out=posi[:], in_=kf[:]), r=["kf"], w=["posi"])
                S.op("dve", lambda e: e.tensor_copy(out=kf[:], in_=posi[:]), r=["posi"], w=["kf"])
                S.op("dve", lambda e: e.scalar_tensor_tensor(out=ang[:], in0=kf[:], scalar=-TWO_PI, in1=ang[:], op0=ALU.mult, op1=ALU.add), r=["kf", "ang"], w=["ang"])
                S.op("dve", lambda e: e.tensor_scalar(out=ang[:], in0=ang[:], scalar1=math.pi, scalar2=-math.pi, op0=ALU.min, op1=ALU.max), r=["ang"], w=["ang"])
                S.op("act", lambda e: e.activation(out=sin_t[:], in_=ang[:], func=AF.Sin, scale=ropec[:, 1:2]), r=["ang", "ropec"], w=["sin_t"])
                S.op("act", lambda e: e.activation(out=kf[:], in_=ang[:], func=AF.Abs), r=["ang"], w=["kf"])
                S.op("act", lambda e: e.activation(out=cos_t[:], in_=kf[:], func=AF.Sin, bias=math.pi / 2, scale=-1.0), r=["kf"], w=["cos_t"])
                for which in ((0, 1) if own else (1,)):
                    base = which * 2048
                    for h in range(8):
                        ps, psk = nps(); fm_chunk(ps, psk, W2, "W2", base + h * 128, 128, ht, hk)
                        ps2, psk2 = nps(); fm_chunk(ps2, psk2, W2, "W2", base + 1024 + h * 128, 128, ht, hk)
                        t1, t1k = t1_r.next(); t2, t2k = t2_r.next()
                        S.op("dve", lambda e: e.tensor_tensor(out=t1[:], in0=ps[:], in1=cos_t[:], op=ALU.mult), r=[psk, "cos_t"], w=[t1k])
                        S.op("dve", lambda e: e.tensor_tensor(out=t2[:], in0=ps2[:], in1=sin_t[:], op=ALU.mult), r=[psk2, "sin_t"], w=[t2k])
                        sh, shk = sh_r.next()
                        S.op("pool", lambda e: e.tensor_tensor(out=sh[:], in0=t1[:], in1=t2[:], op=ALU.add), r=[t1k, t2k], w=[shk])
                        dst = (S_qT if which == 0 else S_kT)[h * 128:(h + 1) * 128, t0:t0 + 512]
                        store(dst, sh[:], shk)
                for sub in range(4):
                    r0 = t0 + sub * 128
                    for c in range(2):
                        ps, psk = nps("act"); tm_chunk(ps, psk, W2, "W2", 4096 + c * 512, 512, ht, hk, sub)
                        sh, shk = sh_r.next()
                        S.op("act", lambda e: e.activation(out=sh[:], in_=ps[:], func=AF.Identity), r=[psk], w=[shk])
                        store(S_dv[r0:r0 + 128, c * 512:(c + 1) * 512], sh[:], shk)
            S.barrier()
            if os.environ.get('K_STOP') == '4':
                S.finish(); return nc

        with ExitStack() as pp:
            W3 = sb("W3", [128, 8, NW3], BF16, pp)
            sf3_r = rot("sf3", [128, 512], F32, 4, pp)
            load_w(W3, "W3", NW1 + NW2, NW3)
            h_r = rot("hb", [128, 8, 512], BF16, 2, pp)
            sh_r = rot("sh", [128, 512], BF16, 4, pp)
            for tt in range(NT_OWN):
                t0 = tt * 512
                ht, hk = h_r.next()
                S.dma("sp", lambda e: e.dma_start(out=ht[:], in_=hT_v[:, :, t0:t0 + 512]), w=[hk])
                for c in range(16):
                    ps, psk = nps(); fm_chunk(ps, psk, W3, "W3", c * 128, 128, ht, hk)
                    sh, shk = sh_r.next()
                    sf, sfk = sf3_r.next()
                    S.op("dve", lambda e: e.tensor_copy(out=sf[:], in_=ps[:]), r=[psk], w=[sfk])
                    S.op("act", lambda e: e.activation(out=sh[:], in_=sf[:], func=AF.Sigmoid), r=[sfk], w=[shk])
                    dst = (S_saT if c < 8 else S_sbT)[(c % 8) * 128:(c % 8 + 1) * 128, t0:t0 + 512]
                    store(dst, sh[:], shk)
            S.barrier()
            if os.environ.get('K_STOP') == '5':
                S.finish(); return nc

        DKS = 128 ** -0.5
        oT_v = S_oT.rearrange("(c p) t -> p c t", p=128)
        gqT_v = S_gqT.rearrange("(h p) t -> p h t", p=128)
        gkT_v = S_gkT.rearrange("(h p) t -> p h t", p=128)
        with ExitStack() as pc:
            S32 = sb("S32", [128, 4, 256], F32, pc); Sbf = sb("Sbf", [128, 4, 256], BF16, pc)
            wa_t = sb("wa_t", [16, 512], F32, pc); ba_t = sb("ba_t", [1, 512], F32, pc)
            lr_r = rot("lr", [16, 128], F32, 2, pc)
            gk_r = rot("gkt", [128, 512], F32, 2, pc); gv_r = rot("gvt", [128, 1024], BF16, 2, pc)
            gq_r = rot("gqT", [128, 4, 128], F32, 2, pc); gkT_r = rot("gkT", [128, 4, 128], F32, 2, pc)
            sp_r = rot("spt", [128, 512], F32, 2, pc)
            eb_r = rot("expb", [128, 512], F32, 2, pc); enb_r = rot("expnb", [128, 512], F32, 2, pc)
            qe_r = rot("qe", [128, 4, 128], BF16, 2, pc); ke_r = rot("ke", [128, 4, 128], BF16, 2, pc)
            ee_r = rot("eend", [128, 512], F32, 2, pc); kend_r = rot("kend", [128, 512], BF16, 2, pc)
            attm_r = rot("attm", [128, 4, 128], BF16, 2, pc)
            osb_r = rot("osb", [128, 8, 128], F32, 2, pc); oa_r = rot("oa", [128, 8, 128], F32, 2, pc)
            for di in range(2):
                tri = triA if di == 0 else triB
                stri = striA if di == 0 else striB
                S.dma("sp", lambda e: e.dma_start(out=wa_t[:], in_=(wa_A if di == 0 else wa_B)), w=["wa_t"])
                S.dma("sp", lambda e: e.dma_start(out=ba_t[:], in_=(ba_A if di == 0 else ba_B)), w=["ba_t"])
                S.op("dve", lambda e: e.memset(S32[:], 0.0), w=["S32"])
                S.op("pool", lambda e: e.memset(Sbf[:], 0.0), w=["Sbf"])
                S_lr = S_lrA if di == 0 else S_lrB
                tiles = list(range(T_OWN // 128)) if di == 0 else list(range(T_ALL // 128 - 1, -1, -1))
                order = (0, 1) if di == 0 else (1, 0)
                for ti in tiles:
                    t0 = ti * 128
                    wout = ti < T_OWN // 128
                    lr, lrk = lr_r.next(); gk, gkk = gk_r.next(); gv, gvk = gv_r.next()
                    S.dma("sp", lambda e: e.dma_start(out=lr[:], in_=S_lr[:, t0:t0 + 128]), w=[lrk])
                    S.dma("act", lambda e: e.dma_start(out=gk[:], in_=S_gk[t0:t0 + 128, :]), w=[gkk])
                    S.dma("sp", lambda e: e.dma_start(out=gv[:], in_=S_gv[t0:t0 + 128, :]), w=[gvk])
                    if wout:
                        gq, gqk = gq_r.next(); gkT, gkTk = gkT_r.next()
                        S.dma("act", lambda e: e.dma_start(out=gq[:], in_=gqT_v[:, :, t0:t0 + 128]), w=[gqk])
                        S.dma("sp", lambda e: e.dma_start(out=gkT[:], in_=gkT_v[:, :, t0:t0 + 128]), w=[gkTk])
                    S.op("pe", lambda e: e.matmul(pb[0][:, :], lhsT=lr[0:16, :], rhs=wa_t[0:16, :], start=True, stop=False), r=[lrk, "wa_t"], w=["pb0"])
                    S.op("pe", lambda e: e.matmul(pb[0][:, :], lhsT=ones_f[0:1, 0:128], rhs=ba_t[0:1, :], start=False, stop=True), r=["ones_f", "ba_t"], w=["pb0"])
                    sp, spk = sp_r.next()
                    S.op("act", lambda e: e.activation(out=sp[:], in_=pb[0][:, :], func=AF.Exp, scale=-1.0), r=["pb0"], w=[spk])
                    S.op("act", lambda e: e.activation(out=sp[:], in_=sp[:], func=AF.Ln, bias=1.0, scale=1.0), r=[spk], w=[spk])
                    for h in range(4):
                        S.op("pe", lambda e: e.matmul(pb[1][:, h * 128:(h + 1) * 128], lhsT=sp[:, h * 128:(h + 1) * 128], rhs=tri, start=True, stop=True), r=[spk, "tri4"], w=["pb1"])
                    eb, ebk = eb_r.next()
                    S.op("act", lambda e: e.activation(out=eb[:], in_=pb[1][:, :], func=AF.Exp, scale=-1.0 / 16), r=["pb1"], w=[ebk])
                    if wout:
                        enb, enbk = enb_r.next()
                        S.op("act", lambda e: e.activation(out=enb[:], in_=pb[1][:, :], func=AF.Exp, scale=1.0 / 16), r=["pb1"], w=[enbk])
                        qe, qek = qe_r.next(); ke, kek = ke_r.next()
                        S.op("dve", lambda e: e.scalar_tensor_tensor(out=qe[:], in0=gq[:], scalar=DKS, in1=eb[:].rearrange("p (h t) -> p h t", h=4), op0=ALU.mult, op1=ALU.mult), r=[gqk, ebk], w=[qek])
                        S.op("pool", lambda e: e.tensor_tensor(out=ke[:], in0=gkT[:], in1=enb[:].rearrange("p (h t) -> p h t", h=4), op=ALU.mult), r=[gkTk, enbk], w=[kek])
                    S.op("pe", lambda e: e.matmul(pb[0][:, :], lhsT=stri, rhs=sp[:], start=True, stop=True), r=[spk, "tri4"], w=["pb0"])
                    ee, eek = ee_r.next(); kend, kendk = kend_r.next()
                    S.op("act", lambda e: e.activation(out=ee[:], in_=pb[0][:, :], func=AF.Exp, scale=-1.0 / 16), r=["pb0"], w=[eek])
                    S.op("dve", lambda e: e.tensor_tensor(out=kend[:], in0=gk[:], in1=ee[:], op=ALU.mult), r=[gkk, eek], w=[kendk])
                    if wout:
                        for h in range(4):
                            S.op("pe", lambda e: e.matmul(pb[3][:, h * 128:(h + 1) * 128], lhsT=ke[:, h, :], rhs=qe[:, h, :], start=True, stop=True), r=[kek, qek], w=["pb3"])
                        attm, attmk = attm_r.next()
                        S.op("dve", lambda e: e.tensor_tensor(out=attm[:], in0=pb[3][:, :].rearrange("p (h t) -> p h t", h=4), in1=tri.unsqueeze(1).to_broadcast([128, 4, 128]), op=ALU.mult), r=["pb3", "tri4"], w=[attmk])
                        for ch in range(8):
                            h, dc = ch // 2, ch % 2
                            ob = pb[4 + ch // 4]; obk = PBK[4 + ch // 4]
                            S.op("pe", lambda e: e.matmul(ob[:, (ch % 4) * 128:(ch % 4 + 1) * 128], lhsT=gv[:, h * 256 + dc * 128:h * 256 + (dc + 1) * 128], rhs=attm[:, h, :], start=(ch % 4 == 0), stop=False), r=[gvk, attmk], w=[obk])
                    for ci, chunk in enumerate(order):
                        r0 = chunk * 64
                        if wout:
                            for ch in range(8):
                                h, dc = ch // 2, ch % 2
                                ob = pb[4 + ch // 4]; obk = PBK[4 + ch // 4]
                                c0 = (ch % 4) * 128 + r0
                                S.op("pe", lambda e: e.matmul(ob[:, c0:c0 + 64], lhsT=Sbf[:, h, dc * 128:(dc + 1) * 128], rhs=qe[:, h, r0:r0 + 64], start=False, stop=(ci == 1 and ch % 4 == 3)), r=["Sbf", qek], w=[obk])
                        for h in range(4):
                            ub = pb[2] if h < 2 else pb[6]
                            ubk = "pb2" if h < 2 else "pb6"
                            S.op("pe", lambda e: e.matmul(ub[:, (h % 2) * 256:(h % 2 + 1) * 256], lhsT=kend[r0:r0 + 64, h * 128:(h + 1) * 128], rhs=gv[r0:r0 + 64, h * 256:(h + 1) * 256], start=True, stop=True), r=[kendk, gvk], w=[ubk])
                        dcol = (r0 + 63) if di == 0 else r0
                        for h in range(4):
                            ub = pb[2] if h < 2 else pb[6]
                            ubk = "pb2" if h < 2 else "pb6"
                            S.op("dve", lambda e: e.scalar_tensor_tensor(out=S32[:, h, :], in0=S32[:, h, :], scalar=eb[:, h * 128 + dcol:h * 128 + dcol + 1], in1=ub[:, (h % 2) * 256:(h % 2 + 1) * 256], op0=ALU.mult, op1=ALU.add), r=["S32", ebk, ubk], w=["S32"])
                        S.op("act", lambda e: e.activation(out=Sbf[:], in_=S32[:], func=AF.Identity), r=["S32"], w=["Sbf"])
                    if wout:
                        osb, osbk = osb_r.next()
                        if di == 0:
                            for half in range(2):
                                S.op("dve", lambda e: e.tensor_copy(out=osb[:, half * 4:(half + 1) * 4, :], in_=pb[4 + half][:, :].rearrange("p (c t) -> p c t", c=4)), r=[PBK[4 + half]], w=[osbk])
                        else:
                            oa, oak = oa_r.next()
                            S.dma("sp", lambda e: e.dma_start(out=oa[:], in_=oT_v[:, :, t0:t0 + 128]), w=[oak])
                            for half in range(2):
                                S.op("dve", lambda e: e.tensor_tensor(out=osb[:, half * 4:(half + 1) * 4, :], in0=pb[4 + half][:, :].rearrange("p (c t) -> p c t", c=4), in1=oa[:, half * 4:(half + 1) * 4, :], op=ALU.add), r=[PBK[4 + half], oak], w=[osbk])
                        S.dma("act", lambda e: e.dma_start(out=oT_v[:, :, t0:t0 + 128], in_=osb[:]), r=[osbk])
                S.barrier()
                if os.environ.get('K_STOP') == '6':
                    S.finish(); return nc

        NKB = T_ALL // 128
        NQG = T_OWN // 512
        dv_v = S_dv.rearrange("(kb p) f -> p kb f", p=128)
        with ExitStack() as pd:
            kT_r = rot("kT", [128, T_ALL], BF16, 2, pd); qT_r = rot("qT", [128, T_OWN], BF16, 2, pd)
            va_r = rot("vaug", [128, NKB, 129], BF16, 2, pd)
            pT_r = rot("pT", [128, 512], BF16, 4, pd)
            rz = sb("rz", [128, 4], F32, pd)
            osb_r = rot("dosb", [128, 8 * 129], F32, 2, pd)
            t1_r = rot("dt1", [128, 128], F32, 2, pd)
            od_r = rot("od", [128, 4, 128], BF16, 2, pd); junk = sb("djunk", [128, 128], F32, pd)
            odT_r = rot("odT", [128, 512], BF16, 2, pd)
            for vt in va_r.tiles:
                S.op("pool", lambda e: e.memset(vt[:, :, 128:129], 1.0), w=["vaug#0", "vaug#1"])
            sps = [(pb[i], PBK[i]) for i in range(3)]

            def oreg(r):
                b = 3 + r // 3
                return pb[b][:, (r % 3) * 129:(r % 3) * 129 + 129], PBK[b]

            heads = {}

            def load_head(h):
                if h in heads or h >= 8:
                    return
                kT, kTk = kT_r.next(); qT, qTk = qT_r.next(); va, vak = va_r.next()
                S.dma("sp", lambda e: e.dma_start(out=kT[:], in_=S_kT[h * 128:(h + 1) * 128, :]), w=[kTk])
                S.dma("sp", lambda e: e.dma_start(out=qT[:], in_=S_qT[h * 128:(h + 1) * 128, :]), w=[qTk])
                S.dma("sp", lambda e: e.dma_start(out=va[:, :, 0:128], in_=dv_v[:, :, h * 128:(h + 1) * 128]), w=[vak])
                heads[h] = (kT, kTk, qT, qTk, va, vak)

            o4_r = rot("do4", [128, 4, 128], F32, 2, pd)
            ssq = sb("dssq", [128, 4], F32, pd)

            def epilogue(h, qg):
                q0 = qg * 512
                osb, osbk = osb_r.next()
                for b in range(3):
                    ncol = 387 if b < 2 else 258
                    S.op("dve", lambda e: e.tensor_copy(out=osb[:, b * 387:b * 387 + ncol], in_=pb[3 + b][:, 0:ncol]), r=[PBK[3 + b]], w=[osbk])
                odT, odTk = odT_r.next()
                o4, o4k = o4_r.next()
                S.op("dve", lambda e: e.memset(ssq[:], 0.0), w=["dssq"])
                for qi in range(4):
                    o0 = osb[:, qi * 129:(qi + 1) * 129]; o1 = osb[:, (4 + qi) * 129:(5 + qi) * 129]
                    S.op("dve", lambda e: e.reciprocal(out=rz[:, 0:1], in_=o0[:, 128:129]), r=[osbk], w=["rz"])
                    S.op("dve", lambda e: e.reciprocal(out=rz[:, 1:2], in_=o1[:, 128:129]), r=[osbk], w=["rz"])
                    S.op("dve", lambda e: e.tensor_tensor(out=rz[:, 2:3], in0=rz[:, 1:2], in1=lam[:], op=ALU.mult), r=["rz", "lam"], w=["rz"])
                    t1, t1k = t1_r.next()
                    S.op("dve", lambda e: e.tensor_scalar(out=t1[:], in0=o1[:, 0:128], scalar1=rz[:, 2:3], scalar2=None, op0=ALU.mult), r=[osbk, "rz"], w=[t1k])
                    S.op("dve", lambda e: e.scalar_tensor_tensor(out=o4[:, qi, :], in0=o0[:, 0:128], scalar=rz[:, 0:1], in1=t1[:], op0=ALU.mult, op1=ALU.subtract), r=[osbk, "rz", t1k], w=[o4k])
                    S.op("dve", lambda e: e.scalar_tensor_tensor(out=junk[:], in0=o4[:, qi, :], scalar=1.0, in1=o4[:, qi, :], op0=ALU.mult, op1=ALU.mult, accum_out=ssq[:, qi:qi + 1]), r=[o4k], w=["djunk", "dssq"])
                S.op("dve", lambda e: e.tensor_scalar(out=ssq[:], in0=ssq[:], scalar1=1.0 / 128, scalar2=EPS, op0=ALU.mult, op1=ALU.add), r=["dssq"], w=["dssq"])
                S.op("act", lambda e: e.activation(out=ssq[:], in_=ssq[:], func=AF.Sqrt), r=["dssq"], w=["dssq"])
                S.op("dve", lambda e: e.reciprocal(out=ssq[:], in_=ssq[:]), r=["dssq"], w=["dssq"])
                od, odk = od_r.next()
                for qi in range(4):
                    S.op("dve", lambda e: e.scalar_tensor_tensor(out=od[:, qi, :], in0=o4[:, qi, :], scalar=ssq[:, qi:qi + 1], in1=dngr[:, h * 128:(h + 1) * 128], op0=ALU.mult, op1=ALU.mult), r=[o4k, "dssq", "dngr"], w=[odk])

                def part_b():
                    for qi in range(4):
                        S.op("pe", lambda e: e.transpose(ptb[:, qi * 128:(qi + 1) * 128], od[:, qi, :], ident_bf[:]), r=[odk, "ident_bf"], w=["ptb"])
                    S.op("dve", lambda e: e.tensor_copy(out=odT[:], in_=ptb[:, 0:512]), r=["ptb"], w=[odTk])
                    S.dma("sp", lambda e: e.dma_start(out=S_odT[h * 128:(h + 1) * 128, q0:q0 + 512], in_=odT[:]), r=[odTk])
                return part_b

            steps = [(h, qg, kb, m) for h in range(8) for qg in range(NQG) for kb in range(NKB) for m in range(2)]
            LA = 2
            info = {}

            def emit_qk(si):
                h, qg, kb, m = steps[si]
                if h not in heads:
                    load_head(h)
                kT, kTk, qT, qTk, va, vak = heads[h]
                sp_, spk_ = sps[si % 3]
                q0 = qg * 512
                S.op("pe", lambda e: e.matmul(sp_[:, :], lhsT=kT[m * 64:(m + 1) * 64, kb * 128:(kb + 1) * 128], rhs=qT[m * 64:(m + 1) * 64, q0:q0 + 512], start=True, stop=True), r=[kTk, qTk], w=[spk_])
                pT, pTk = pT_r.next()
                S.op("act", lambda e: e.activation(out=pT[:], in_=sp_[:, :], func=AF.Exp, scale=0.125), r=[spk_], w=[pTk])
                info[si] = (pT, pTk)

            def emit_av(si):
                h, qg, kb, m = steps[si]
                if qg == 0 and kb == 0 and m == 0:
                    load_head(h + 1)
                kT, kTk, qT, qTk, va, vak = heads[h]
                pT, pTk = info.pop(si)
                for qi in range(4):
                    r = m * 4 + qi
                    oap, obk = oreg(r)
                    S.op("pe", lambda e: e.matmul(oap, lhsT=pT[:, qi * 128:(qi + 1) * 128], rhs=va[:, kb, :], start=(kb == 0 and r % 3 == 0), stop=(kb == NKB - 1 and (r % 3 == 2 or r == 7))), r=[pTk, vak], w=[obk])
                if kb == NKB - 1 and m == 1:
                    deferred.append((si + 24, epilogue(h, qg)))

            deferred = []
            for si in range(len(steps) + LA):
                if si < len(steps):
                    emit_qk(si)
                if si - LA >= 0:
                    emit_av(si - LA)
                while deferred and deferred[0][0] <= si:
                    deferred.pop(0)[1]()
            for _, fn in deferred:
                fn()
            S.barrier()

        NTE = T_OWN // 256
        cv = lambda ap: ap.rearrange("(c p) t -> p c t", p=128)
        grT_v = cv(S_grT); odT_v = cv(S_odT); saT_v = cv(S_saT); sbT_v = cv(S_sbT)
        with ExitStack() as pe_:
            Wgp = sb("Wgp", [128, 8, 1024], BF16, pe_); Wdp = sb("Wdp", [128, 8, 1024], BF16, pe_); Wo = sb("Wo", [128, 8, 1024], BF16, pe_)
            load_w(Wgp, "Wgp", 0, 1024, w_gp); load_w(Wdp, "Wdp", 0, 1024, w_dp); load_w(Wo, "Wo", 0, 1024, w_out)
            o_r = rot("eo", [128, 8, 256], F32, 2, pe_); gr_r = rot("egr", [128, 8, 256], BF16, 2, pe_)
            sq_r = rot("esq", [128, 8, 256], BF16, 1, pe_); rsd = sb("ersd", [128, 4, 256], F32, pe_)
            og_r = rot("eog", [128, 8, 256], BF16, 1, pe_); od_r2 = rot("eod", [128, 8, 256], BF16, 2, pe_)
            sa_r = rot("esa", [128, 8, 256], BF16, 2, pe_); sb_r = rot("esb", [128, 8, 256], BF16, 2, pe_)
            mg_r = rot("emg", [128, 8, 256], BF16, 1, pe_); m1_r = rot("em1", [128, 256], F32, 2, pe_); m2_r = rot("em2", [128, 256], F32, 2, pe_)
            x_r2 = rot("ex", [128, 8, 256], F32, 2, pe_); x1tm_r = rot("ex1tm", [128, 1024], F32, 2, pe_)
            for te in range(NTE):
                t0 = te * 256
                o_, ok_ = o_r.next(); gr, grk = gr_r.next(); odt, odtk = od_r2.next(); sa, sak = sa_r.next(); sbb, sbk = sb_r.next(); xt, xk = x_r2.next()
                S.dma("sp", lambda e: e.dma_start(out=o_[:], in_=oT_v[:, :, t0:t0 + 256]), w=[ok_])
                S.dma("act", lambda e: e.dma_start(out=gr[:], in_=grT_v[:, :, t0:t0 + 256]), w=[grk])
                S.dma("sp", lambda e: e.dma_start(out=odt[:], in_=odT_v[:, :, t0:t0 + 256]), w=[odtk])
                S.dma("act", lambda e: e.dma_start(out=sa[:], in_=saT_v[:, :, t0:t0 + 256]), w=[sak])
                S.dma("sp", lambda e: e.dma_start(out=sbb[:], in_=sbT_v[:, :, t0:t0 + 256]), w=[sbk])
                S.dma("act", lambda e: e.dma_start(out=xt[:], in_=xT_v[:, :, t0:t0 + 256]), w=[xk])
                sq, sqk = sq_r.next()
                S.op("act", lambda e: e.activation(out=sq[:], in_=o_[:], func=AF.Square), r=[ok_], w=[sqk])
                for h in range(4):
                    bk = pb[4 + h // 2]; bkk = PBK[4 + h // 2]
                    for dc in range(2):
                        S.op("pe", lambda e: e.matmul(bk[:, (h % 2) * 256:(h % 2 + 1) * 256], lhsT=ones_bf[:], rhs=sq[:, h * 2 + dc, :], start=(dc == 0), stop=(dc == 1)), r=["ones_bf", sqk], w=[bkk])
                for half in range(2):
                    S.op("act", lambda e: e.activation(out=rsd[:, half * 2:(half + 1) * 2, :], in_=pb[4 + half][:, :].rearrange("p (h t) -> p h t", h=2), func=AF.Sqrt, bias=EPS, scale=1.0 / 256), r=[PBK[4 + half]], w=["ersd"])
                S.op("dve", lambda e: e.reciprocal(out=rsd[:], in_=rsd[:]), r=["ersd"], w=["ersd"])
                for ch in range(8):
                    S.op("dve", lambda e: e.scalar_tensor_tensor(out=o_[:, ch, :], in0=o_[:, ch, :], scalar=gvec[:, 24 + ch:25 + ch], in1=rsd[:, ch // 2, :], op0=ALU.mult, op1=ALU.mult), r=[ok_, "gvec", "ersd"], w=[ok_])
                og, ogk = og_r.next()
                S.op("pool", lambda e: e.tensor_tensor(out=og[:], in0=o_[:], in1=gr[:], op=ALU.mult), r=[ok_, grk], w=[ogk])
                mg, mgk = mg_r.next()
                for oc in range(8):
                    ps, psk = nps()
                    for kc in range(8):
                        S.op("pe", lambda e: e.matmul(ps[:, 0:256], lhsT=Wgp[:, kc, oc * 128:(oc + 1) * 128], rhs=og[:, kc, :], start=(kc == 0), stop=False), r=["Wgp", ogk], w=[psk])
                    for kc in range(8):
                        S.op("pe", lambda e: e.matmul(ps[:, 256:512], lhsT=Wdp[:, kc, oc * 128:(oc + 1) * 128], rhs=odt[:, kc, :], start=False, stop=(kc == 7)), r=["Wdp", odtk], w=[psk])
                    m1, m1k = m1_r.next(); m2, m2k = m2_r.next()
                    S.op("dve", lambda e: e.tensor_tensor(out=m1[:], in0=ps[:, 0:256], in1=sa[:, oc, :], op=ALU.mult), r=[psk, sak], w=[m1k])
                    S.op("dve", lambda e: e.tensor_tensor(out=m2[:], in0=ps[:, 256:512], in1=sbb[:, oc, :], op=ALU.mult), r=[psk, sbk], w=[m2k])
                    S.op("pool", lambda e: e.tensor_tensor(out=mg[:, oc, :], in0=m1[:], in1=m2[:], op=ALU.add), r=[m1k, m2k], w=[mgk])
                for oc in range(8):
                    ps, psk = nps()
                    for kc in range(8):
                        S.op("pe", lambda e: e.matmul(ps[:, 0:256], lhsT=Wo[:, kc, oc * 128:(oc + 1) * 128], rhs=mg[:, kc, :], start=(kc == 0), stop=(kc == 7)), r=["Wo", mgk], w=[psk])
                    S.op("dve", lambda e: e.scalar_tensor_tensor(out=xt[:, oc, :], in0=ps[:, 0:256], scalar=gt1(oc), in1=xt[:, oc, :], op0=ALU.mult, op1=ALU.add), r=[psk, "modT", xk], w=[xk])
                for sub in range(2):
                    x1tm, x1tmk = x1tm_r.next()
                    for half in range(2):
                        ps, psk = nps("act")
                        for k4 in range(4):
                            kc = half * 4 + k4
                            S.op("pe", lambda e: e.matmul(ps[:, k4 * 128:(k4 + 1) * 128], lhsT=xt[:, kc, sub * 128:(sub + 1) * 128], rhs=ident[:], start=True, stop=True), r=[xk, "ident"], w=[psk])
                        S.op("act", lambda e: e.activation(out=x1tm[:, half * 512:(half + 1) * 512], in_=ps[:, :], func=AF.Identity), r=[psk], w=[x1tmk])
                    store(S_x1[t0 + sub * 128:t0 + (sub + 1) * 128, :], x1tm[:], x1tmk)
            S.barrier()
            if os.environ.get('K_STOP') == '8':
                S.finish(); return nc

        NTF = T_OWN // 128
        with ExitStack() as pf:
            Wq = sb("Wq", [128, 8, 2048], BF16, pf); skb = sb("skb", [128, 2048], BF16, pf)
            load_w(Wq, "Wq", 0, 2048, w_pq)
            load_w(skb, "skb", 0, 2048, skT, kcs=1)
            rowsb = sb("rowsb", [128, 5120], F32, pf)
            S.dma("sp", lambda e: e.dma_start(out=rowsb[:], in_=S_rows.partition_broadcast(128)), w=["rowsb"])
            a2row, sh2row, gt2row, afrow, fshrow = (rowsb[:, i * 1024:(i + 1) * 1024] for i in range(5))
            pci = sb("pci", [128, 384], I32, pf); pcs = sb("pcs", [128, 8], I32, pf); pcf = sb("pcf", [128, 64], F32, pf)
            S.dma("sp", lambda e: e.dma_start(out=pci[:, 0:128], in_=pconst_d[:, 0:128]), w=["pci"])
            S.dma("sp", lambda e: e.dma_start(out=pci[:, 128:384], in_=pconst_d[:, 2048:2304]), w=["pci"])
            S.dma("sp", lambda e: e.dma_start(out=pcs[:], in_=pcs_d), w=["pcs"])
            S.dma("sp", lambda e: e.dma_start(out=pcf[:], in_=pconstf_d), w=["pcf"])
            niota = pci[:, 0:128].unsqueeze(1).to_broadcast([128, 16, 128]); ciota = pci[:, 128:384].unsqueeze(1).to_broadcast([128, 8, 256])
            cst = lambda j, n: pcs[:, j:j + 1].to_broadcast([128, n])
            x1_r = rot("fx1", [128, 1024], F32, 2, pf)
            h2 = sb("fh2", [128, 1024], F32, pf); h2b = sb("fh2b", [128, 1024], BF16, pf); h2T = sb("fh2T", [128, 8, 128], BF16, pf)
            qpT = sb("fqpT", [128, 16, 128], BF16, pf)
            sc = sb("fsc", [128, 2048], F32, pf); sc2 = sb("fsc2", [128, 2048], F32, pf)
            cand = sb("fcand", [128, 2048], F32, pf); tmpb = sb("ftmp", [128, 2048], F32, pf)
            v16 = sb("fv16", [128, 256], F32, pf); i16i = sb("fi16i", [128, 256], I32, pf); i16f = sb("fi16f", [128, 256], F32, pf)
            f16 = sb("ff16", [128, 128], F32, pf); cpi = sb("fcpi", [128, 128], I32, pf); cpj = sb("fcpj", [128, 128], I32, pf)
            aif = sb("faif", [128, 128], F32, pf); bjf = sb("fbjf", [128, 128], F32, pf)
            iA = sb("fiA", [128, 128], F32, pf); iB = sb("fiB", [128, 128], F32, pf)
            eif = sb("feif", [128, 128], F32, pf); eii = sb("feii", [128, 128], I32, pf)
            gm = sb("fgm", [128, 8], F32, pf); ge = sb("fge", [128, 128], F32, pf); gs = sb("fgs", [128, 8], F32, pf)
            gw = sb("fgw", [128, 128], F32, pf); aa = sb("faa", [128, 128], F32, pf); ww = sb("fww", [128, 128], F32, pf)
            st4 = sb("fst4", [128, 4], F32, pf)
            g_r = rot("fgat", [128, 1024], F32, 8, pf)
            tv_r = rot("ftv", [128, 1024], BF16, 3, pf)
            junk = sb("fjunk", [128, 1024], F32, pf); acc = sb("facc", [128, 1024], F32, pf)
            x2 = sb("fx2", [128, 1024], F32, pf); ot_r = rot("fout", [128, 1024], F32, 2, pf)
            v16v = v16[:].rearrange("p (h q k) -> p h q k", h=8, q=2)
            i16v = i16f[:].rearrange("p (h q k) -> p h q k", h=8, q=2)
            B4 = [128, 8, 16, 16]
            r4 = lambda t: t[:].rearrange("p (h a b) -> p h a b", h=8, a=16)
            r3 = lambda t: t[:].rearrange("p (h k) -> p h k", h=8)
            for tf in range(NTF):
                t0 = tf * 128
                x1, x1k = x1_r.next()
                S.dma("sp", lambda e: e.dma_start(out=x1[:], in_=S_x1[t0:t0 + 128, :]), w=[x1k])
                S.op("act", lambda e: e.activation(out=junk[:], in_=x1[:], func=AF.Square, accum_out=st4[:, 0:1]), r=[x1k], w=["fjunk", "fst4"])
                S.op("act", lambda e: e.activation(out=st4[:, 1:2], in_=st4[:, 0:1], func=AF.Sqrt, bias=EPS, scale=1.0 / D), r=["fst4"], w=["fst4"])
                S.op("dve", lambda e: e.reciprocal(out=st4[:, 1:2], in_=st4[:, 1:2]), r=["fst4"], w=["fst4"])
                S.op("dve", lambda e: e.scalar_tensor_tensor(out=h2[:], in0=x1[:], scalar=st4[:, 1:2], in1=a2row, op0=ALU.mult, op1=ALU.mult), r=[x1k, "fst4", "rowsb"], w=["fh2"])
                S.op("dve", lambda e: e.tensor_tensor(out=h2[:], in0=h2[:], in1=sh2row, op=ALU.add), r=["fh2", "rowsb"], w=["fh2"])
                S.op("act", lambda e: e.activation(out=h2b[:], in_=h2[:], func=AF.Identity), r=["fh2"], w=["fh2b"])
                for kc in range(8):
                    S.op("pe", lambda e: e.transpose(ptb[:, kc * 128:(kc + 1) * 128], h2b[:, kc * 128:(kc + 1) * 128], ident_bf[:]), r=["fh2b", "ident_bf"], w=["ptb"])
                S.op("dve", lambda e: e.tensor_copy(out=h2T[:].rearrange("p k t -> p (k t)"), in_=ptb[:, :]), r=["ptb"], w=["fh2T"])
                for g4 in range(4):
                    ps, psk = nps("act")
                    for q in range(4):
                        hp = g4 * 4 + q
                        for kc in range(8):
                            S.op("pe", lambda e: e.matmul(ps[:, q * 128:(q + 1) * 128], lhsT=Wq[:, kc, hp * 128:(hp + 1) * 128], rhs=h2T[:, kc, :], start=(kc == 0), stop=(kc == 7)), r=["Wq", "fh2T"], w=[psk])
                    S.op("act", lambda e: e.activation(out=qpT[:, g4 * 4:(g4 + 1) * 4, :], in_=ps[:, :].rearrange("p (q t) -> p q t", q=4), func=AF.Identity), r=[psk], w=["fqpT"])
                for g4 in range(4):
                    ps, psk = nps()
                    for q in range(4):
                        hp = g4 * 4 + q
                        S.op("pe", lambda e: e.matmul(ps[:, q * 128:(q + 1) * 128], lhsT=qpT[:, hp, :], rhs=skb[:, hp * 128:(hp + 1) * 128], start=True, stop=True), r=["fqpT", "skb"], w=[psk])
                    S.op("dve", lambda e: e.tensor_copy(out=sc[:, g4 * 512:(g4 + 1) * 512], in_=ps[:, :]), r=[psk], w=["fsc"])
                sci = sc[:].bitcast(I32)
                S.op("dve", lambda e: e.tensor_tensor(out=sci, in0=sci, in1=cst(0, 2048), op=ALU.bitwise_and), r=["fsc", "pcs"], w=["fsc"])
                S.op("dve", lambda e: e.tensor_tensor(out=sci.rearrange("p (g n) -> p g n", g=16), in0=sci.rearrange("p (g n) -> p g n", g=16), in1=niota, op=ALU.bitwise_or), r=["fsc", "pci"], w=["fsc"])
                for hp in range(16):
                    sl = slice(hp * 128, (hp + 1) * 128)
                    S.op("dve", lambda e: e.max(out=v16[:, hp * 16:hp * 16 + 8], in_=sc[:, sl]), r=["fsc"], w=["fv16"])
                    S.op("dve", lambda e: e.match_replace(out=sc2[:, sl], in_to_replace=v16[:, hp * 16:hp * 16 + 8], in_values=sc[:, sl], imm_value=-1e30), r=["fsc", "fv16"], w=["fsc2"])
                    S.op("dve", lambda e: e.max(out=v16[:, hp * 16 + 8:hp * 16 + 16], in_=sc2[:, sl]), r=["fsc2"], w=["fv16"])
                S.op("dve", lambda e: e.tensor_tensor(out=i16i[:], in0=v16[:].bitcast(I32), in1=cst(2, 256), op=ALU.bitwise_and), r=["fv16", "pcs"], w=["fi16i"])
                S.op("dve", lambda e: e.tensor_copy(out=i16f[:], in_=i16i[:]), r=["fi16i"], w=["fi16f"])
                S.op("dve", lambda e: e.tensor_tensor(out=r4(cand), in0=v16v[:, :, 0, :].unsqueeze(3).to_broadcast(B4), in1=v16v[:, :, 1, :].unsqueeze(2).to_broadcast(B4), op=ALU.add), r=["fv16"], w=["fcand"])
                cai = cand[:].bitcast(I32)
                S.op("dve", lambda e: e.tensor_tensor(out=cai, in0=cai, in1=cst(1, 2048), op=ALU.bitwise_and), r=["fcand", "pcs"], w=["fcand"])
                S.op("dve", lambda e: e.tensor_tensor(out=cai.rearrange("p (g n) -> p g n", g=8), in0=cai.rearrange("p (g n) -> p g n", g=8), in1=ciota, op=ALU.bitwise_or), r=["fcand", "pci"], w=["fcand"])
                for h in range(8):
                    sl = slice(h * 256, (h + 1) * 256)
                    S.op("dve", lambda e: e.max(out=f16[:, h * 16:h * 16 + 8], in_=cand[:, sl]), r=["fcand"], w=["ff16"])
                    S.op("dve", lambda e: e.match_replace(out=tmpb[:, sl], in_to_replace=f16[:, h * 16:h * 16 + 8], in_values=cand[:, sl], imm_value=-1e30), r=["fcand", "ff16"], w=["ftmp"])
                    S.op("dve", lambda e: e.max(out=f16[:, h * 16 + 8:h * 16 + 16], in_=tmpb[:, sl]), r=["ftmp"], w=["ff16"])
                S.op("dve", lambda e: e.tensor_tensor(out=cpi[:], in0=f16[:].bitcast(I32), in1=cst(3, 128), op=ALU.bitwise_and), r=["ff16", "pcs"], w=["fcpi"])
                S.op("dve", lambda e: e.tensor_tensor(out=cpj[:], in0=cpi[:], in1=cst(5, 128), op=ALU.bitwise_and), r=["fcpi", "pcs"], w=["fcpj"])
                S.op("dve", lambda e: e.tensor_tensor(out=cpi[:], in0=cpi[:], in1=cst(4, 128), op=ALU.arith_shift_right), r=["fcpi", "pcs"], w=["fcpi"])
                S.op("dve", lambda e: e.tensor_copy(out=aif[:], in_=cpi[:]), r=["fcpi"], w=["faif"])
                S.op("dve", lambda e: e.tensor_copy(out=bjf[:], in_=cpj[:]), r=["fcpj"], w=["fbjf"])
                io4 = pcf[:, 0:16].unsqueeze(1).unsqueeze(1).to_broadcast(B4)
                for which, (sel, dst_) in enumerate(((aif, iA), (bjf, iB))):
                    S.op("dve", lambda e: e.tensor_tensor(out=r4(sc2), in0=io4, in1=r3(sel).unsqueeze(3).to_broadcast(B4), op=ALU.is_equal), r=["pcf", "faif", "fbjf"], w=["fsc2"])
                    S.op("dve", lambda e: e.tensor_tensor(out=r4(sc2), in0=r4(sc2), in1=i16v[:, :, which, :].unsqueeze(2).to_broadcast(B4), op=ALU.mult), r=["fsc2", "fi16f"], w=["fsc2"])
                    S.op("dve", lambda e: e.tensor_reduce(out=dst_[:], in_=sc2[:].rearrange("p (k i) -> p k i", i=16), axis=AX.X, op=ALU.add), r=["fsc2"], w=["fiA", "fiB"])
                S.op("dve", lambda e: e.scalar_tensor_tensor(out=eif[:], in0=iA[:], scalar=128.0, in1=iB[:], op0=ALU.mult, op1=ALU.add), r=["fiA", "fiB"], w=["feif"])
                S.op("dve", lambda e: e.tensor_copy(out=eii[:], in_=eif[:]), r=["feif"], w=["feii"])
                S.op("dve", lambda e: e.tensor_reduce(out=gm[:], in_=r3(f16), axis=AX.X, op=ALU.max), r=["ff16"], w=["fgm"])
                S.op("dve", lambda e: e.tensor_tensor(out=r3(ge), in0=r3(f16), in1=gm[:].unsqueeze(2).to_broadcast([128, 8, 16]), op=ALU.subtract), r=["ff16", "fgm"], w=["fge"])
                S.op("act", lambda e: e.activation(out=ge[:], in_=ge[:], func=AF.Exp), r=["fge"], w=["fge"])
                S.op("dve", lambda e: e.tensor_reduce(out=gs[:], in_=r3(ge), axis=AX.X, op=ALU.add), r=["fge"], w=["fgs"])
                S.op("dve", lambda e: e.reciprocal(out=gs[:], in_=gs[:]), r=["fgs"], w=["fgs"])
                S.op("dve", lambda e: e.tensor_tensor(out=r3(gw), in0=r3(ge), in1=gs[:].unsqueeze(2).to_broadcast([128, 8, 16]), op=ALU.mult), r=["fge", "fgs"], w=["fgw"])
                S.op("dve", lambda e: e.memset(aa[:], 0.0), w=["faa"])
                for k in range(128):
                    gt_, gtk = g_r.next()
                    S.dma("pool", lambda e: e.indirect_dma_start(out=gt_[:], out_offset=None, in_=peer_u, in_offset=bass.IndirectOffsetOnAxis(ap=eii[:, k:k + 1], axis=0)), r=["feii"], w=[gtk])
                    S.op("dve", lambda e: e.scalar_tensor_tensor(out=junk[:], in0=gt_[:], scalar=1.0, in1=h2[:], op0=ALU.mult, op1=ALU.mult, accum_out=aa[:, k:k + 1]), r=[gtk, "fh2"], w=["fjunk", "faa"])
                S.op("act", lambda e: e.activation(out=ww[:], in_=aa[:], func=AF.Gelu), r=["faa"], w=["fww"])
                S.op("dve", lambda e: e.tensor_tensor(out=ww[:], in0=ww[:], in1=gw[:], op=ALU.mult), r=["fww", "fgw"], w=["fww"])
                accb = [nps("act"), nps("act")]
                for k in range(128):
                    gt_, gtk = g_r.next()
                    S.dma("pool", lambda e: e.indirect_dma_start(out=gt_[:], out_offset=None, in_=peer_v, in_offset=bass.IndirectOffsetOnAxis(ap=eii[:, k:k + 1], axis=0)), r=["feii"], w=[gtk])
                    tv, tvk = tv_r.next()
                    S.op("act", lambda e: e.activation(out=tv[:], in_=gt_[:], func=AF.Identity, scale=ww[:, k:k + 1]), r=[gtk, "fww"], w=[tvk])
                    for half in range(2):
                        S.op("pe", lambda e: e.matmul(accb[half][0][:, :], lhsT=ident_bf[:], rhs=tv[:, half * 512:(half + 1) * 512], start=(k == 0), stop=(k == 127)), r=[tvk, "ident_bf"], w=[accb[half][1]])
                for half in range(2):
                    S.op("act", lambda e: e.activation(out=acc[:, half * 512:(half + 1) * 512], in_=accb[half][0][:, :], func=AF.Identity), r=[accb[half][1]], w=["facc"])
                S.op("dve", lambda e: e.tensor_tensor(out=acc[:], in0=acc[:], in1=gt2row, op=ALU.mult), r=["facc", "rowsb"], w=["facc"])
                S.op("dve", lambda e: e.tensor_tensor(out=x2[:], in0=acc[:], in1=x1[:], op=ALU.add), r=["facc", x1k], w=["fx2"])
                S.op("act", lambda e: e.activation(out=junk[:], in_=x2[:], func=AF.Square, accum_out=st4[:, 2:3]), r=["fx2"], w=["fjunk", "fst4"])
                S.op("act", lambda e: e.activation(out=st4[:, 3:4], in_=st4[:, 2:3], func=AF.Sqrt, bias=EPS, scale=1.0 / D), r=["fst4"], w=["fst4"])
                S.op("dve", lambda e: e.reciprocal(out=st4[:, 3:4], in_=st4[:, 3:4]), r=["fst4"], w=["fst4"])
                ot, otk = ot_r.next()
                S.op("dve", lambda e: e.scalar_tensor_tensor(out=ot[:], in0=x2[:], scalar=st4[:, 3:4], in1=afrow, op0=ALU.mult, op1=ALU.mult), r=["fx2", "fst4", "rowsb"], w=[otk])
                S.op("dve", lambda e: e.tensor_tensor(out=ot[:], in0=ot[:], in1=fshrow, op=ALU.add), r=[otk, "rowsb"], w=[otk])
                S.dma("sp", lambda e: e.dma_start(out=out[t0:t0 + 128, :], in_=ot[:]), r=[otk], is_out=True)
            S.barrier()

        S.finish()
    return nc


def PHASES_AFTER_B(nc, S, st, L):
    pass


def _consts():
    p = np.arange(128)
    ident = np.eye(128, dtype=np.float32)
    j = p[:, None]; c = p[None, :]
    same = (j // 64) == (c // 64)
    triA = (same & (j <= c)).astype(np.float32)
    triB = (same & (j >= c)).astype(np.float32)
    striA = (same & (j > c)).astype(np.float32)
    striB = (same & (j < c)).astype(np.float32)
    tri4 = np.concatenate([triA, triB, striA, striB], axis=1)
    f = (p % 32).astype(np.float64)
    inv = (10000.0 ** (-(2.0 * f) / 64.0)).astype(np.float32)
    sign = np.where((p % 64) < 32, -1.0, 1.0).astype(np.float32)
    ropec = np.stack([inv, sign], axis=1).astype(np.float32)
    return ident, tri4, ropec


def make_in_maps(inputs, T_OWN=None):
    x = np.asarray(inputs["x"]); B, S_, _ = x.shape
    T_OWN = S_ // 2
    ident, tri4, ropec = _consts()
    w_in = np.asarray(inputs["w_in"])[0]
    cols = lambda a, b: w_in[:, a:b]
    perm = np.arange(1024).reshape(16, 64)
    perm = np.concatenate([perm[:, 32:], perm[:, :32]], axis=1).reshape(-1)
    dq = cols(3104, 4128); dk = cols(4128, 5152)
    pconst = np.zeros((128, 4096), np.int32)
    pconst[:, 0:2048] = (np.arange(2048) % 128)[None, :]
    pconst[:, 2048:4096] = (np.arange(2048) % 256)[None, :]
    pconstf = np.zeros((128, 64), np.float32)
    pconstf[:, 0:16] = np.arange(16, dtype=np.float32)[None, :]
    pcs = np.tile(np.array([-128, -256, 127, 255, 4, 15, 0, 0], np.int32)[None, :], (128, 1))
    common = {
        "w_ada": np.ascontiguousarray(inputs["w_ada"][0]),
        "w_fada": np.ascontiguousarray(inputs["w_final_ada"]),
        "lqk": np.concatenate([inputs["diff_lq1"][0], inputs["diff_lk1"][0], inputs["diff_lq2"][0], inputs["diff_lk2"][0]])[None, :].astype(np.float32),
        "dng": np.ascontiguousarray(inputs["diff_norm_g"]),
        "w_gp": np.ascontiguousarray(inputs["w_gla_proj"][0]), "w_dp": np.ascontiguousarray(inputs["w_diff_proj"][0]),
        "w_out": np.ascontiguousarray(inputs["w_out"][0]), "w_pq": np.ascontiguousarray(inputs["peer_wq"][0]),
        "skT": np.ascontiguousarray(np.transpose(inputs["peer_subkeys"][0].reshape(16, 128, 128), (2, 0, 1)).reshape(128, 2048)),
        "peer_u": np.ascontiguousarray(inputs["peer_u"][0]), "peer_v": np.ascontiguousarray(inputs["peer_v"][0]),
        "ident": ident, "tri4": tri4, "ropec": ropec, "pconst": pconst, "pconstf": pconstf, "pcs": pcs,
        "gvecT": np.ascontiguousarray(np.concatenate([inputs["norm1_g"][0].reshape(8, 128).T, inputs["norm2_g"][0].reshape(8, 128).T,
                                                       inputs["normf_g"].reshape(8, 128).T, inputs["gla_norm_g"][0].reshape(8, 128).T], axis=1)),
        "badaT": np.ascontiguousarray(np.concatenate([inputs["b_ada"][0].reshape(48, 128).T, inputs["b_final_ada"].reshape(16, 128).T], axis=1)),
    }
    w_all_hf = []
    for hf in range(2):
        lrA = cols(3072, 3088) if hf == 0 else cols(3088, 3104)
        lrB = cols(3088, 3104) if hf == 0 else cols(3072, 3088)
        w_all_hf.append(np.ascontiguousarray(np.concatenate(
            [cols(0, 3072), lrA, lrB, dq, dq[:, perm], dk, dk[:, perm], cols(5152, 6176), cols(6176, 8224)], axis=1)))
    fw = (inputs["gla_wa_fw"][0], inputs["gla_ba_fw"]); bw = (inputs["gla_wa_bw"][0], inputs["gla_ba_bw"])
    in_maps = []
    for b in range(B):
        for hf in range(2):
            xb = x[b]; pb_ = np.asarray(inputs["positions"])[b]
            if hf == 1:
                xb = xb[::-1]; pb_ = pb_[::-1]
            A, Bd = (fw, bw) if hf == 0 else (bw, fw)
            m = dict(common)
            m.update({
                "xT": np.ascontiguousarray(xb.T), "pos": np.ascontiguousarray(pb_[None, :].astype(np.int32)),
                "cT": np.ascontiguousarray(np.asarray(inputs["c"])[b].reshape(8, 128).T),
                "w_all": w_all_hf[hf],
                "wa_A": np.ascontiguousarray(A[0]), "ba_A": np.ascontiguousarray(A[1].reshape(1, 512)),
                "wa_B": np.ascontiguousarray(Bd[0]), "ba_B": np.ascontiguousarray(Bd[1].reshape(1, 512)),
            })
            in_maps.append(m)
    return in_maps, B, T_OWN


def kernel(**inputs):
    inputs = {k: np.asarray(v) for k, v in inputs.items()}
    in_maps, B, T_OWN = make_in_maps(inputs)
    nc = build(T_OWN)
    res = run_bass_kernel_spmd(nc, in_maps, core_ids=list(range(len(in_maps))))
    S_ = 2 * T_OWN
    out = np.zeros((B, S_, D), np.float32)
    for b in range(B):
        for hf in range(2):
            o = res.results[b * 2 + hf]["out"]
            if hf == 0:
                out[b, :T_OWN] = o
            else:
                out[b, T_OWN:] = o[::-1]
    return out
```

```python
import math, os
from contextlib import ExitStack
import numpy as np
import concourse.bass as bass
import concourse.mybir as mybir
from concourse.bass_utils import run_bass_kernel_spmd

F32 = mybir.dt.float32
BF16 = mybir.dt.bfloat16
I32 = mybir.dt.int32
AF = mybir.ActivationFunctionType
ALU = mybir.AluOpType
AX = mybir.AxisListType

D = 1024
NW1 = 3104
NW2 = 5120
NW3 = 2048
EPS = 1e-6
TWO_PI = 2.0 * math.pi


class Sched:
    NDSEM = 6
    NDQ = {"sp": 6, "act": 6, "pool": 12}

    def __init__(self, nc, stack):
        self.nc = nc
        self.engs = {"pe": nc.tensor, "dve": nc.vector, "act": nc.scalar, "pool": nc.gpsimd, "sp": nc.sync}
        self.sem = {k: stack.enter_context(nc.semaphore("s_" + k)) for k in self.engs}
        self.cnt = {k: 0 for k in self.engs}
        self.dq = ["sp", "act", "pool"]
        self.dsem = {q: [stack.enter_context(nc.semaphore("d_%s%d" % (q, i))) for i in range(self.NDQ[q])] for q in self.dq}
        self.dcnt = {q: 0 for q in self.dq}
        self.waited = {}
        self.lastw = {}
        self.readers = {}
        self.out_tokens = []

    def _need(self, stream, tok):
        sem, val, tstream, isdma = tok
        if (not isdma) and tstream == stream and stream == "pe":
            return
        key = (stream, id(sem))
        if self.waited.get(key, 0) >= val:
            return
        self.waited[key] = val
        self.engs[stream].wait_ge(sem, val)

    def _deps(self, stream, r, w):
        for k in r:
            t = self.lastw.get(k)
            if t is not None:
                self._need(stream, t)
        for k in w:
            t = self.lastw.get(k)
            if t is not None:
                self._need(stream, t)
            for t in self.readers.get(k, {}).values():
                self._need(stream, t)

    def _commit(self, tok, r, w):
        for k in r:
            self.readers.setdefault(k, {})[(tok[2], tok[3], id(tok[0]))] = tok
        for k in w:
            self.lastw[k] = tok
            self.readers[k] = {}

    def op(self, stream, fn, r=(), w=()):
        self._deps(stream, r, w)
        ins = fn(self.engs[stream])
        self.cnt[stream] += 1
        ins.then_inc(self.sem[stream], 1)
        tok = (self.sem[stream], self.cnt[stream], stream, False)
        self._commit(tok, r, w)
        return tok

    def dma(self, q, fn, r=(), w=(), is_out=False):
        j = self.dcnt[q]
        nd = self.NDQ[q]
        sem = self.dsem[q][j % nd]
        if j >= nd:
            self._need(q, (sem, 16 * (j // nd), q, True))
        self._deps(q, r, w)
        ins = fn(self.engs[q])
        ins.then_inc(sem, 16)
        self.dcnt[q] += 1
        tok = (sem, 16 * (j // nd + 1), q, True)
        self._commit(tok, r, w)
        if is_out:
            self.out_tokens.append(tok)
        return tok

    def barrier(self):
        toks = []
        for s in self.engs:
            if self.cnt[s] > 0:
                toks.append((self.sem[s], self.cnt[s], s, False))
        for q in self.dq:
            j = self.dcnt[q]
            nd = self.NDQ[q]
            for i in range(max(0, j - nd), j):
                toks.append((self.dsem[q][i % nd], 16 * (i // nd + 1), q, True))
        for s in self.engs:
            for t in toks:
                if (not t[3]) and t[2] == s:
                    continue
                self._need(s, t)

    def finish(self):
        for tok in self.out_tokens:
            self._need("sp", tok)


class Rot:
    def __init__(self, tiles, name, keys=None):
        self.tiles = tiles
        self.name = name
        self.keys = keys
        self.i = 0

    def next(self):
        j = self.i % len(self.tiles)
        self.i += 1
        return self.tiles[j], (self.keys[j] if self.keys else "%s#%d" % (self.name, j))


def build(T_OWN, debug=False):
    KSKIP = os.environ.get('K_SKIP', '')
    T_ALL = 2 * T_OWN
    NT_OWN = T_OWN // 512
    NT_ALL = T_ALL // 512
    nc = bass.Bass("TRN2", target_bir_lowering=False)

    def din(name, shape, dt=F32):
        return nc.dram_tensor(name, shape, dt, kind="ExternalInput").ap()

    def dscr(name, shape, dt=F32):
        return nc.dram_tensor(name, shape, dt, kind="ExternalOutput" if debug else "Internal").ap()

    xT = din("xT", [D, T_ALL])
    pos = din("pos", [1, T_ALL], I32)
    cT = din("cT", [128, 8])
    w_ada = din("w_ada", [D, 6144])
    w_fada = din("w_fada", [D, 2048])
    badaT = din("badaT", [128, 64])
    gvecT = din("gvecT", [128, 32])
    w_all = din("w_all", [D, NW1 + NW2 + NW3])
    wa_A = din("wa_A", [16, 512]); ba_A = din("ba_A", [1, 512])
    wa_B = din("wa_B", [16, 512]); ba_B = din("ba_B", [1, 512])
    lqk = din("lqk", [1, 256])
    dng = din("dng", [1, 1024])
    w_gp = din("w_gp", [D, D]); w_dp = din("w_dp", [D, D]); w_out = din("w_out", [D, D])
    w_pq = din("w_pq", [D, 2048])
    skT = din("skT", [128, 2048])
    peer_u = din("peer_u", [16384, D]); peer_v = din("peer_v", [16384, D])
    ident_d = din("ident", [128, 128])
    tri_d = din("tri4", [128, 512])
    ropec_d = din("ropec", [128, 2])
    pconst_d = din("pconst", [128, 4096], I32)
    pconstf_d = din("pconstf", [128, 64])
    pcs_d = din("pcs", [128, 8], I32)
    out = nc.dram_tensor("out", [T_OWN, D], F32, kind="ExternalOutput").ap()

    S_hT = dscr("S_hT", [D, T_ALL], BF16)
    S_rows = dscr("S_rows", [1, 40 * 128])
    S_gqT = dscr("S_gqT", [512, T_OWN]); S_gkT = dscr("S_gkT", [512, T_OWN])
    S_grT = dscr("S_grT", [1024, T_OWN], BF16)
    S_lrA = dscr("S_lrA", [16, T_OWN]); S_lrB = dscr("S_lrB", [16, T_ALL])
    S_gk = dscr("S_gk", [T_ALL, 512]); S_gv = dscr("S_gv", [T_ALL, 1024], BF16)
    S_qT = dscr("S_qT", [1024, T_OWN], BF16); S_kT = dscr("S_kT", [1024, T_ALL], BF16)
    S_dv = dscr("S_dv", [T_ALL, 1024], BF16)
    S_saT = dscr("S_saT", [1024, T_OWN], BF16); S_sbT = dscr("S_sbT", [1024, T_OWN], BF16)
    S_oT = dscr("S_oT", [1024, T_OWN])
    S_odT = dscr("S_odT", [1024, T_OWN], BF16)
    S_x1 = dscr("S_x1", [T_OWN, D])
    S_idx = dscr("S_idx", [T_OWN, 128], I32)
    S_gw = dscr("S_gw", [T_OWN, 128])

    with ExitStack() as st:
        S = Sched(nc, st)

        uniq = [0]

        def sb(name, shape, dt=F32, stack=st):
            uniq[0] += 1
            return stack.enter_context(nc.sbuf_tensor("s%d_%s" % (uniq[0], name), shape, dt))

        def rot(name, shape, dt, n, stack=st):
            return Rot([sb("%s%d" % (name, i), shape, dt, stack) for i in range(n)], name)

        pb = [st.enter_context(nc.psum_tensor("pb%d" % i, [128, 512], F32)) for i in range(7)]
        ptb = st.enter_context(nc.psum_tensor("ptb", [128, 1024], BF16))
        PBK = ["pb%d" % i for i in range(7)]

        ident = sb("ident", [128, 128]); tri4 = sb("tri4", [128, 512]); ropec = sb("ropec", [128, 2])
        ident_bf = sb("ident_bf", [128, 128], BF16)
        ones_bf = sb("ones_bf", [128, 128], BF16); ones_f = sb("ones_f", [1, 128])
        S.dma("sp", lambda e: e.dma_start(out=ident[:], in_=ident_d), w=["ident"])
        S.dma("sp", lambda e: e.dma_start(out=tri4[:], in_=tri_d), w=["tri4"])
        S.dma("sp", lambda e: e.dma_start(out=ropec[:], in_=ropec_d), w=["ropec"])
        S.op("dve", lambda e: e.tensor_copy(out=ident_bf[:], in_=ident[:]), r=["ident"], w=["ident_bf"])
        S.op("pool", lambda e: e.memset(ones_bf[:], 1.0), w=["ones_bf"])
        S.op("pool", lambda e: e.memset(ones_f[:], 1.0), w=["ones_f"])
        triA, triB, striA, striB = (tri4[:, i * 128:(i + 1) * 128] for i in range(4))

        modT = sb("modT", [128, 64]); gvec = sb("gvec", [128, 32]); avec = sb("avec", [128, 24])
        lam = sb("lam", [128, 1]); dngr = sb("dngr", [128, 1024])
        sh1 = lambda kc: modT[:, kc:kc + 1]
        gt1 = lambda kc: modT[:, 16 + kc:17 + kc]

        with ExitStack() as p0:
            cTt = sb("cTt", [128, 8], F32, p0); cact = sb("cact", [128, 8], F32, p0)
            badat = sb("badat", [128, 64], F32, p0)
            wa_r = rot("wada", [128, 6144], F32, 2, p0)
            wf_r = rot("wfada", [128, 2048], F32, 2, p0)
            lq = sb("lq", [128, 256], F32, p0); lj = sb("lj", [128, 64], F32, p0); l2 = sb("l2", [128, 2], F32, p0)
            rows = sb("rows", [128, 40], F32, p0)
            S.dma("sp", lambda e: e.dma_start(out=cTt[:], in_=cT), w=["cTt"])
            S.dma("sp", lambda e: e.dma_start(out=badat[:], in_=badaT), w=["badat"])
            S.dma("sp", lambda e: e.dma_start(out=gvec[:], in_=gvecT), w=["gvec"])
            S.dma("sp", lambda e: e.dma_start(out=lq[:], in_=lqk.partition_broadcast(128)), w=["lq"])
            S.dma("sp", lambda e: e.dma_start(out=dngr[:], in_=dng.partition_broadcast(128)), w=["dngr"])
            S.op("act", lambda e: e.activation(out=cact[:], in_=cTt[:], func=AF.Silu), r=["cTt"], w=["cact"])
            mps = pb[0]
            for kc in range(8):
                wt, wk = wa_r.next(); ft, fk = wf_r.next()
                S.dma("sp", lambda e: e.dma_start(out=wt[:], in_=w_ada[kc * 128:(kc + 1) * 128, :]), w=[wk])
                S.dma("act", lambda e: e.dma_start(out=ft[:], in_=w_fada[kc * 128:(kc + 1) * 128, :]), w=[fk])
                for j in range(48):
                    S.op("pe", lambda e: e.matmul(mps[:, j:j + 1], lhsT=wt[:, j * 128:(j + 1) * 128], rhs=cact[:, kc:kc + 1], start=(kc == 0 and j == 0), stop=False), r=[wk, "cact"], w=["pb0"])
                for j in range(16):
                    S.op("pe", lambda e: e.matmul(mps[:, 48 + j:49 + j], lhsT=ft[:, j * 128:(j + 1) * 128], rhs=cact[:, kc:kc + 1], start=False, stop=(kc == 7 and j == 15)), r=[fk, "cact"], w=["pb0"])
            S.op("dve", lambda e: e.tensor_tensor(out=modT[:], in0=mps[:, 0:64], in1=badat[:], op=ALU.add), r=["pb0", "badat"], w=["modT"])
            for i, (c0, g0) in enumerate(((8, 0), (32, 8), (56, 16))):
                S.op("dve", lambda e: e.scalar_tensor_tensor(out=avec[:, i * 8:(i + 1) * 8], in0=modT[:, c0:c0 + 8], scalar=1.0, in1=gvec[:, g0:g0 + 8], op0=ALU.add, op1=ALU.mult), r=["modT", "gvec"], w=["avec"])
            for i in range(2):
                S.op("dve", lambda e: e.scalar_tensor_tensor(out=lj[:], in0=lq[:, i * 128:i * 128 + 64], scalar=1.0, in1=lq[:, i * 128 + 64:i * 128 + 128], op0=ALU.mult, op1=ALU.mult, accum_out=l2[:, i:i + 1]), r=["lq"], w=["lj", "l2"])
            S.op("act", lambda e: e.activation(out=l2[:], in_=l2[:], func=AF.Exp), r=["l2"], w=["l2"])
            S.op("dve", lambda e: e.scalar_tensor_tensor(out=lam[:], in0=l2[:, 0:1], scalar=0.2, in1=l2[:, 1:2], op0=ALU.add, op1=ALU.subtract), r=["l2"], w=["lam"])
            S.op("dve", lambda e: e.tensor_scalar(out=dngr[:], in0=dngr[:], scalar1=0.8, scalar2=None, op0=ALU.mult), r=["dngr"], w=["dngr"])
            S.op("dve", lambda e: e.tensor_copy(out=rows[:, 0:8], in_=avec[:, 8:16]), r=["avec"], w=["rows"])
            S.op("dve", lambda e: e.tensor_copy(out=rows[:, 8:16], in_=modT[:, 24:32]), r=["modT"], w=["rows"])
            S.op("dve", lambda e: e.tensor_copy(out=rows[:, 16:24], in_=modT[:, 40:48]), r=["modT"], w=["rows"])
            S.op("dve", lambda e: e.tensor_copy(out=rows[:, 24:32], in_=avec[:, 16:24]), r=["avec"], w=["rows"])
            S.op("dve", lambda e: e.tensor_copy(out=rows[:, 32:40], in_=modT[:, 48:56]), r=["modT"], w=["rows"])
            S.dma("sp", lambda e: e.dma_start(out=S_rows.rearrange("o (j p) -> p (o j)", p=128), in_=rows[:], allow_slow_non_contiguous=True), r=["rows"])
            S.barrier()
            if os.environ.get('K_STOP') == '1':
                S.finish(); return nc

        def norm_mod(xt, xk, hT, hk, a_col, sh_col, ntok, stk_rs, ps, psk):
            sq, sqk = stk_rs["sq"].next()
            rs, rsk = stk_rs["rs"].next()
            S.op("act", lambda e: e.activation(out=sq[:, :, 0:ntok], in_=xt[:, :, 0:ntok], func=AF.Square), r=[xk], w=[sqk])
            for kc in range(8):
                S.op("pe", lambda e: e.matmul(ps[:, 0:ntok], lhsT=ones_bf[:], rhs=sq[:, kc, 0:ntok], start=(kc == 0), stop=(kc == 7)), r=["ones_bf", sqk], w=[psk])
            S.op("act", lambda e: e.activation(out=rs[:, 0:ntok], in_=ps[:, 0:ntok], func=AF.Sqrt, bias=EPS, scale=1.0 / D), r=[psk], w=[rsk])
            S.op("dve", lambda e: e.reciprocal(out=rs[:, 0:ntok], in_=rs[:, 0:ntok]), r=[rsk], w=[rsk])
            for kc in range(8):
                S.op("dve", lambda e: e.scalar_tensor_tensor(out=xt[:, kc, 0:ntok], in0=xt[:, kc, 0:ntok], scalar=a_col(kc), in1=rs[:, 0:ntok], op0=ALU.mult, op1=ALU.mult), r=[xk, rsk, "avec"], w=[xk])
            for kc in range(8):
                S.op("act", lambda e: e.activation(out=hT[:, kc, 0:ntok], in_=xt[:, kc, 0:ntok], func=AF.Identity, bias=sh_col(kc), scale=1.0), r=[xk, "modT"], w=[hk])

        xT_v = xT.rearrange("(k p) t -> p k t", p=128)
        hT_v = S_hT.rearrange("(k p) t -> p k t", p=128)
        with ExitStack() as pa:
            x_r = rot("xa", [128, 8, 512], F32, 2, pa)
            h_r = rot("ha", [128, 8, 512], BF16, 2, pa)
            rs_r = {"sq": rot("sqa", [128, 8, 512], BF16, 2, pa), "rs": rot("rsa", [128, 512], F32, 2, pa)}
            for tt in range(NT_ALL):
                xt, xk = x_r.next(); ht, hk = h_r.next()
                t0 = tt * 512
                S.dma("sp", lambda e: e.dma_start(out=xt[:], in_=xT_v[:, :, t0:t0 + 512]), w=[xk])
                norm_mod(xt, xk, ht, hk, lambda kc: avec[:, kc:kc + 1], sh1, 512, rs_r, pb[tt % 2], PBK[tt % 2])
                S.dma("act", lambda e: e.dma_start(out=hT_v[:, :, t0:t0 + 512], in_=ht[:]), r=[hk])
            S.barrier()
            if os.environ.get('K_STOP') == '2':
                S.finish(); return nc

        psrot = Rot(pb[0:6], "pb", PBK[0:6])

        psrotD = Rot(pb[0:4], "pbD", PBK[0:4])
        psrotA = Rot(pb[4:7], "pbA", PBK[4:7])
        AFFINITY = not os.environ.get("K_NOAFF")

        def nps(eng="dve"):
            if not AFFINITY:
                return psrot.next()
            return (psrotD if eng == "dve" else psrotA).next()

        wst_r = rot("wst", [128, 1024], F32, 3)
        lwc = [0]

        def load_w(Wt, wkey, c0, n, src=None, kcs=8):
            src = w_all if src is None else src
            for kc in range(kcs):
                for p0 in range(0, n, 1024):
                    pn = min(1024, n - p0)
                    wst, wstk = wst_r.next()
                    q = "sp" if lwc[0] % 2 == 0 else "act"
                    S.dma(q, lambda e: e.dma_start(out=wst[:, 0:pn], in_=src[kc * 128:(kc + 1) * 128, c0 + p0:c0 + p0 + pn]), w=[wstk])
                    eng = "pool" if (lwc[0] % 2 == 0 and not os.environ.get("K_NOPOOL")) else "dve"
                    lwc[0] += 1
                    dst = Wt[:, kc, p0:p0 + pn] if kcs > 1 else Wt[:, p0:p0 + pn]
                    S.op(eng, lambda e: e.tensor_copy(out=dst, in_=wst[:, 0:pn]), r=[wstk], w=[wkey])

        def fm_chunk(ps, psk, Wt, wkey, wc, n, ht, hk):
            for kc in range(8):
                S.op("pe", lambda e: e.matmul(ps[0:n, :], lhsT=Wt[:, kc, wc:wc + n], rhs=ht[:, kc, :], start=(kc == 0), stop=(kc == 7)), r=[wkey, hk], w=[psk])

        def tm_chunk(ps, psk, Wt, wkey, wc, n, ht, hk, sub):
            for kc in range(8):
                S.op("pe", lambda e: e.matmul(ps[:, 0:n], lhsT=ht[:, kc, sub * 128:(sub + 1) * 128], rhs=Wt[:, kc, wc:wc + n], start=(kc == 0), stop=(kc == 7)), r=[wkey, hk], w=[psk])

        stq = ["sp", "act"]
        stc = [0]

        def store(dst, src, key):
            q = stq[stc[0] % 2] if not os.environ.get("K_Q1") else "sp"; stc[0] += 1
            S.dma(q, lambda e: e.dma_start(out=dst, in_=src), r=[key])

        with ExitStack() as pp:
            W1 = sb("W1", [128, 8, NW1], BF16, pp)
            load_w(W1, "W1", 0, NW1)
            h_r = rot("hb", [128, 8, 512], BF16, 2, pp)
            sf_r = rot("sf", [128, 512], F32, 4, pp)
            sh_r = rot("sh", [128, 512], BF16, 4, pp)
            for tt in range(NT_ALL):
                own = tt < NT_OWN
                t0 = tt * 512
                ht, hk = h_r.next()
                S.dma("sp", lambda e: e.dma_start(out=ht[:], in_=hT_v[:, :, t0:t0 + 512]), w=[hk])
                if own and 'f' not in KSKIP:
                    for c in (range(8) if 'g' not in KSKIP else ()):
                        ps, psk = nps(); fm_chunk(ps, psk, W1, "W1", c * 128, 128, ht, hk)
                        sf, sfk = sf_r.next()
                        S.op("dve", lambda e: e.tensor_copy(out=sf[:], in_=ps[:]), r=[psk], w=[sfk])
                        dst = (S_gqT if c < 4 else S_gkT)[(c % 4) * 128:(c % 4 + 1) * 128, t0:t0 + 512]
                        store(dst, sf[:], sfk)
                    for c in (range(8) if 'r' not in KSKIP else ()):
                        ps, psk = nps("act" if os.environ.get("K_GRF") in ("exp", "ident") else "dve"); fm_chunk(ps, psk, W1, "W1", 2048 + c * 128, 128, ht, hk)
                        sh, shk = sh_r.next()
                        if os.environ.get("K_GRF") == "exp":
                            S.op("act", lambda e: e.activation(out=sh[:], in_=ps[:], func=AF.Exp), r=[psk], w=[shk])
                        elif os.environ.get("K_GRF") == "ident":
                            S.op("act", lambda e: e.activation(out=sh[:], in_=ps[:], func=AF.Identity), r=[psk], w=[shk])
                        else:
                            sf, sfk = sf_r.next()
                            S.op("dve", lambda e: e.tensor_copy(out=sf[:], in_=ps[:]), r=[psk], w=[sfk])
                            S.op("act", lambda e: e.activation(out=sh[:], in_=sf[:], func=AF.Silu), r=[sfk], w=[shk])
                        store(S_grT[c * 128:(c + 1) * 128, t0:t0 + 512], sh[:], shk)
                    if "l" not in KSKIP:
                        ps, psk = nps(); fm_chunk(ps, psk, W1, "W1", 3072, 16, ht, hk)
                        sf, sfk = sf_r.next()
                        S.op("dve", lambda e: e.tensor_copy(out=sf[0:16, :], in_=ps[0:16, :]), r=[psk], w=[sfk])
                        store(S_lrA[:, t0:t0 + 512], sf[0:16, :], sfk)
                if "l" not in KSKIP:
                    ps, psk = nps(); fm_chunk(ps, psk, W1, "W1", 3088, 16, ht, hk)
                    sf, sfk = sf_r.next()
                    S.op("dve", lambda e: e.tensor_copy(out=sf[0:16, :], in_=ps[0:16, :]), r=[psk], w=[sfk])
                    store(S_lrB[:, t0:t0 + 512], sf[0:16, :], sfk)
                for sub in (range(4) if "t" not in KSKIP else ()):
                    r0 = t0 + sub * 128
                    ps, psk = nps(); tm_chunk(ps, psk, W1, "W1", 512, 512, ht, hk, sub)
                    sf, sfk = sf_r.next()
                    S.op("dve", lambda e: e.tensor_copy(out=sf[:], in_=ps[:]), r=[psk], w=[sfk])
                    store(S_gk[r0:r0 + 128, :], sf[:], sfk)
                    for c in range(2):
                        ps, psk = nps("act"); tm_chunk(ps, psk, W1, "W1", 1024 + c * 512, 512, ht, hk, sub)
                        sh, shk = sh_r.next()
                        S.op("act", lambda e: e.activation(out=sh[:], in_=ps[:], func=AF.Identity), r=[psk], w=[shk])
                        store(S_gv[r0:r0 + 128, c * 512:(c + 1) * 512], sh[:], shk)
            S.barrier()
            if os.environ.get('K_STOP') == '3':
                S.finish(); return nc

        with ExitStack() as pp:
            W2 = sb("W2", [128, 8, NW2], BF16, pp)
            load_w(W2, "W2", NW1, NW2)
            h_r = rot("hb", [128, 8, 512], BF16, 2, pp)
            sh_r = rot("sh", [128, 512], BF16, 4, pp)
            posi = sb("posi", [128, 512], I32, pp); ang = sb("ang", [128, 512], F32, pp)
            kf = sb("kf", [128, 512], F32, pp)
            cos_t = sb("cos_t", [128, 512], F32, pp); sin_t = sb("sin_t", [128, 512], F32, pp)
            t1_r = rot("t1", [128, 512], F32, 2, pp); t2_r = rot("t2", [128, 512], F32, 2, pp)
            for tt in range(NT_ALL):
                own = tt < NT_OWN
                t0 = tt * 512
                ht, hk = h_r.next()
                S.dma("sp", lambda e: e.dma_start(out=ht[:], in_=hT_v[:, :, t0:t0 + 512]), w=[hk])
                S.dma("sp", lambda e: e.dma_start(out=posi[:], in_=pos[:, t0:t0 + 512].partition_broadcast(128)), w=["posi"])
                S.op("dve", lambda e: e.tensor_copy(out=ang[:], in_=posi[:]), r=["posi"], w=["ang"])
                S.op("dve", lambda e: e.tensor_scalar(out=ang[:], in0=ang[:], scalar1=ropec[:, 0:1], scalar2=None, op0=ALU.mult), r=["ang", "ropec"], w=["ang"])
                S.op("dve", lambda e: e.tensor_scalar(out=kf[:], in0=ang[:], scalar1=1.0 / TWO_PI, scalar2=None, op0=ALU.mult), r=["ang"], w=["kf"])
                S.op("dve", lambda e: e.tensor_copy(out=posi[:], in_=kf[:]), r=["kf"], w=["posi"])
                S.op("dve", lambda e: e.tensor_copy(out=kf[:], in_=posi[:]), r=["posi"], w=["kf"])
                S.op("dve", lambda e: e.scalar_tensor_tensor(out=ang[:], in0=kf[:], scalar=-TWO_PI, in1=ang[:], op0=ALU.mult, op1=ALU.add), r=["kf", "ang"], w=["ang"])
                S.op("dve", lambda e: e.tensor_scalar(out=ang[:], in0=ang[:], scalar1=math.pi, scalar2=-math.pi, op0=ALU.min, op1=ALU.max), r=["ang"], w=["ang"])
                S.op("act", lambda e: e.activation(out=sin_t[:], in_=ang[:], func=AF.Sin, scale=ropec[:, 1:2]), r=["ang", "ropec"], w=["sin_t"])
                S.op("act", lambda e: e.activation(out=kf[:], in_=ang[:], func=AF.Abs), r=["ang"], w=["kf"])
                S.op("act", lambda e: e.activation(out=cos_t[:], in_=kf[:], func=AF.Sin, bias=math.pi / 2, scale=-1.0), r=["kf"], w=["cos_t"])
                for which in ((0, 1) if own else (1,)):
                    base = which * 2048
                    for h in range(8):
                        ps, psk = nps(); fm_chunk(ps, psk, W2, "W2", base + h * 128, 128, ht, hk)
                        ps2, psk2 = nps(); fm_chunk(ps2, psk2, W2, "W2", base + 1024 + h * 128, 128, ht, hk)
                        t1, t1k = t1_r.next(); t2, t2k = t2_r.next()
                        S.op("dve", lambda e: e.tensor_tensor(out=t1[:], in0=ps[:], in1=cos_t[:], op=ALU.mult), r=[psk, "cos_t"], w=[t1k])
                        S.op("dve", lambda e: e.tensor_tensor(out=t2[:], in0=ps2[:], in1=sin_t[:], op=ALU.mult), r=[psk2, "sin_t"], w=[t2k])
                        sh, shk = sh_r.next()
                        S.op("pool", lambda e: e.tensor_tensor(out=sh[:], in0=t1[:], in1=t2[:], op=ALU.add), r=[t1k, t2k], w=[shk])
                        dst = (S_qT if which == 0 else S_kT)[h * 128:(h + 1) * 128, t0:t0 + 512]
                        store(dst, sh[:], shk)
                for sub in range(4):
                    r0 = t0 + sub * 128
                    for c in range(2):
                        ps, psk = nps("act"); tm_chunk(ps, psk, W2, "W2", 4096 + c * 512, 512, ht, hk, sub)
                        sh, shk = sh_r.next()
                        S.op("act", lambda e: e.activation(out=sh[:], in_=ps[:], func=AF.Identity), r=[psk], w=[shk])
                        store(S_dv[r0:r0 + 128, c * 512:(c + 1) * 512], sh[:], shk)
            S.barrier()
            if os.environ.get('K_STOP') == '4':
                S.finish(); return nc

        with ExitStack() as pp:
            W3 = sb("W3", [128, 8, NW3], BF16, pp)
            sf3_r = rot("sf3", [128, 512], F32, 4, pp)
            load_w(W3, "W3", NW1 + NW2, NW3)
            h_r = rot("hb", [128, 8, 512], BF16, 2, pp)
            sh_r = rot("sh", [128, 512], BF16, 4, pp)
            for tt in range(NT_OWN):
                t0 = tt * 512
                ht, hk = h_r.next()
                S.dma("sp", lambda e: e.dma_start(out=ht[:], in_=hT_v[:, :, t0:t0 + 512]), w=[hk])
                for c in range(16):
                    ps, psk = nps(); fm_chunk(ps, psk, W3, "W3", c * 128, 128, ht, hk)
                    sh, shk = sh_r.next()
                    sf, sfk = sf3_r.next()
                    S.op("dve", lambda e: e.tensor_copy(out=sf[:], in_=ps[:]), r=[psk], w=[sfk])
                    S.op("act", lambda e: e.activation(out=sh[:], in_=sf[:], func=AF.Sigmoid), r=[sfk], w=[shk])
                    dst = (S_saT if c < 8 else S_sbT)[(c % 8) * 128:(c % 8 + 1) * 128, t0:t0 + 512]
                    store(dst, sh[:], shk)
            S.barrier()
            if os.environ.get('K_STOP') == '5':
                S.finish(); return nc

        DKS = 128 ** -0.5
        oT_v = S_oT.rearrange("(c p) t -> p c t", p=128)
        gqT_v = S_gqT.rearrange("(h p) t -> p h t", p=128)
        gkT_v = S_gkT.rearrange("(h p) t -> p h t", p=128)
        with ExitStack() as pc:
            S32 = sb("S32", [128, 4, 256], F32, pc); Sbf = sb("Sbf", [128, 4, 256], BF16, pc)
            wa_t = sb("wa_t", [16, 512], F32, pc); ba_t = sb("ba_t", [1, 512], F32, pc)
            lr_r = rot("lr", [16, 128], F32, 2, pc)
            gk_r = rot("gkt", [128, 512], F32, 2, pc); gv_r = rot("gvt", [128, 1024], BF16, 2, pc)
            gq_r = rot("gqT", [128, 4, 128], F32, 2, pc); gkT_r = rot("gkT", [128, 4, 128], F32, 2, pc)
            sp_r = rot("spt", [128, 512], F32, 2, pc)
            eb_r = rot("expb", [128, 512], F32, 2, pc); enb_r = rot("expnb", [128, 512], F32, 2, pc)
            qe_r = rot("qe", [128, 4, 128], BF16, 2, pc); ke_r = rot("ke", [128, 4, 128], BF16, 2, pc)
            ee_r = rot("eend", [128, 512], F32, 2, pc); kend_r = rot("kend", [128, 512], BF16, 2, pc)
            attm_r = rot("attm", [128, 4, 128], BF16, 2, pc)
            osb_r = rot("osb", [128, 8, 128], F32, 2, pc); oa_r = rot("oa", [128, 8, 128], F32, 2, pc)
            for di in range(2):
                tri = triA if di == 0 else triB
                stri = striA if di == 0 else striB
                S.dma("sp", lambda e: e.dma_start(out=wa_t[:], in_=(wa_A if di == 0 else wa_B)), w=["wa_t"])
                S.dma("sp", lambda e: e.dma_start(out=ba_t[:], in_=(ba_A if di == 0 else ba_B)), w=["ba_t"])
                S.op("dve", lambda e: e.memset(S32[:], 0.0), w=["S32"])
                S.op("pool", lambda e: e.memset(Sbf[:], 0.0), w=["Sbf"])
                S_lr = S_lrA if di == 0 else S_lrB
                tiles = list(range(T_OWN // 128)) if di == 0 else list(range(T_ALL // 128 - 1, -1, -1))
                order = (0, 1) if di == 0 else (1, 0)
                for ti in tiles:
                    t0 = ti * 128
                    wout = ti < T_OWN // 128
                    lr, lrk = lr_r.next(); gk, gkk = gk_r.next(); gv, gvk = gv_r.next()
                    S.dma("sp", lambda e: e.dma_start(out=lr[:], in_=S_lr[:, t0:t0 + 128]), w=[lrk])
                    S.dma("act", lambda e: e.dma_start(out=gk[:], in_=S_gk[t0:t0 + 128, :]), w=[gkk])
                    S.dma("sp", lambda e: e.dma_start(out=gv[:], in_=S_gv[t0:t0 + 128, :]), w=[gvk])
                    if wout:
                        gq, gqk = gq_r.next(); gkT, gkTk = gkT_r.next()
                        S.dma("act", lambda e: e.dma_start(out=gq[:], in_=gqT_v[:, :, t0:t0 + 128]), w=[gqk])
                        S.dma("sp", lambda e: e.dma_start(out=gkT[:], in_=gkT_v[:, :, t0:t0 + 128]), w=[gkTk])
                    S.op("pe", lambda e: e.matmul(pb[0][:, :], lhsT=lr[0:16, :], rhs=wa_t[0:16, :], start=True, stop=False), r=[lrk, "wa_t"], w=["pb0"])
                    S.op("pe", lambda e: e.matmul(pb[0][:, :], lhsT=ones_f[0:1, 0:128], rhs=ba_t[0:1, :], start=False, stop=True), r=["ones_f", "ba_t"], w=["pb0"])
                    sp, spk = sp_r.next()
                    S.op("act", lambda e: e.activation(out=sp[:], in_=pb[0][:, :], func=AF.Exp, scale=-1.0), r=["pb0"], w=[spk])
                    S.op("act", lambda e: e.activation(out=sp[:], in_=sp[:], func=AF.Ln, bias=1.0, scale=1.0), r=[spk], w=[spk])
                    for h in range(4):
                        S.op("pe", lambda e: e.matmul(pb[1][:, h * 128:(h + 1) * 128], lhsT=sp[:, h * 128:(h + 1) * 128], rhs=tri, start=True, stop=True), r=[spk, "tri4"], w=["pb1"])
                    eb, ebk = eb_r.next()
                    S.op("act", lambda e: e.activation(out=eb[:], in_=pb[1][:, :], func=AF.Exp, scale=-1.0 / 16), r=["pb1"], w=[ebk])
                    if wout:
                        enb, enbk = enb_r.next()
                        S.op("act", lambda e: e.activation(out=enb[:], in_=pb[1][:, :], func=AF.Exp, scale=1.0 / 16), r=["pb1"], w=[enbk])
                        qe, qek = qe_r.next(); ke, kek = ke_r.next()
                        S.op("dve", lambda e: e.scalar_tensor_tensor(out=qe[:], in0=gq[:], scalar=DKS, in1=eb[:].rearrange("p (h t) -> p h t", h=4), op0=ALU.mult, op1=ALU.mult), r=[gqk, ebk], w=[qek])
                        S.op("pool", lambda e: e.tensor_tensor(out=ke[:], in0=gkT[:], in1=enb[:].rearrange("p (h t) -> p h t", h=4), op=ALU.mult), r=[gkTk, enbk], w=[kek])
                    S.op("pe", lambda e: e.matmul(pb[0][:, :], lhsT=stri, rhs=sp[:], start=True, stop=True), r=[spk, "tri4"], w=["pb0"])
                    ee, eek = ee_r.next(); kend, kendk = kend_r.next()
                    S.op("act", lambda e: e.activation(out=ee[:], in_=pb[0][:, :], func=AF.Exp, scale=-1.0 / 16), r=["pb0"], w=[eek])
                    S.op("dve", lambda e: e.tensor_tensor(out=kend[:], in0=gk[:], in1=ee[:], op=ALU.mult), r=[gkk, eek], w=[kendk])
                    if wout:
                        for h in range(4):
                            S.op("pe", lambda e: e.matmul(pb[3][:, h * 128:(h + 1) * 128], lhsT=ke[:, h, :], rhs=qe[:, h, :], start=True, stop=True), r=[kek, qek], w=["pb3"])
                        attm, attmk = attm_r.next()
                        S.op("dve", lambda e: e.tensor_tensor(out=attm[:], in0=pb[3][:, :].rearrange("p (h t) -> p h t", h=4), in1=tri.unsqueeze(1).to_broadcast([128, 4, 128]), op=ALU.mult), r=["pb3", "tri4"], w=[attmk])
                        for ch in range(8):
                            h, dc = ch // 2, ch % 2
                            ob = pb[4 + ch // 4]; obk = PBK[4 + ch // 4]
                            S.op("pe", lambda e: e.matmul(ob[:, (ch % 4) * 128:(ch % 4 + 1) * 128], lhsT=gv[:, h * 256 + dc * 128:h * 256 + (dc + 1) * 128], rhs=attm[:, h, :], start=(ch % 4 == 0), stop=False), r=[gvk, attmk], w=[obk])
                    for ci, chunk in enumerate(order):
                        r0 = chunk * 64
                        if wout:
                            for ch in range(8):
                                h, dc = ch // 2, ch % 2
                                ob = pb[4 + ch // 4]; obk = PBK[4 + ch // 4]
                                c0 = (ch % 4) * 128 + r0
                                S.op("pe", lambda e: e.matmul(ob[:, c0:c0 + 64], lhsT=Sbf[:, h, dc * 128:(dc + 1) * 128], rhs=qe[:, h, r0:r0 + 64], start=False, stop=(ci == 1 and ch % 4 == 3)), r=["Sbf", qek], w=[obk])
                        for h in range(4):
                            ub = pb[2] if h < 2 else pb[6]
                            ubk = "pb2" if h < 2 else "pb6"
                            S.op("pe", lambda e: e.matmul(ub[:, (h % 2) * 256:(h % 2 + 1) * 256], lhsT=kend[r0:r0 + 64, h * 128:(h + 1) * 128], rhs=gv[r0:r0 + 64, h * 256:(h + 1) * 256], start=True, stop=True), r=[kendk, gvk], w=[ubk])
                        dcol = (r0 + 63) if di == 0 else r0
                        for h in range(4):
                            ub = pb[2] if h < 2 else pb[6]
                            ubk = "pb2" if h < 2 else "pb6"
                            S.op("dve", lambda e: e.scalar_tensor_tensor(out=S32[:, h, :], in0=S32[:, h, :], scalar=eb[:, h * 128 + dcol:h * 128 + dcol + 1], in1=ub[:, (h % 2) * 256:(h % 2 + 1) * 256], op0=ALU.mult, op1=ALU.add), r=["S32", ebk, ubk], w=["S32"])
                        S.op("act", lambda e: e.activation(out=Sbf[:], in_=S32[:], func=AF.Identity), r=["S32"], w=["Sbf"])
                    if wout:
                        osb, osbk = osb_r.next()
                        if di == 0:
                            for half in range(2):
                                S.op("dve", lambda e: e.tensor_copy(out=osb[:, half * 4:(half + 1) * 4, :], in_=pb[4 + half][:, :].rearrange("p (c t) -> p c t", c=4)), r=[PBK[4 + half]], w=[osbk])
                        else:
                            oa, oak = oa_r.next()
                            S.dma("sp", lambda e: e.dma_start(out=oa[:], in_=oT_v[:, :, t0:t0 + 128]), w=[oak])
                            for half in range(2):
                                S.op("dve", lambda e: e.tensor_tensor(out=osb[:, half * 4:(half + 1) * 4, :], in0=pb[4 + half][:, :].rearrange("p (c t) -> p c t", c=4), in1=oa[:, half * 4:(half + 1) * 4, :], op=ALU.add), r=[PBK[4 + half], oak], w=[osbk])
                        S.dma("act", lambda e: e.dma_start(out=oT_v[:, :, t0:t0 + 128], in_=osb[:]), r=[osbk])
                S.barrier()
                if os.environ.get('K_STOP') == '6':
                    S.finish(); return nc

        NKB = T_ALL // 128
        NQG = T_OWN // 512
        dv_v = S_dv.rearrange("(kb p) f -> p kb f", p=128)
        with ExitStack() as pd:
            kT_r = rot("kT", [128, T_ALL], BF16, 2, pd); qT_r = rot("qT", [128, T_OWN], BF16, 2, pd)
            va_r = rot("vaug", [128, NKB, 129], BF16, 2, pd)
            pT_r = rot("pT", [128, 512], BF16, 4, pd)
            rz = sb("rz", [128, 4], F32, pd)
            osb_r = rot("dosb", [128, 8 * 129], F32, 2, pd)
            t1_r = rot("dt1", [128, 128], F32, 2, pd)
            od_r = rot("od", [128, 4, 128], BF16, 2, pd); junk = sb("djunk", [128, 128], F32, pd)
            odT_r = rot("odT", [128, 512], BF16, 2, pd)
            for vt in va_r.tiles:
                S.op("pool", lambda e: e.memset(vt[:, :, 128:129], 1.0), w=["vaug#0", "vaug#1"])
            sps = [(pb[i], PBK[i]) for i in (0, 1, 2, 6)]

            def oreg(r):
                b = 3 + r // 3
                return pb[b][:, (r % 3) * 129:(r % 3) * 129 + 129], PBK[b]

            heads = {}

            def load_head(h):
                if h in heads or h >= 8:
                    return
                kT, kTk = kT_r.next(); qT, qTk = qT_r.next(); va, vak = va_r.next()
                S.dma("sp", lambda e: e.dma_start(out=kT[:], in_=S_kT[h * 128:(h + 1) * 128, :]), w=[kTk])
                S.dma("sp", lambda e: e.dma_start(out=qT[:], in_=S_qT[h * 128:(h + 1) * 128, :]), w=[qTk])
                S.dma("sp", lambda e: e.dma_start(out=va[:, :, 0:128], in_=dv_v[:, :, h * 128:(h + 1) * 128]), w=[vak])
                heads[h] = (kT, kTk, qT, qTk, va, vak)

            o4_r = rot("do4", [128, 4, 128], F32, 2, pd)
            ssq = sb("dssq", [128, 4], F32, pd)

            def epilogue(h, qg):
                q0 = qg * 512
                osb, osbk = osb_r.next()
                for b in range(3):
                    ncol = 387 if b < 2 else 258
                    S.op("dve", lambda e: e.tensor_copy(out=osb[:, b * 387:b * 387 + ncol], in_=pb[3 + b][:, 0:ncol]), r=[PBK[3 + b]], w=[osbk])
                odT, odTk = odT_r.next()
                o4, o4k = o4_r.next()
                S.op("dve", lambda e: e.memset(ssq[:], 0.0), w=["dssq"])
                for qi in range(4):
                    o0 = osb[:, qi * 129:(qi + 1) * 129]; o1 = osb[:, (4 + qi) * 129:(5 + qi) * 129]
                    S.op("dve", lambda e: e.reciprocal(out=rz[:, 0:1], in_=o0[:, 128:129]), r=[osbk], w=["rz"])
                    S.op("dve", lambda e: e.reciprocal(out=rz[:, 1:2], in_=o1[:, 128:129]), r=[osbk], w=["rz"])
                    S.op("dve", lambda e: e.tensor_tensor(out=rz[:, 2:3], in0=rz[:, 1:2], in1=lam[:], op=ALU.mult), r=["rz", "lam"], w=["rz"])
                    t1, t1k = t1_r.next()
                    S.op("dve", lambda e: e.tensor_scalar(out=t1[:], in0=o1[:, 0:128], scalar1=rz[:, 2:3], scalar2=None, op0=ALU.mult), r=[osbk, "rz"], w=[t1k])
                    S.op("dve", lambda e: e.scalar_tensor_tensor(out=o4[:, qi, :], in0=o0[:, 0:128], scalar=rz[:, 0:1], in1=t1[:], op0=ALU.mult, op1=ALU.subtract), r=[osbk, "rz", t1k], w=[o4k])
                    S.op("dve", lambda e: e.scalar_tensor_tensor(out=junk[:], in0=o4[:, qi, :], scalar=1.0, in1=o4[:, qi, :], op0=ALU.mult, op1=ALU.mult, accum_out=ssq[:, qi:qi + 1]), r=[o4k], w=["djunk", "dssq"])
                S.op("dve", lambda e: e.tensor_scalar(out=ssq[:], in0=ssq[:], scalar1=1.0 / 128, scalar2=EPS, op0=ALU.mult, op1=ALU.add), r=["dssq"], w=["dssq"])
                S.op("act", lambda e: e.activation(out=ssq[:], in_=ssq[:], func=AF.Sqrt), r=["dssq"], w=["dssq"])
                S.op("dve", lambda e: e.reciprocal(out=ssq[:], in_=ssq[:]), r=["dssq"], w=["dssq"])
                od, odk = od_r.next()
                for qi in range(4):
                    S.op("dve", lambda e: e.scalar_tensor_tensor(out=od[:, qi, :], in0=o4[:, qi, :], scalar=ssq[:, qi:qi + 1], in1=dngr[:, h * 128:(h + 1) * 128], op0=ALU.mult, op1=ALU.mult), r=[o4k, "dssq", "dngr"], w=[odk])

                def part_b():
                    for qi in range(4):
                        S.op("pe", lambda e: e.transpose(ptb[:, qi * 128:(qi + 1) * 128], od[:, qi, :], ident_bf[:]), r=[odk, "ident_bf"], w=["ptb"])
                    S.op("dve", lambda e: e.tensor_copy(out=odT[:], in_=ptb[:, 0:512]), r=["ptb"], w=[odTk])
                    S.dma("sp", lambda e: e.dma_start(out=S_odT[h * 128:(h + 1) * 128, q0:q0 + 512], in_=odT[:]), r=[odTk])
                return part_b

            steps = [(h, qg, kb, m) for h in range(8) for qg in range(NQG) for kb in range(NKB) for m in range(2)]
            LA = 2
            info = {}

            def emit_qk(si):
                h, qg, kb, m = steps[si]
                if h not in heads:
                    load_head(h)
                kT, kTk, qT, qTk, va, vak = heads[h]
                sp_, spk_ = sps[si % 4]
                q0 = qg * 512
                S.op("pe", lambda e: e.matmul(sp_[:, :], lhsT=kT[m * 64:(m + 1) * 64, kb * 128:(kb + 1) * 128], rhs=qT[m * 64:(m + 1) * 64, q0:q0 + 512], start=True, stop=True), r=[kTk, qTk], w=[spk_])
                pT, pTk = pT_r.next()
                S.op("act", lambda e: e.activation(out=pT[:], in_=sp_[:, :], func=AF.Exp, scale=0.125), r=[spk_], w=[pTk])
                info[si] = (pT, pTk)

            def emit_av(si):
                h, qg, kb, m = steps[si]
                if qg == 0 and kb == 0 and m == 0:
                    load_head(h + 1)
                kT, kTk, qT, qTk, va, vak = heads[h]
                pT, pTk = info.pop(si)
                for qi in range(4):
                    r = m * 4 + qi
                    oap, obk = oreg(r)
                    S.op("pe", lambda e: e.matmul(oap, lhsT=pT[:, qi * 128:(qi + 1) * 128], rhs=va[:, kb, :], start=(kb == 0 and r % 3 == 0), stop=(kb == NKB - 1 and (r % 3 == 2 or r == 7))), r=[pTk, vak], w=[obk])
                if kb == NKB - 1 and m == 1:
                    deferred.append((si + 24, epilogue(h, qg)))

            deferred = []
            for si in range(0, len(steps) + LA, 2):
                for sj in (si, si + 1):
                    if sj < len(steps):
                        emit_qk(sj)
                for sj in (si - LA, si - LA + 1):
                    if 0 <= sj < len(steps):
                        emit_av(sj)
                while deferred and deferred[0][0] <= si:
                    deferred.pop(0)[1]()
            for _, fn in deferred:
                fn()
            S.barrier()

        NTE = T_OWN // 256
        cv = lambda ap: ap.rearrange("(c p) t -> p c t", p=128)
        grT_v = cv(S_grT); odT_v = cv(S_odT); saT_v = cv(S_saT); sbT_v = cv(S_sbT)
        with ExitStack() as pe_:
            Wgp = sb("Wgp", [128, 8, 1024], BF16, pe_); Wdp = sb("Wdp", [128, 8, 1024], BF16, pe_); Wo = sb("Wo", [128, 8, 1024], BF16, pe_)
            load_w(Wgp, "Wgp", 0, 1024, w_gp); load_w(Wdp, "Wdp", 0, 1024, w_dp); load_w(Wo, "Wo", 0, 1024, w_out)
            o_r = rot("eo", [128, 8, 256], F32, 2, pe_); gr_r = rot("egr", [128, 8, 256], BF16, 2, pe_)
            sq_r = rot("esq", [128, 8, 256], BF16, 1, pe_); rsd = sb("ersd", [128, 4, 256], F32, pe_)
            og_r = rot("eog", [128, 8, 256], BF16, 1, pe_); od_r2 = rot("eod", [128, 8, 256], BF16, 2, pe_)
            sa_r = rot("esa", [128, 8, 256], BF16, 2, pe_); sb_r = rot("esb", [128, 8, 256], BF16, 2, pe_)
            mg_r = rot("emg", [128, 8, 256], BF16, 1, pe_); m1_r = rot("em1", [128, 256], F32, 2, pe_); m2_r = rot("em2", [128, 256], F32, 2, pe_)
            x_r2 = rot("ex", [128, 8, 256], F32, 2, pe_); x1tm_r = rot("ex1tm", [128, 1024], F32, 2, pe_)
            for te in range(NTE):
                t0 = te * 256
                o_, ok_ = o_r.next(); gr, grk = gr_r.next(); odt, odtk = od_r2.next(); sa, sak = sa_r.next(); sbb, sbk = sb_r.next(); xt, xk = x_r2.next()
                S.dma("sp", lambda e: e.dma_start(out=o_[:], in_=oT_v[:, :, t0:t0 + 256]), w=[ok_])
                S.dma("act", lambda e: e.dma_start(out=gr[:], in_=grT_v[:, :, t0:t0 + 256]), w=[grk])
                S.dma("sp", lambda e: e.dma_start(out=odt[:], in_=odT_v[:, :, t0:t0 + 256]), w=[odtk])
                S.dma("act", lambda e: e.dma_start(out=sa[:], in_=saT_v[:, :, t0:t0 + 256]), w=[sak])
                S.dma("sp", lambda e: e.dma_start(out=sbb[:], in_=sbT_v[:, :, t0:t0 + 256]), w=[sbk])
                S.dma("act", lambda e: e.dma_start(out=xt[:], in_=xT_v[:, :, t0:t0 + 256]), w=[xk])
                sq, sqk = sq_r.next()
                S.op("act", lambda e: e.activation(out=sq[:], in_=o_[:], func=AF.Square), r=[ok_], w=[sqk])
                for h in range(4):
                    bk = pb[4 + h // 2]; bkk = PBK[4 + h // 2]
                    for dc in range(2):
                        S.op("pe", lambda e: e.matmul(bk[:, (h % 2) * 256:(h % 2 + 1) * 256], lhsT=ones_bf[:], rhs=sq[:, h * 2 + dc, :], start=(dc == 0), stop=(dc == 1)), r=["ones_bf", sqk], w=[bkk])
                for half in range(2):
                    S.op("act", lambda e: e.activation(out=rsd[:, half * 2:(half + 1) * 2, :], in_=pb[4 + half][:, :].rearrange("p (h t) -> p h t", h=2), func=AF.Sqrt, bias=EPS, scale=1.0 / 256), r=[PBK[4 + half]], w=["ersd"])
                S.op("dve", lambda e: e.reciprocal(out=rsd[:], in_=rsd[:]), r=["ersd"], w=["ersd"])
                for ch in range(8):
                    S.op("dve", lambda e: e.scalar_tensor_tensor(out=o_[:, ch, :], in0=o_[:, ch, :], scalar=gvec[:, 24 + ch:25 + ch], in1=rsd[:, ch // 2, :], op0=ALU.mult, op1=ALU.mult), r=[ok_, "gvec", "ersd"], w=[ok_])
                og, ogk = og_r.next()
                S.op("pool", lambda e: e.tensor_tensor(out=og[:], in0=o_[:], in1=gr[:], op=ALU.mult), r=[ok_, grk], w=[ogk])
                mg, mgk = mg_r.next()
                for oc in range(8):
                    ps, psk = nps()
                    for kc in range(8):
                        S.op("pe", lambda e: e.matmul(ps[:, 0:256], lhsT=Wgp[:, kc, oc * 128:(oc + 1) * 128], rhs=og[:, kc, :], start=(kc == 0), stop=False), r=["Wgp", ogk], w=[psk])
                    for kc in range(8):
                        S.op("pe", lambda e: e.matmul(ps[:, 256:512], lhsT=Wdp[:, kc, oc * 128:(oc + 1) * 128], rhs=odt[:, kc, :], start=False, stop=(kc == 7)), r=["Wdp", odtk], w=[psk])
                    m1, m1k = m1_r.next(); m2, m2k = m2_r.next()
                    S.op("dve", lambda e: e.tensor_tensor(out=m1[:], in0=ps[:, 0:256], in1=sa[:, oc, :], op=ALU.mult), r=[psk, sak], w=[m1k])
                    S.op("dve", lambda e: e.tensor_tensor(out=m2[:], in0=ps[:, 256:512], in1=sbb[:, oc, :], op=ALU.mult), r=[psk, sbk], w=[m2k])
                    S.op("pool", lambda e: e.tensor_tensor(out=mg[:, oc, :], in0=m1[:], in1=m2[:], op=ALU.add), r=[m1k, m2k], w=[mgk])
                for oc in range(8):
                    ps, psk = nps()
                    for kc in range(8):
                        S.op("pe", lambda e: e.matmul(ps[:, 0:256], lhsT=Wo[:, kc, oc * 128:(oc + 1) * 128], rhs=mg[:, kc, :], start=(kc == 0), stop=(kc == 7)), r=["Wo", mgk], w=[psk])
                    S.op("dve", lambda e: e.scalar_tensor_tensor(out=xt[:, oc, :], in0=ps[:, 0:256], scalar=gt1(oc), in1=xt[:, oc, :], op0=ALU.mult, op1=ALU.add), r=[psk, "modT", xk], w=[xk])
                for sub in range(2):
                    x1tm, x1tmk = x1tm_r.next()
                    for half in range(2):
                        ps, psk = nps("act")
                        for k4 in range(4):
                            kc = half * 4 + k4
                            S.op("pe", lambda e: e.matmul(ps[:, k4 * 128:(k4 + 1) * 128], lhsT=xt[:, kc, sub * 128:(sub + 1) * 128], rhs=ident[:], start=True, stop=True), r=[xk, "ident"], w=[psk])
                        S.op("act", lambda e: e.activation(out=x1tm[:, half * 512:(half + 1) * 512], in_=ps[:, :], func=AF.Identity), r=[psk], w=[x1tmk])
                    store(S_x1[t0 + sub * 128:t0 + (sub + 1) * 128, :], x1tm[:], x1tmk)
            S.barrier()
            if os.environ.get('K_STOP') == '8':
                S.finish(); return nc

        NTF = T_OWN // 128
        with ExitStack() as pf:
            Wq = sb("Wq", [128, 8, 2048], BF16, pf); skb = sb("skb", [128, 2048], BF16, pf)
            load_w(Wq, "Wq", 0, 2048, w_pq)
            load_w(skb, "skb", 0, 2048, skT, kcs=1)
            rowsb = sb("rowsb", [128, 5120], F32, pf)
            S.dma("sp", lambda e: e.dma_start(out=rowsb[:], in_=S_rows.partition_broadcast(128)), w=["rowsb"])
            a2row, sh2row, gt2row, afrow, fshrow = (rowsb[:, i * 1024:(i + 1) * 1024] for i in range(5))
            pci = sb("pci", [128, 384], I32, pf); pcs = sb("pcs", [128, 8], I32, pf); pcf = sb("pcf", [128, 64], F32, pf)
            S.dma("sp", lambda e: e.dma_start(out=pci[:, 0:128], in_=pconst_d[:, 0:128]), w=["pci"])
            S.dma("sp", lambda e: e.dma_start(out=pci[:, 128:384], in_=pconst_d[:, 2048:2304]), w=["pci"])
            S.dma("sp", lambda e: e.dma_start(out=pcs[:], in_=pcs_d), w=["pcs"])
            S.dma("sp", lambda e: e.dma_start(out=pcf[:], in_=pconstf_d), w=["pcf"])
            niota = pci[:, 0:128].unsqueeze(1).to_broadcast([128, 16, 128]); ciota = pci[:, 128:384].unsqueeze(1).to_broadcast([128, 8, 256])
            cst = lambda j, n: pcs[:, j:j + 1].to_broadcast([128, n])
            x1_r = rot("fx1", [128, 1024], F32, 2, pf)
            h2 = sb("fh2", [128, 1024], F32, pf); h2b = sb("fh2b", [128, 1024], BF16, pf); h2T = sb("fh2T", [128, 8, 128], BF16, pf)
            qpT = sb("fqpT", [128, 16, 128], BF16, pf)
            sc = sb("fsc", [128, 2048], F32, pf); sc2 = sb("fsc2", [128, 2048], F32, pf)
            cand = sb("fcand", [128, 2048], F32, pf); tmpb = sb("ftmp", [128, 2048], F32, pf)
            v16 = sb("fv16", [128, 256], F32, pf); i16i = sb("fi16i", [128, 256], I32, pf); i16f = sb("fi16f", [128, 256], F32, pf)
            f16 = sb("ff16", [128, 128], F32, pf); cpi = sb("fcpi", [128, 128], I32, pf); cpj = sb("fcpj", [128, 128], I32, pf)
            aif = sb("faif", [128, 128], F32, pf); bjf = sb("fbjf", [128, 128], F32, pf)
            iA = sb("fiA", [128, 128], F32, pf); iB = sb("fiB", [128, 128], F32, pf)
            eif = sb("feif", [128, 128], F32, pf); eii = sb("feii", [128, 128], I32, pf)
            gm = sb("fgm", [128, 8], F32, pf); ge = sb("fge", [128, 128], F32, pf); gs = sb("fgs", [128, 8], F32, pf)
            gw = sb("fgw", [128, 128], F32, pf); aa = sb("faa", [128, 128], F32, pf); ww = sb("fww", [128, 128], F32, pf)
            st4 = sb("fst4", [128, 4], F32, pf)
            g_r = rot("fgat", [128, 1024], F32, 8, pf)
            junk = sb("fjunk", [128, 1024], F32, pf); acc = sb("facc", [128, 1024], F32, pf)
            x2 = sb("fx2", [128, 1024], F32, pf); ot_r = rot("fout", [128, 1024], F32, 2, pf)
            v16v = v16[:].rearrange("p (h q k) -> p h q k", h=8, q=2)
            i16v = i16f[:].rearrange("p (h q k) -> p h q k", h=8, q=2)
            B4 = [128, 8, 16, 16]
            r4 = lambda t: t[:].rearrange("p (h a b) -> p h a b", h=8, a=16)
            r3 = lambda t: t[:].rearrange("p (h k) -> p h k", h=8)
            for tf in range(NTF):
                t0 = tf * 128
                x1, x1k = x1_r.next()
                S.dma("sp", lambda e: e.dma_start(out=x1[:], in_=S_x1[t0:t0 + 128, :]), w=[x1k])
                S.op("act", lambda e: e.activation(out=junk[:], in_=x1[:], func=AF.Square, accum_out=st4[:, 0:1]), r=[x1k], w=["fjunk", "fst4"])
                S.op("act", lambda e: e.activation(out=st4[:, 1:2], in_=st4[:, 0:1], func=AF.Sqrt, bias=EPS, scale=1.0 / D), r=["fst4"], w=["fst4"])
                S.op("dve", lambda e: e.reciprocal(out=st4[:, 1:2], in_=st4[:, 1:2]), r=["fst4"], w=["fst4"])
                S.op("dve", lambda e: e.scalar_tensor_tensor(out=h2[:], in0=x1[:], scalar=st4[:, 1:2], in1=a2row, op0=ALU.mult, op1=ALU.mult), r=[x1k, "fst4", "rowsb"], w=["fh2"])
                S.op("dve", lambda e: e.tensor_tensor(out=h2[:], in0=h2[:], in1=sh2row, op=ALU.add), r=["fh2", "rowsb"], w=["fh2"])
                S.op("act", lambda e: e.activation(out=h2b[:], in_=h2[:], func=AF.Identity), r=["fh2"], w=["fh2b"])
                for kc in range(8):
                    S.op("pe", lambda e: e.transpose(ptb[:, kc * 128:(kc + 1) * 128], h2b[:, kc * 128:(kc + 1) * 128], ident_bf[:]), r=["fh2b", "ident_bf"], w=["ptb"])
                S.op("dve", lambda e: e.tensor_copy(out=h2T[:].rearrange("p k t -> p (k t)"), in_=ptb[:, :]), r=["ptb"], w=["fh2T"])
                for g4 in range(4):
                    ps, psk = nps("act")
                    for q in range(4):
                        hp = g4 * 4 + q
                        for kc in range(8):
                            S.op("pe", lambda e: e.matmul(ps[:, q * 128:(q + 1) * 128], lhsT=Wq[:, kc, hp * 128:(hp + 1) * 128], rhs=h2T[:, kc, :], start=(kc == 0), stop=(kc == 7)), r=["Wq", "fh2T"], w=[psk])
                    S.op("act", lambda e: e.activation(out=qpT[:, g4 * 4:(g4 + 1) * 4, :], in_=ps[:, :].rearrange("p (q t) -> p q t", q=4), func=AF.Identity), r=[psk], w=["fqpT"])
                for g4 in range(4):
                    ps, psk = nps()
                    for q in range(4):
                        hp = g4 * 4 + q
                        S.op("pe", lambda e: e.matmul(ps[:, q * 128:(q + 1) * 128], lhsT=qpT[:, hp, :], rhs=skb[:, hp * 128:(hp + 1) * 128], start=True, stop=True), r=["fqpT", "skb"], w=[psk])
                    S.op("dve", lambda e: e.tensor_copy(out=sc[:, g4 * 512:(g4 + 1) * 512], in_=ps[:, :]), r=[psk], w=["fsc"])
                sci = sc[:].bitcast(I32)
                S.op("dve", lambda e: e.tensor_tensor(out=sci, in0=sci, in1=cst(0, 2048), op=ALU.bitwise_and), r=["fsc", "pcs"], w=["fsc"])
                S.op("dve", lambda e: e.tensor_tensor(out=sci.rearrange("p (g n) -> p g n", g=16), in0=sci.rearrange("p (g n) -> p g n", g=16), in1=niota, op=ALU.bitwise_or), r=["fsc", "pci"], w=["fsc"])
                for hp in range(16):
                    sl = slice(hp * 128, (hp + 1) * 128)
                    S.op("dve", lambda e: e.max(out=v16[:, hp * 16:hp * 16 + 8], in_=sc[:, sl]), r=["fsc"], w=["fv16"])
                    S.op("dve", lambda e: e.match_replace(out=sc2[:, sl], in_to_replace=v16[:, hp * 16:hp * 16 + 8], in_values=sc[:, sl], imm_value=-1e30), r=["fsc", "fv16"], w=["fsc2"])
                    S.op("dve", lambda e: e.max(out=v16[:, hp * 16 + 8:hp * 16 + 16], in_=sc2[:, sl]), r=["fsc2"], w=["fv16"])
                S.op("dve", lambda e: e.tensor_tensor(out=i16i[:], in0=v16[:].bitcast(I32), in1=cst(2, 256), op=ALU.bitwise_and), r=["fv16", "pcs"], w=["fi16i"])
                S.op("dve", lambda e: e.tensor_copy(out=i16f[:], in_=i16i[:]), r=["fi16i"], w=["fi16f"])
                S.op("dve", lambda e: e.tensor_tensor(out=r4(cand), in0=v16v[:, :, 0, :].unsqueeze(3).to_broadcast(B4), in1=v16v[:, :, 1, :].unsqueeze(2).to_broadcast(B4), op=ALU.add), r=["fv16"], w=["fcand"])
                cai = cand[:].bitcast(I32)
                S.op("dve", lambda e: e.tensor_tensor(out=cai, in0=cai, in1=cst(1, 2048), op=ALU.bitwise_and), r=["fcand", "pcs"], w=["fcand"])
                S.op("dve", lambda e: e.tensor_tensor(out=cai.rearrange("p (g n) -> p g n", g=8), in0=cai.rearrange("p (g n) -> p g n", g=8), in1=ciota, op=ALU.bitwise_or), r=["fcand", "pci"], w=["fcand"])
                for h in range(8):
                    sl = slice(h * 256, (h + 1) * 256)
                    S.op("dve", lambda e: e.max(out=f16[:, h * 16:h * 16 + 8], in_=cand[:, sl]), r=["fcand"], w=["ff16"])
                    S.op("dve", lambda e: e.match_replace(out=tmpb[:, sl], in_to_replace=f16[:, h * 16:h * 16 + 8], in_values=cand[:, sl], imm_value=-1e30), r=["fcand", "ff16"], w=["ftmp"])
                    S.op("dve", lambda e: e.max(out=f16[:, h * 16 + 8:h * 16 + 16], in_=tmpb[:, sl]), r=["ftmp"], w=["ff16"])
                S.op("dve", lambda e: e.tensor_tensor(out=cpi[:], in0=f16[:].bitcast(I32), in1=cst(3, 128), op=ALU.bitwise_and), r=["ff16", "pcs"], w=["fcpi"])
                S.op("dve", lambda e: e.tensor_tensor(out=cpj[:], in0=cpi[:], in1=cst(5, 128), op=ALU.bitwise_and), r=["fcpi", "pcs"], w=["fcpj"])
                S.op("dve", lambda e: e.tensor_tensor(out=cpi[:], in0=cpi[:], in1=cst(4, 128), op=ALU.arith_shift_right), r=["fcpi", "pcs"], w=["fcpi"])
                S.op("dve", lambda e: e.tensor_copy(out=aif[:], in_=cpi[:]), r=["fcpi"], w=["faif"])
                S.op("dve", lambda e: e.tensor_copy(out=bjf[:], in_=cpj[:]), r=["fcpj"], w=["fbjf"])
                io4 = pcf[:, 0:16].unsqueeze(1).unsqueeze(1).to_broadcast(B4)
                for which, (sel, dst_) in enumerate(((aif, iA), (bjf, iB))):
                    S.op("dve", lambda e: e.tensor_tensor(out=r4(sc2), in0=io4, in1=r3(sel).unsqueeze(3).to_broadcast(B4), op=ALU.is_equal), r=["pcf", "faif", "fbjf"], w=["fsc2"])
                    S.op("dve", lambda e: e.tensor_tensor(out=r4(sc2), in0=r4(sc2), in1=i16v[:, :, which, :].unsqueeze(2).to_broadcast(B4), op=ALU.mult), r=["fsc2", "fi16f"], w=["fsc2"])
                    S.op("dve", lambda e: e.tensor_reduce(out=dst_[:], in_=sc2[:].rearrange("p (k i) -> p k i", i=16), axis=AX.X, op=ALU.add), r=["fsc2"], w=["fiA", "fiB"])
                S.op("dve", lambda e: e.scalar_tensor_tensor(out=eif[:], in0=iA[:], scalar=128.0, in1=iB[:], op0=ALU.mult, op1=ALU.add), r=["fiA", "fiB"], w=["feif"])
                S.op("dve", lambda e: e.tensor_copy(out=eii[:], in_=eif[:]), r=["feif"], w=["feii"])
                S.op("dve", lambda e: e.tensor_reduce(out=gm[:], in_=r3(f16), axis=AX.X, op=ALU.max), r=["ff16"], w=["fgm"])
                S.op("dve", lambda e: e.tensor_tensor(out=r3(ge), in0=r3(f16), in1=gm[:].unsqueeze(2).to_broadcast([128, 8, 16]), op=ALU.subtract), r=["ff16", "fgm"], w=["fge"])
                S.op("act", lambda e: e.activation(out=ge[:], in_=ge[:], func=AF.Exp), r=["fge"], w=["fge"])
                S.op("dve", lambda e: e.tensor_reduce(out=gs[:], in_=r3(ge), axis=AX.X, op=ALU.add), r=["fge"], w=["fgs"])
                S.op("dve", lambda e: e.reciprocal(out=gs[:], in_=gs[:]), r=["fgs"], w=["fgs"])
                S.op("dve", lambda e: e.tensor_tensor(out=r3(gw), in0=r3(ge), in1=gs[:].unsqueeze(2).to_broadcast([128, 8, 16]), op=ALU.mult), r=["fge", "fgs"], w=["fgw"])
                S.op("dve", lambda e: e.memset(aa[:], 0.0), w=["faa"])
                for k in range(128):
                    gt_, gtk = g_r.next()
                    S.dma("pool", lambda e: e.indirect_dma_start(out=gt_[:], out_offset=None, in_=peer_u, in_offset=bass.IndirectOffsetOnAxis(ap=eii[:, k:k + 1], axis=0)), r=["feii"], w=[gtk])
                    S.op("dve", lambda e: e.scalar_tensor_tensor(out=junk[:], in0=gt_[:], scalar=1.0, in1=h2[:], op0=ALU.mult, op1=ALU.mult, accum_out=aa[:, k:k + 1]), r=[gtk, "fh2"], w=["fjunk", "faa"])
                S.op("act", lambda e: e.activation(out=ww[:], in_=aa[:], func=AF.Gelu), r=["faa"], w=["fww"])
                S.op("dve", lambda e: e.tensor_tensor(out=ww[:], in0=ww[:], in1=gw[:], op=ALU.mult), r=["fww", "fgw"], w=["fww"])
                S.op("dve", lambda e: e.memset(acc[:], 0.0), w=["facc"])
                for k in range(128):
                    gt_, gtk = g_r.next()
                    S.dma("pool", lambda e: e.indirect_dma_start(out=gt_[:], out_offset=None, in_=peer_v, in_offset=bass.IndirectOffsetOnAxis(ap=eii[:, k:k + 1], axis=0)), r=["feii"], w=[gtk])
                    S.op("dve", lambda e: e.scalar_tensor_tensor(out=acc[:], in0=gt_[:], scalar=ww[:, k:k + 1], in1=acc[:], op0=ALU.mult, op1=ALU.add), r=[gtk, "fww", "facc"], w=["facc"])
                S.op("dve", lambda e: e.tensor_tensor(out=acc[:], in0=acc[:], in1=gt2row, op=ALU.mult), r=["facc", "rowsb"], w=["facc"])
                S.op("dve", lambda e: e.tensor_tensor(out=x2[:], in0=acc[:], in1=x1[:], op=ALU.add), r=["facc", x1k], w=["fx2"])
                S.op("act", lambda e: e.activation(out=junk[:], in_=x2[:], func=AF.Square, accum_out=st4[:, 2:3]), r=["fx2"], w=["fjunk", "fst4"])
                S.op("act", lambda e: e.activation(out=st4[:, 3:4], in_=st4[:, 2:3], func=AF.Sqrt, bias=EPS, scale=1.0 / D), r=["fst4"], w=["fst4"])
                S.op("dve", lambda e: e.reciprocal(out=st4[:, 3:4], in_=st4[:, 3:4]), r=["fst4"], w=["fst4"])
                ot, otk = ot_r.next()
                S.op("dve", lambda e: e.scalar_tensor_tensor(out=ot[:], in0=x2[:], scalar=st4[:, 3:4], in1=afrow, op0=ALU.mult, op1=ALU.mult), r=["fx2", "fst4", "rowsb"], w=[otk])
                S.op("dve", lambda e: e.tensor_tensor(out=ot[:], in0=ot[:], in1=fshrow, op=ALU.add), r=[otk, "rowsb"], w=[otk])
                S.dma("sp", lambda e: e.dma_start(out=out[t0:t0 + 128, :], in_=ot[:]), r=[otk], is_out=True)
            S.barrier()

        S.finish()
    return nc


def PHASES_AFTER_B(nc, S, st, L):
    pass


def _consts():
    p = np.arange(128)
    ident = np.eye(128, dtype=np.float32)
    j = p[:, None]; c = p[None, :]
    same = (j // 64) == (c // 64)
    triA = (same & (j <= c)).astype(np.float32)
    triB = (same & (j >= c)).astype(np.float32)
    striA = (same & (j > c)).astype(np.float32)
    striB = (same & (j < c)).astype(np.float32)
    tri4 = np.concatenate([triA, triB, striA, striB], axis=1)
    f = (p % 32).astype(np.float64)
    inv = (10000.0 ** (-(2.0 * f) / 64.0)).astype(np.float32)
    sign = np.where((p % 64) < 32, -1.0, 1.0).astype(np.float32)
    ropec = np.stack([inv, sign], axis=1).astype(np.float32)
    return ident, tri4, ropec


def make_in_maps(inputs, T_OWN=None):
    x = np.asarray(inputs["x"]); B, S_, _ = x.shape
    T_OWN = S_ // 2
    ident, tri4, ropec = _consts()
    w_in = np.asarray(inputs["w_in"])[0]
    cols = lambda a, b: w_in[:, a:b]
    perm = np.arange(1024).reshape(16, 64)
    perm = np.concatenate([perm[:, 32:], perm[:, :32]], axis=1).reshape(-1)
    dq = cols(3104, 4128); dk = cols(4128, 5152)
    pconst = np.zeros((128, 4096), np.int32)
    pconst[:, 0:2048] = (np.arange(2048) % 128)[None, :]
    pconst[:, 2048:4096] = (np.arange(2048) % 256)[None, :]
    pconstf = np.zeros((128, 64), np.float32)
    pconstf[:, 0:16] = np.arange(16, dtype=np.float32)[None, :]
    pcs = np.tile(np.array([-128, -256, 127, 255, 4, 15, 0, 0], np.int32)[None, :], (128, 1))
    common = {
        "w_ada": np.ascontiguousarray(inputs["w_ada"][0]),
        "w_fada": np.ascontiguousarray(inputs["w_final_ada"]),
        "lqk": np.concatenate([inputs["diff_lq1"][0], inputs["diff_lk1"][0], inputs["diff_lq2"][0], inputs["diff_lk2"][0]])[None, :].astype(np.float32),
        "dng": np.ascontiguousarray(inputs["diff_norm_g"]),
        "w_gp": np.ascontiguousarray(inputs["w_gla_proj"][0]), "w_dp": np.ascontiguousarray(inputs["w_diff_proj"][0]),
        "w_out": np.ascontiguousarray(inputs["w_out"][0]), "w_pq": np.ascontiguousarray(inputs["peer_wq"][0]),
        "skT": np.ascontiguousarray(np.transpose(inputs["peer_subkeys"][0].reshape(16, 128, 128), (2, 0, 1)).reshape(128, 2048)),
        "peer_u": np.ascontiguousarray(inputs["peer_u"][0]), "peer_v": np.ascontiguousarray(inputs["peer_v"][0]),
        "ident": ident, "tri4": tri4, "ropec": ropec, "pconst": pconst, "pconstf": pconstf, "pcs": pcs,
        "gvecT": np.ascontiguousarray(np.concatenate([inputs["norm1_g"][0].reshape(8, 128).T, inputs["norm2_g"][0].reshape(8, 128).T,
                                                       inputs["normf_g"].reshape(8, 128).T, inputs["gla_norm_g"][0].reshape(8, 128).T], axis=1)),
        "badaT": np.ascontiguousarray(np.concatenate([inputs["b_ada"][0].reshape(48, 128).T, inputs["b_final_ada"].reshape(16, 128).T], axis=1)),
    }
    w_all_hf = []
    for hf in range(2):
        lrA = cols(3072, 3088) if hf == 0 else cols(3088, 3104)
        lrB = cols(3088, 3104) if hf == 0 else cols(3072, 3088)
        w_all_hf.append(np.ascontiguousarray(np.concatenate(
            [cols(0, 3072), lrA, lrB, dq, dq[:, perm], dk, dk[:, perm], cols(5152, 6176), cols(6176, 8224)], axis=1)))
    fw = (inputs["gla_wa_fw"][0], inputs["gla_ba_fw"]); bw = (inputs["gla_wa_bw"][0], inputs["gla_ba_bw"])
    in_maps = []
    for b in range(B):
        for hf in range(2):
            xb = x[b]; pb_ = np.asarray(inputs["positions"])[b]
            if hf == 1:
                xb = xb[::-1]; pb_ = pb_[::-1]
            A, Bd = (fw, bw) if hf == 0 else (bw, fw)
            m = dict(common)
            m.update({
                "xT": np.ascontiguousarray(xb.T), "pos": np.ascontiguousarray(pb_[None, :].astype(np.int32)),
                "cT": np.ascontiguousarray(np.asarray(inputs["c"])[b].reshape(8, 128).T),
                "w_all": w_all_hf[hf],
                "wa_A": np.ascontiguousarray(A[0]), "ba_A": np.ascontiguousarray(A[1].reshape(1, 512)),
                "wa_B": np.ascontiguousarray(Bd[0]), "ba_B": np.ascontiguousarray(Bd[1].reshape(1, 512)),
            })
            in_maps.append(m)
    return in_maps, B, T_OWN


def kernel(**inputs):
    inputs = {k: np.asarray(v) for k, v in inputs.items()}
    in_maps, B, T_OWN = make_in_maps(inputs)
    nc = build(T_OWN)
    res = run_bass_kernel_spmd(nc, in_maps, core_ids=list(range(len(in_maps))))
    S_ = 2 * T_OWN
    out = np.zeros((B, S_, D), np.float32)
    for b in range(B):
        for hf in range(2):
            o = res.results[b * 2 + hf]["out"]
            if hf == 0:
                out[b, :T_OWN] = o
            else:
                out[b, T_OWN:] = o[::-1]
    return out
```
